# Optimizing a Trainium2 kernel written in Bass

```python
import jax, jax.numpy as jnp
from jax import lax
import numpy as np

D_MODEL = 1024
BATCH = 2
SEQ = 16384
DEPTH = 2

GLA_HEADS = 4
GLA_KEY_DIM = D_MODEL // 2
GLA_VALUE_DIM = D_MODEL
GLA_HEAD_K = GLA_KEY_DIM // GLA_HEADS
GLA_HEAD_V = GLA_VALUE_DIM // GLA_HEADS
GATE_RANK = 16
GATE_NORMALIZER = 16.0
CHUNK = 64
POOL_GROUPS = 4
POOL_WIDTH = D_MODEL // 2
POOL_GROUP_W = POOL_WIDTH // POOL_GROUPS
POOL_OUT_GROUP_W = D_MODEL // POOL_GROUPS
POOL_WINDOWS = (2, 4, 8, 16)
N_BRANCHES = 2
D_FF = -(-8 * D_MODEL // (3 * 256)) * 256
IN_WIDTH = 2 * GLA_KEY_DIM + 2 * GLA_VALUE_DIM + 2 * GATE_RANK + POOL_WIDTH + N_BRANCHES * D_MODEL
EPS = 1e-6

kernel_name = "bidir_gla_pool_hybrid_block"


def rmsnorm(x, g):
    xf = x.astype(jnp.float32)
    y = xf * lax.rsqrt(jnp.mean(xf * xf, axis=-1, keepdims=True) + EPS)
    return (y * g.astype(jnp.float32)).astype(x.dtype)


def gla_chunked(q, k, v, log_a, include_diag):
    bsz, heads, length, dk = q.shape
    dv = v.shape[-1]
    n = length // CHUNK
    q = q.reshape(bsz, heads, n, CHUNK, dk).astype(jnp.float32)
    k = k.reshape(bsz, heads, n, CHUNK, dk).astype(jnp.float32)
    v = v.reshape(bsz, heads, n, CHUNK, dv).astype(jnp.float32)
    b = jnp.cumsum(log_a.reshape(bsz, heads, n, CHUNK, dk).astype(jnp.float32), axis=3)
    b_last = b[:, :, :, -1:, :]
    q_e = q * jnp.exp(b)
    k_e = k * jnp.exp(-b)
    k_d = k * jnp.exp(b_last - b)
    mask = jnp.tril(jnp.ones((CHUNK, CHUNK), dtype=bool), 0 if include_diag else -1)
    scores = jnp.einsum('bhncd,bhnsd->bhncs', q_e, k_e)
    scores = jnp.where(mask, scores, 0.0)
    o_intra = jnp.einsum('bhncs,bhnse->bhnce', scores, v)
    kv = jnp.einsum('bhnsd,bhnse->bhnde', k_d, v)
    decay = jnp.exp(b_last[:, :, :, 0, :])

    def step(state, inp):
        kv_n, dec_n = inp
        return dec_n[..., None] * state + kv_n, state

    s0 = jnp.zeros((bsz, heads, dk, dv), jnp.float32)
    _, s_in = lax.scan(step, s0, (jnp.moveaxis(kv, 2, 0), jnp.moveaxis(decay, 2, 0)))
    s_in = jnp.moveaxis(s_in, 0, 2)
    o_inter = jnp.einsum('bhncd,bhnde->bhnce', q_e, s_in)
    return (o_intra + o_inter).reshape(bsz, heads, length, dv)


def to_heads(t, heads):
    bsz, length, width = t.shape
    return t.reshape(bsz, length, heads, width // heads).transpose(0, 2, 1, 3)


def multiscale_pool(u):
    bsz, length, _ = u.shape
    ug = u.reshape(bsz, length, POOL_GROUPS, POOL_GROUP_W).astype(jnp.float32)
    cs = jnp.concatenate([jnp.zeros((bsz, 1, POOL_GROUPS, POOL_GROUP_W), jnp.float32),
                          jnp.cumsum(ug, axis=1)], axis=1)
    pos = jnp.arange(length)[:, None]
    win = jnp.array(POOL_WINDOWS, dtype=jnp.int32)[None, :]
    lo = jnp.clip(pos - win // 2, 0, length)
    hi = jnp.clip(pos + win - win // 2, 0, length)
    gi = jnp.arange(POOL_GROUPS)[None, :]
    window_sum = cs[:, hi, gi] - cs[:, lo, gi]
    count = (hi - lo).astype(jnp.float32)[None, :, :, None]
    return window_sum / count - ug


def setup_inputs(seed: int = 0) -> dict:
    key = jax.random.key(seed)
    ks = jax.random.split(key, 16)

    def nrm(k, shape, scale):
        return jax.random.normal(k, shape, jnp.float32) * scale

    return {
        'x': nrm(ks[0], (BATCH, SEQ, D_MODEL), 1.0),
        'norm_mix': 1.0 + nrm(ks[1], (DEPTH, D_MODEL), 0.05),
        'w_in': nrm(ks[2], (DEPTH, D_MODEL, IN_WIDTH), D_MODEL ** -0.5),
        'w_decay_up_fwd': nrm(ks[3], (DEPTH, GATE_RANK, GLA_KEY_DIM), GATE_RANK ** -0.5),
        'b_decay_fwd': nrm(ks[4], (DEPTH, GLA_KEY_DIM), 0.1),
        'w_decay_up_bwd': nrm(ks[5], (DEPTH, GATE_RANK, GLA_KEY_DIM), GATE_RANK ** -0.5),
        'b_decay_bwd': nrm(ks[6], (DEPTH, GLA_KEY_DIM), 0.1),
        'gla_norm': 1.0 + nrm(ks[7], (DEPTH, GLA_VALUE_DIM), 0.05),
        'w_branch_gla': nrm(ks[8], (DEPTH, GLA_VALUE_DIM, D_MODEL), GLA_VALUE_DIM ** -0.5),
        'w_pool_group': nrm(ks[9], (DEPTH, POOL_GROUPS, POOL_GROUP_W, POOL_OUT_GROUP_W), POOL_GROUP_W ** -0.5),
        'pool_scale': 1.0 + nrm(ks[10], (DEPTH, D_MODEL), 0.05),
        'w_out': nrm(ks[11], (DEPTH, D_MODEL, D_MODEL), D_MODEL ** -0.5),
        'norm_ffn': 1.0 + nrm(ks[12], (DEPTH, D_MODEL), 0.05),
        'w_ffn_in': nrm(ks[13], (DEPTH, D_MODEL, 2 * D_FF), D_MODEL ** -0.5),
        'w_ffn_out': nrm(ks[14], (DEPTH, D_FF, D_MODEL), D_FF ** -0.5),
        'norm_final': 1.0 + nrm(ks[15], (D_MODEL,), 0.05),
    }


def reference(x, norm_mix, w_in, w_decay_up_fwd, b_decay_fwd, w_decay_up_bwd, b_decay_bwd,
              gla_norm, w_branch_gla, w_pool_group, pool_scale, w_out, norm_ffn,
              w_ffn_in, w_ffn_out, norm_final):
    bsz, length, _ = x.shape
    split_sizes = [GLA_KEY_DIM, GLA_KEY_DIM, GLA_VALUE_DIM, GLA_VALUE_DIM, GATE_RANK, GATE_RANK,
                   POOL_WIDTH, D_MODEL, D_MODEL]
    split_points = [int(p) for p in np.cumsum(split_sizes)[:-1]]
    for l in range(DEPTH):
        h = rmsnorm(x, norm_mix[l])
        proj = h @ w_in[l]
        q, k, v, r, lr_f, lr_b, u, g_a, g_b = jnp.split(proj, split_points, axis=-1)

        log_a_f = jax.nn.log_sigmoid((lr_f @ w_decay_up_fwd[l] + b_decay_fwd[l]).astype(jnp.float32)) / GATE_NORMALIZER
        log_a_b = jax.nn.log_sigmoid((lr_b @ w_decay_up_bwd[l] + b_decay_bwd[l]).astype(jnp.float32)) / GATE_NORMALIZER
        qh = to_heads(q, GLA_HEADS) * (GLA_HEAD_K ** -0.5)
        kh = to_heads(k, GLA_HEADS)
        vh = to_heads(v, GLA_HEADS)
        af = to_heads(log_a_f, GLA_HEADS)
        ab = to_heads(log_a_b, GLA_HEADS)
        o_fwd = gla_chunked(qh, kh, vh, af, include_diag=True)
        o_bwd = jnp.flip(gla_chunked(jnp.flip(qh, 2), jnp.flip(kh, 2), jnp.flip(vh, 2), jnp.flip(ab, 2),
                                     include_diag=False), 2)
        o = o_fwd + o_bwd
        o = o * lax.rsqrt(jnp.mean(o * o, axis=-1, keepdims=True) + EPS)
        o = o.transpose(0, 2, 1, 3).reshape(bsz, length, GLA_VALUE_DIM) * gla_norm[l].astype(jnp.float32)
        o = (o * jax.nn.silu(r.astype(jnp.float32))).astype(x.dtype)
        y_a = o @ w_branch_gla[l]

        pooled = multiscale_pool(u)
        y_b = jnp.einsum('blgc,gcd->blgd', pooled, w_pool_group[l]).reshape(bsz, length, D_MODEL)
        y_b = (y_b * pool_scale[l]).astype(x.dtype)

        merged = jax.nn.sigmoid(g_a) * y_a + jax.nn.sigmoid(g_b) * y_b
        x = x + merged @ w_out[l]

        h2 = rmsnorm(x, norm_ffn[l])
        gate, up = jnp.split(h2 @ w_ffn_in[l], 2, axis=-1)
        x = x + (jax.nn.silu(gate) * up) @ w_ffn_out[l]
    return rmsnorm(x, norm_final)
```

```python
import contextlib
import numpy as np
import concourse.bass as bass
import concourse.mybir as mybir
from concourse.bass_utils import run_bass_kernel_spmd

F32 = mybir.dt.float32
BF16 = mybir.dt.bfloat16
AF = mybir.ActivationFunctionType
ALU = mybir.AluOpType
AX = mybir.AxisListType

ENGS = ("pe", "act", "dve", "pool", "sp")
SAME_ENGINE_SYNC = True
SAME_ENGINE_WAR = True

NCORES = 8
TOK = 4096
D = 1024
T = 512
NB = TOK // T
NTL = T // 128
DFF = 2816
NJ = DFF // 128
INW = 5664
C_Q, C_K, C_V, C_R, C_LR, C_U, C_GA, C_GB = 0, 512, 1024, 2048, 3072, 3104, 3616, 4640
PAYW = 2048 + 8 + 1024
EPS = 1e-6
WLOOK = 1
NWSLOT = 4
MMB = 4
NDSEM = 36


OP_LIMIT = 0
MILESTONES = False
DBG_VARIANT = 0
SIM_CC = False


class StopRecording(Exception):
    pass


class Buf:
    __slots__ = ("name", "last_w", "readers", "dsem", "ndma", "psum", "temp", "dkind")

    def __init__(self, name):
        self.name = name
        self.last_w = None
        self.readers = []
        self.dsem = None
        self.ndma = 0
        self.psum = False
        self.temp = False
        self.dkind = None


class Op:
    __slots__ = ("eng", "fn", "idx", "waits", "dwaits", "signal", "sigval", "dma_buf", "dma_val")

    def __init__(self, eng, fn, idx):
        self.eng = eng
        self.fn = fn
        self.idx = idx
        self.waits = []
        self.dwaits = {}
        self.signal = False
        self.sigval = 0
        self.dma_buf = None
        self.dma_val = 0


class Sched:
    def __init__(self, nc, sems, dsem_pool):
        self.nc = nc
        self.sems = sems
        self.dsem_pool = dsem_pool
        self.sigcount = {e: 0 for e in ENGS}
        self.ops = {e: [] for e in ENGS}
        self.seen = {e: {f: -1 for f in ENGS} for e in ENGS}
        self.dseen = {e: {} for e in ENGS}
        self.bufs = []
        self.nops = {e: 0 for e in ENGS}
        self.pending_barrier = None
        self.dma_bufs_used = []
        self.total_ops = 0
        self._dma_base = {}
        self.unlimited = False

    def buf(self, name):
        b = Buf(name)
        self.bufs.append(b)
        return b

    def _add_dep(self, op, p):
        if p is None or p is op:
            return
        if p.dma_buf is not None:
            b = p.dma_buf
            if self.dseen[op.eng].get(b, 0) >= p.dma_val:
                return
            if op.dwaits.get(b, 0) < p.dma_val:
                op.dwaits[b] = p.dma_val
            return
        if p.eng == op.eng:
            if op.eng in ("pe", "sp") or not SAME_ENGINE_SYNC:
                return
        if self.seen[op.eng][p.eng] >= p.idx:
            return
        op.waits.append(p)

    def op(self, eng, fn, r=(), w=()):
        if OP_LIMIT and self.total_ops >= OP_LIMIT and not self.unlimited:
            return Op(eng, fn, -1)
        o = Op(eng, fn, self.nops[eng])
        self.nops[eng] += 1
        self.total_ops += 1
        for b in r:
            self._add_dep(o, b.last_w)
            if b.psum:
                for rd in b.readers:
                    if rd.eng != eng:
                        self._add_dep(o, rd)
        for b in w:
            self._add_dep(o, b.last_w)
            for rd in b.readers:
                if rd.eng == eng and rd.dma_buf is None and (eng in ("pe", "sp") or not SAME_ENGINE_SYNC or not SAME_ENGINE_WAR):
                    continue
                self._add_dep(o, rd)
        best = {}
        for p in o.waits:
            if p.eng not in best or best[p.eng].idx < p.idx:
                best[p.eng] = p
        o.waits = list(best.values())
        for p in o.waits:
            p.signal = True
            self.seen[eng][p.eng] = p.idx
        for b, v in o.dwaits.items():
            self.dseen[eng][b] = v
        for b in r:
            b.readers.append(o)
        for b in w:
            b.last_w = o
            b.readers = []
        self.ops[eng].append(o)
        return o

    def dma(self, eng, fn, buf, load, r=(), w=()):
        if buf.dsem is None:
            buf.dkind = "sw" if eng == "pool" else "hw"
            buf.dsem, buf.ndma = self.dsem_pool[buf.dkind].pop()
            self.dma_bufs_used.append(buf)
            self._dma_base[id(buf)] = 16 * buf.ndma
        assert buf.dkind == ("sw" if eng == "pool" else "hw"), buf.name
        rr = list(r) + ([] if load else [buf])
        ww = list(w) + ([buf] if load else [])
        o = self.op(eng, fn, rr, ww)
        if o.idx < 0:
            return o
        buf.ndma += 1
        o.dma_buf = buf
        o.dma_val = 16 * buf.ndma
        return o

    def _simulate(self):
        ptr = {e: 0 for e in ENGS}
        done = set()
        dmac = {}
        progress = True
        while progress:
            progress = False
            for e in ENGS:
                while ptr[e] < len(self.ops[e]):
                    o = self.ops[e][ptr[e]]
                    ok = all(id(p) in done for p in o.waits) and all(dmac.get(id(b), 0) >= v for b, v in o.dwaits.items())
                    if not ok:
                        break
                    done.add(id(o))
                    if o.dma_buf is not None:
                        dmac[id(o.dma_buf)] = dmac.get(id(o.dma_buf), self._dma_base.get(id(o.dma_buf), 0)) + 16
                    ptr[e] += 1
                    progress = True
        stuck = {e: (ptr[e], len(self.ops[e])) for e in ENGS if ptr[e] < len(self.ops[e])}
        if stuck:
            for e in stuck:
                o = self.ops[e][ptr[e]]
                print("STUCK", e, ptr[e], [(p.eng, p.idx, id(p) in done) for p in o.waits], [(b.name, v, dmac.get(id(b), 0)) for b, v in o.dwaits.items()])
            raise RuntimeError(f"static deadlock: {stuck}")
        for b in self.dma_bufs_used:
            self._dma_base[id(b)] = 16 * b.ndma

    def emit(self, final=False):
        nc = self.nc
        for e in ENGS:
            comp = [o for o in self.ops[e] if o.dma_buf is None]
            if comp:
                comp[-1].signal = True
        for e in ENGS:
            c = self.sigcount[e]
            for o in self.ops[e]:
                if o.signal and o.dma_buf is None:
                    c += 1
                    o.sigval = c
            self.sigcount[e] = c
        self._simulate()
        prev_barrier = self.pending_barrier
        end_vals = {e: self.sigcount[e] for e in ENGS}
        dma_end = [(b.dsem, 16 * b.ndma) for b in self.dma_bufs_used]
        ops = self.ops
        sems = self.sems

        def body(ename):
            def run(eng):
                if prev_barrier is not None:
                    ev, dv = prev_barrier
                    for f in ENGS:
                        if f != ename and ev[f] > 0:
                            eng.wait_ge(sems[f], ev[f])
                    for (ds, v) in dv:
                        if v > 0:
                            eng.wait_ge(ds, v)
                for o in ops[ename]:
                    for p in o.waits:
                        eng.wait_ge(sems[p.eng], p.sigval)
                    for b, v in o.dwaits.items():
                        eng.wait_ge(b.dsem, v)
                    inst = o.fn(eng)
                    if o.dma_buf is not None:
                        inst.then_inc(o.dma_buf.dsem, 16)
                    elif o.signal:
                        inst.then_inc(sems[ename], 1)
                if final:
                    for f in ENGS:
                        if f != ename and end_vals[f] > 0:
                            eng.wait_ge(sems[f], end_vals[f])
                    for (ds, v) in dma_end:
                        if v > 0:
                            eng.wait_ge(ds, v)
            return run

        with nc.Block() as block:
            block.tensor(body("pe"))
            block.scalar(body("act"))
            block.vector(body("dve"))
            block.gpsimd(body("pool"))
            block.sync(body("sp"))
        self.pending_barrier = (end_vals, dma_end)
        keep = []
        for b in self.dma_bufs_used:
            if b.temp:
                self.dsem_pool[b.dkind].append((b.dsem, b.ndma))
                b.dsem = None
            else:
                keep.append(b)
        self.dma_bufs_used = keep
        self.ops = {e: [] for e in ENGS}
        self.seen = {e: {f: -1 for f in ENGS} for e in ENGS}
        self.dseen = {e: {} for e in ENGS}
        self.nops = {e: 0 for e in ENGS}
        for b in self.bufs:
            b.last_w = None
            b.readers = []


def mk(name, *args, **kw):
    def fn(e):
        return getattr(e, name)(*args, **kw)
    return fn


class Ring:
    def __init__(self, items):
        self.items = items
        self.i = 0

    def next(self):
        it = self.items[self.i % len(self.items)]
        self.i += 1
        return it


def build(n_layers=2, dbg=False, stop_after=None):
    nc = bass.Bass("TRN2", target_bir_lowering=False)
    dk = "ExternalOutput" if dbg else "Internal"

    def din(name, shape, dt=F32):
        return nc.dram_tensor(name, shape, dt, kind="ExternalInput").ap()

    x_in = din("x", [TOK, D])
    w_in = din("w_in", [2, D, INW])
    w_upf = din("w_decay_up_fwd", [2, 16, 512])
    w_upb = din("w_decay_up_bwd", [2, 16, 512])
    w_bg = din("w_branch_gla", [2, D, D])
    w_pg = din("w_pool_group", [2, 4, 128, 256])
    w_o = din("w_out", [2, D, D])
    w_fi = din("w_ffn_in", [2, D, 2 * DFF])
    w_fo = din("w_ffn_out", [2, DFF, D])
    vecs_d = din("vecs", [128, 80])
    gfin_d = din("gfin", [128, D])
    ident_d = din("ident", [128, 128])
    masks_d = din("masks", [128, 2, 128])
    bands_d = din("bands", [128, 7, 4, 128])
    sel_d = din("sel", [128, 16])
    out_d = nc.dram_tensor("out", [TOK, D], F32, kind="ExternalOutput").ap()
    XA = nc.dram_tensor("XA", [TOK, D], F32, kind=dk).ap()
    XB = nc.dram_tensor("XB", [TOK, D], F32, kind=dk).ap()
    UALL = nc.dram_tensor("UALL", [34, 128, 512], BF16, kind="Internal").ap()
    SLOC = nc.dram_tensor("SLOC", [NB, 2, 128, 1024], F32, kind="Internal").ap()
    SIN = nc.dram_tensor("SIN", [NB, 2, 128, 1024], F32, kind=dk).ap()
    SEGa = nc.dram_tensor("SEGa", [128, 2048], F32, kind="Internal").ap()
    GATHa = nc.dram_tensor("GATHa", [4 * 128, 2048], F32, kind="Internal").ap()
    SEGb = nc.dram_tensor("SEGb", [128, PAYW - 2048], F32, kind="Internal").ap()
    GATHb = nc.dram_tensor("GATHb", [4 * 128, PAYW - 2048], F32, kind="Internal").ap()

    with contextlib.ExitStack() as top:
        E = top.enter_context
        sems = {e: E(nc.semaphore("s_" + e)) for e in ENGS}
        dpool = {"hw": [(E(nc.semaphore(f"dh{i}")), 0) for i in range(NDSEM)], "sw": [(E(nc.semaphore(f"ds{i}")), 0) for i in range(16)]}
        ccsem = E(nc.semaphore("ccsem"))
        S = Sched(nc, sems, dpool)
        cc_count = [0]

        uid = [0]

        def sb(st, name, shape, dt):
            uid[0] += 1
            t = st.enter_context(nc.sbuf_tensor(f"sb{uid[0]}_{name}", shape, dt))
            b = S.buf(name)
            b.temp = st is not top
            return t, b

        ident, B_ident = sb(top, "ident", [128, 128], BF16)
        vecs, B_vecs = sb(top, "vecs", [128, 80], F32)
        negb, B_negb = sb(top, "negb", [128, 16], F32)
        gnh, B_gnh = sb(top, "gnh", [128, 16], F32)
        psh, B_psh = sb(top, "psh", [128, 16], F32)
        sel, B_sel = sb(top, "sel", [128, 16], F32)
        LD, B_LD = sb(top, "LD", [128, 2, 4, NB], F32)
        wsl = [sb(top, f"wslot{i}", [128, 8, 512], BF16) for i in range(NWSLOT)]
        xt = [sb(top, f"xt{i}", [128, D], F32) for i in range(NTL)]
        xn = [sb(top, f"xn{i}", [128, D], BF16) for i in range(NTL)]
        hT0 = sb(top, "hT", [128, 8, T], BF16)
        xtB = [sb(top, f"xtB{i}", [128, D], F32) for i in range(NTL)]
        xts = [xt, xtB]

        class CUR:
            pass
        CUR.xt = xt
        CUR.hT, CUR.B_hT = hT0

        def setcur(j, hTs):
            CUR.xt = xts[j % 2]
            CUR.hT, CUR.B_hT = hTs[j % 2]
        junk, _ = sb(top, "junk", [128, D], BF16)
        stat, B_stat = sb(top, "stat", [128, 8], F32)
        pbanks = []
        for i in range(8):
            p = E(nc.psum_tensor(f"ps{i}", [128, 512], F32))
            pbanks.append(p)
        bankB = [S.buf(f"psb{i}") for i in range(8)]
        for b_ in bankB:
            b_.psum = True
        mmring = Ring([(pbanks[i], pbanks[i].bitcast(BF16), bankB[i]) for i in range(MMB)])
        gring = Ring([(pbanks[i], 0, bankB[i]) for i in range(MMB, 8)])
        fring = Ring([(pbanks[i], pbanks[i].bitcast(BF16), bankB[i]) for i in range(8)])
        wring = Ring(wsl)

        wlive = {}

        class WS:
            slots = list(wsl)

        class WQ:
            def __init__(self):
                self.plan = []
                self.issued = 0
                self.slots = {}

            def add(self, key, src, nk=8, ncols=512):
                self.plan.append((key, src, nk, ncols))

            def _issue(self, i):
                key, src, nk, ncols = self.plan[i]
                free = [k for k in range(len(WS.slots)) if k not in wlive]
                assert free, ("no free weight slot for", key, dict(wlive))
                k = free[0]
                wlive[k] = key
                t, b = WS.slots[k]
                S.dma("pool", mk("dma_start", out=t[:, 0:nk, 0:ncols], in_=src), b, True)
                self.slots[key] = (t, b, k)

            def get(self, key):
                idx = [i for i, p in enumerate(self.plan) if p[0] == key][0]
                while self.issued <= idx:
                    self._issue(self.issued)
                    self.issued += 1
                self.prefetch()
                t, b, k = self.slots[key]
                assert wlive.get(k) == key, (key, wlive)
                return t, b

            def prefetch(self):
                while self.issued < len(self.plan) and len(wlive) < len(WS.slots):
                    self._issue(self.issued)
                    self.issued += 1

            def done(self, key):
                t, b, k = self.slots[key]
                assert wlive.get(k) == key
                del wlive[k]
                self.prefetch()

        def win_grp(l, c0, n=512):
            return w_in[l].rearrange("(kc p) c -> p kc c", p=128)[:, :, c0:c0 + n]

        def load_consts():
            S.dma("pool", mk("dma_start", out=ident[:], in_=ident_d), B_ident, True)
            S.dma("sp", mk("dma_start", out=vecs[:], in_=vecs_d), B_vecs, True)
            S.dma("sp", mk("dma_start", out=sel[:], in_=sel_d), B_sel, True)
            S.op("pool", mk("memset", mhw[:], -0.5), w=[B_mhw])
            for l in range(2):
                S.op("dve", mk("tensor_scalar", out=negb[:, l * 8:(l + 1) * 8], in0=vecs[:, l * 40 + 32:l * 40 + 40],
                                                           scalar1=-1.0, scalar2=None, op0=ALU.mult), r=[B_vecs], w=[B_negb])
                S.op("dve", mk("tensor_scalar", out=gnh[:, l * 8:(l + 1) * 8], in0=vecs[:, l * 40 + 16:l * 40 + 24],
                                                           scalar1=0.5, scalar2=None, op0=ALU.mult), r=[B_vecs], w=[B_gnh])
                S.op("dve", mk("tensor_scalar", out=psh[:, l * 8:(l + 1) * 8], in0=vecs[:, l * 40 + 24:l * 40 + 32],
                                                           scalar1=0.5, scalar2=None, op0=ALU.mult), r=[B_vecs], w=[B_psh])

        def load_x(src, j):
            for t in range(NTL):
                r0 = j * T + t * 128
                S.dma("sp", mk("dma_start", out=CUR.xt[t][0][:], in_=src[r0:r0 + 128, :]), CUR.xt[t][1], True)

        mhw, B_mhw = sb(top, "mhw", [128, 16], F32)

        def rsqrt_cols(tile, B, c0, c1, scale):
            S.op("dve", mk("tensor_scalar", out=tile[:, c0:c1], in0=tile[:, c0:c1], scalar1=scale, scalar2=EPS,
                                                  op0=ALU.mult, op1=ALU.add), r=[B], w=[B])
            S.op("pool", mk("tensor_tensor", out=tile[:, c0:c1], in0=tile[:, c0:c1], in1=mhw[:, 0:c1 - c0], op=ALU.pow),
                 r=[B, B_mhw], w=[B])

        def norm_block(gcol):
            for t in range(NTL):
                S.op("act", mk("activation", out=junk[:], in_=CUR.xt[t][0][:], func=AF.Square, accum_out=stat[:, t:t + 1]),
                     r=[CUR.xt[t][1]], w=[B_stat])
            rsqrt_cols(stat, B_stat, 0, NTL, 1.0 / D)
            for t in range(NTL):
                S.op("dve", mk("tensor_scalar", out=xn[t][0][:], in0=CUR.xt[t][0][:], scalar1=stat[:, t:t + 1], scalar2=None,
                                                           op0=ALU.mult), r=[CUR.xt[t][1], B_stat], w=[xn[t][1]])
            for pc in range(4):
                pf, pb, Bp = mmring.next()
                for t in range(NTL):
                    for c2 in range(2):
                        ch = 2 * pc + c2
                        S.op("pe", mk("transpose",
                            out=pb[:, c2 * 512 + t * 128:c2 * 512 + (t + 1) * 128], in_=xn[t][0][:, ch * 128:(ch + 1) * 128], identity=ident[:]),
                            r=[xn[t][1], B_ident], w=[Bp])
                for c2 in range(2):
                    ch = 2 * pc + c2
                    S.op("act", mk("activation", out=CUR.hT[:, ch, :], in_=pb[:, c2 * 512:(c2 + 1) * 512], func=AF.Identity,
                                                                          scale=vecs[:, gcol + ch:gcol + ch + 1]), r=[Bp, B_vecs], w=[CUR.B_hT])

        def fm_matmul(wt, Bw, cofs, dst_ap, Bd):
            for kc in range(8):
                S.op("pe", mk("matmul", out=dst_ap, lhsT=wt[:, kc, cofs:cofs + 128], rhs=CUR.hT[:, kc, :], start=(kc == 0), stop=(kc == 7)),
                     r=[Bw, CUR.B_hT], w=[Bd])

        def tm_matmul(t, wt, Bw, dst_ap, Bd, ncols=512):
            for kc in range(8):
                S.op("pe", mk("matmul", out=dst_ap, lhsT=CUR.hT[:, kc, t * 128:(t + 1) * 128], rhs=wt[:, kc, 0:ncols], start=(kc == 0), stop=(kc == 7)),
                     r=[Bw, CUR.B_hT], w=[Bd])

        def decay_chain(P, l, h, dr, full):
            sp, B_sp = P["sp"]
            bneg, B_bn = P["bneg"]
            tmp, B_tmp = P["tmp"]
            Ed, B_Ed = P["Ed"][dr]
            nbl, B_nbl = P["nbl"]
            dec, B_dec = P["dec"]
            lrT, B_lrT = P["lrT"]
            wup, B_wup = P["wup"]
            di = dr * 4 + h
            pf, pb, Bp = mmring.next()
            S.op("pe", mk("matmul", out=pf[:, :], lhsT=wup[0:32, dr, h * 128:(h + 1) * 128], rhs=lrT[0:32, :], start=True, stop=True),
                 r=[B_wup, B_lrT], w=[Bp])
            S.op("act", mk("activation", out=sp[:], in_=pf[:, :], func=AF.Exp, bias=negb[:, l * 8 + di:l * 8 + di + 1], scale=-1.0),
                 r=[Bp, B_negb], w=[B_sp])
            S.op("act", mk("activation", out=sp[:], in_=sp[:], func=AF.Ln, bias=1.0, scale=1.0), r=[B_sp], w=[B_sp])
            S.op("dve", mk("tensor_tensor_scan", out=bneg[:], data0=P["msk"][0][:], data1=sp[:], initial=0.0, op0=ALU.mult, op1=ALU.add),
                 r=[B_sp, P["msk"][1]], w=[B_bn])
            S.op("dve", mk("tensor_scalar", out=nbl[:, di * 4:di * 4 + 4], in0=bneg[:].rearrange("p (c t) -> p c t", t=128)[:, :, 127],
                                                  scalar1=-1.0 / 16, scalar2=None, op0=ALU.mult), r=[B_bn], w=[B_nbl])
            S.op("act", mk("activation", out=dec[:, di * 4:di * 4 + 4], in_=nbl[:, di * 4:di * 4 + 4], func=AF.Exp), r=[B_nbl], w=[B_dec])
            if dr == 0:
                for c in range(4):
                    S.op("act", mk("activation", out=Ed[:, c * 128:(c + 1) * 128], in_=bneg[:, c * 128:(c + 1) * 128], func=AF.Exp,
                                                            bias=nbl[:, di * 4 + c:di * 4 + c + 1], scale=1.0 / 16), r=[B_bn, B_nbl], w=[B_Ed])
                if full:
                    Ep, B_Ep = P["Ep"][dr]
                    Em, B_Em = P["Em"][dr]
                    S.op("act", mk("activation", out=Ep[:], in_=bneg[:], func=AF.Exp, scale=-1.0 / 16), r=[B_bn], w=[B_Ep])
                    S.op("act", mk("activation", out=Em[:], in_=bneg[:], func=AF.Exp, scale=1.0 / 16), r=[B_bn], w=[B_Em])
            else:
                S.op("dve", mk("tensor_tensor", out=tmp[:], in0=sp[:], in1=bneg[:], op=ALU.subtract), r=[B_sp, B_bn], w=[B_tmp])
                S.op("act", mk("activation", out=Ed[:], in_=tmp[:], func=AF.Exp, scale=1.0 / 16), r=[B_tmp], w=[B_Ed])
                if full:
                    Ep, B_Ep = P["Ep"][dr]
                    Em, B_Em = P["Em"][dr]
                    nnbl, B_nn = P["nnbl"]
                    S.op("dve", mk("tensor_scalar", out=nnbl[:, 0:4], in0=nbl[:, di * 4:di * 4 + 4], scalar1=-1.0, scalar2=None, op0=ALU.mult),
                         r=[B_nbl], w=[B_nn])
                    for c in range(4):
                        S.op("act", mk("activation", out=Ep[:, c * 128:(c + 1) * 128], in_=tmp[:, c * 128:(c + 1) * 128], func=AF.Exp,
                                                                bias=nbl[:, di * 4 + c:di * 4 + c + 1], scale=-1.0 / 16), r=[B_tmp, B_nbl], w=[B_Ep])
                        S.op("act", mk("activation", out=Em[:, c * 128:(c + 1) * 128], in_=tmp[:, c * 128:(c + 1) * 128], func=AF.Exp,
                                                                bias=nnbl[:, c:c + 1], scale=1.0 / 16), r=[B_tmp, B_nn], w=[B_Em])

        def alloc_decay_tiles(st, P, full):
            P["sp"] = sb(st, "sp", [128, T], F32)
            P["bneg"] = sb(st, "bneg", [128, T], F32)
            P["tmp"] = sb(st, "tmpd", [128, T], F32)
            P["Ed"] = [sb(st, f"Ed{d}", [128, T], F32) for d in range(2)]
            P["nbl"] = sb(st, "nbl", [128, 32], F32)
            P["nnbl"] = sb(st, "nnbl", [128, 4], F32)
            P["dec"] = sb(st, "dec", [128, 32], F32)
            P["lrT"] = sb(st, "lrT", [128, T], BF16)
            P["wup"] = sb(st, "wup", [32, 2, 512], BF16)
            P["msk"] = sb(st, "msk", [128, T], F32)
            P["kdT"] = [sb(st, f"kdT{d}", [128, T], BF16) for d in range(2)]
            P["kd"] = [sb(st, f"kd{d}", [128, NTL, 128], BF16) for d in range(2)]
            P["v"] = sb(st, "v", [128, NTL, D], BF16)
            if full:
                P["Ep"] = [sb(st, f"Ep{d}", [128, T], F32) for d in range(2)]
                P["Em"] = [sb(st, f"Em{d}", [128, T], F32) for d in range(2)]

        def phase_setup(P, l):
            msk, B_msk = P["msk"]
            wup, B_wup = P["wup"]
            S.op("dve", mk("memset", msk[:], 1.0), w=[B_msk])
            S.op("dve", mk("memset", msk[:].rearrange("p (c t) -> p c t", t=128)[:, :, 0:1], 0.0), w=[B_msk])
            S.op("dve", mk("memset", wup[:], 0.0), w=[B_wup])
            S.dma("pool", mk("dma_start", out=wup[0:16, 0, :], in_=w_upf[l]), B_wup, True)
            S.dma("pool", mk("dma_start", out=wup[16:32, 1, :], in_=w_upb[l]), B_wup, True)

        def lr_and_v(P, l, wq, j):
            lrT, B_lrT = P["lrT"]
            v, B_v = P["v"]
            wt, Bw = wq.get(("lr", j))
            pf, pb, Bp = mmring.next()
            for kc in range(8):
                S.op("pe", mk("matmul", out=pf[0:32, :], lhsT=wt[:, kc, 0:32], rhs=CUR.hT[:, kc, :], start=(kc == 0), stop=(kc == 7)),
                     r=[Bw, CUR.B_hT], w=[Bp])
            S.op("act", mk("activation", out=lrT[0:32, :], in_=pf[0:32, :], func=AF.Identity), r=[Bp], w=[B_lrT])
            wq.done(("lr", j))

        def v_tiles(P, wq, j):
            v, B_v = P["v"]
            for half in range(2):
                wt, Bw = wq.get(("v", j, half))
                for t in range(NTL):
                    pf, pb, Bp = mmring.next()
                    tm_matmul(t, wt, Bw, pf[:, :], Bp)
                    S.op("act", mk("activation", out=v[:, t, half * 512:(half + 1) * 512], in_=pf[:, :], func=AF.Identity),
                         r=[Bp], w=[B_v])
                wq.done(("v", j, half))

        def k_side(P, wq, j, h, full):
            wt, Bw = wq.get(("k", j))
            pf, pb, Bp = mmring.next()
            fm_matmul(wt, Bw, h * 128, pf[:, :], Bp)
            if h == 3:
                wq.done(("k", j))
            for dr in range(2):
                kdT, B_kdT = P["kdT"][dr]
                Ed, B_Ed = P["Ed"][dr]
                S.op("dve", mk("tensor_tensor", out=kdT[:], in0=pf[:, :], in1=Ed[:], op=ALU.mult), r=[Bp, B_Ed], w=[B_kdT])
                if full:
                    keT, B_keT = P["keT"][dr]
                    Em, B_Em = P["Em"][dr]
                    S.op("dve", mk("tensor_tensor", out=keT[:], in0=pf[:, :], in1=Em[:], op=ALU.mult), r=[Bp, B_Em], w=[B_keT])
            tf, tb, Bt = mmring.next()
            for dr in range(2):
                kdT, B_kdT = P["kdT"][dr]
                for t in range(NTL):
                    S.op("pe", mk("transpose", out=tb[:, dr * 512 + t * 128:dr * 512 + (t + 1) * 128],
                                                                          in_=kdT[:, t * 128:(t + 1) * 128], identity=ident[:]),
                         r=[B_kdT, B_ident], w=[Bt])
            for dr in range(2):
                kd, B_kd = P["kd"][dr]
                S.op("act", mk("activation", out=kd[:].rearrange("p t d -> p (t d)"), in_=tb[:, dr * 512:(dr + 1) * 512], func=AF.Identity),
                     r=[Bt], w=[B_kd])

        def kv_mm(P, h, dr, t):
            kd, B_kd = P["kd"][dr]
            v, B_v = P["v"]
            pk, cof, Bg = gring.next()
            tt = 0 if DBG_VARIANT == 1 else t
            if DBG_VARIANT == 2:
                cof = 0
            S.op("pe", mk("matmul", out=pk[:, cof:cof + 256], lhsT=kd[:, tt, :], rhs=v[:, tt, h * 256:(h + 1) * 256], start=True, stop=True),
                 r=[B_kd, B_v], w=[Bg])
            return pk, cof, Bg

        def phase_A(l, xsrc, pay, B_pay):
            with contextlib.ExitStack() as st:
                P = {}
                alloc_decay_tiles(st, P, False)
                Sw = [sb(st, f"Sw{i}", [128, 1024], F32) for i in range(2)]
                ubf = [sb(st, f"ubf{i}", [128, 512], BF16) for i in range(2)]
                if l == 0:
                    load_consts()
                phase_setup(P, l)
                ubr = Ring(ubf)
                hTs = [hT0, sb(st, "hT1", [128, 8, T], BF16)]
                WS.slots = list(wsl) + [sb(st, f"wx{i}", [128, 8, 512], BF16) for i in range(2)]
                EdA = [[sb(st, f"EdA{h}{d}", [128, T], F32) for d in range(2)] for h in range(4)]
                kdTs = [P["kdT"], [sb(st, f"kdTb{d}", [128, T], BF16) for d in range(2)]]
                kds = [P["kd"], [sb(st, f"kdb{d}", [128, NTL, 128], BF16) for d in range(2)]]
                scrA = [(P["sp"], P["bneg"], P["tmp"]), (sb(st, "sp2", [128, T], F32), sb(st, "bneg2", [128, T], F32), sb(st, "tmp2", [128, T], F32))]

                def Pv(h, k):
                    d = dict(P)
                    d["Ed"] = EdA[h]
                    d["sp"], d["bneg"], d["tmp"] = scrA[k % 2]
                    return d

                def u_section(wq, j):
                        wt, Bw = wq.get(("u", j))
                        for t in range(NTL):
                            pf, pb, Bp = mmring.next()
                            tm_matmul(t, wt, Bw, pf[:, :], Bp)
                            ub, B_ub = ubr.next()
                            gt = j * NTL + t
                            if gt == 0 or gt == NB * NTL - 1:
                                pc0 = 2056 if gt == 0 else 2056 + 512
                                S.op("act", mk("activation", out=pay[:, pc0:pc0 + 512], in_=pf[:, :], func=AF.Identity), r=[Bp], w=[B_pay])
                                S.op("dve", mk("tensor_copy", out=ub[:], in_=pay[:, pc0:pc0 + 512]), r=[B_pay], w=[B_ub])
                            else:
                                S.op("act", mk("activation", out=ub[:], in_=pf[:, :], func=AF.Identity), r=[Bp], w=[B_ub])
                            S.dma("sp", mk("dma_start", out=UALL[1 + gt], in_=ub[:]), B_ub, False)
                        wq.done(("u", j))

                setcur(0, hTs)
                load_x(xsrc, 0)
                norm_block(l * 40 + 0)
                wq = WQ()
                for j in range(NB):
                    wq.add(("lr", j), win_grp(l, C_LR, 32), 8, 32)
                    for half in range(2):
                        wq.add(("v", j, half), win_grp(l, C_V + half * 512))
                    wq.add(("u", j), win_grp(l, C_U))
                    wq.add(("k", j), win_grp(l, C_K))
                for j in range(NB):
                    setcur(j, hTs)
                    if j + 1 < NB:
                        setcur(j + 1, hTs)
                        load_x(xsrc, j + 1)
                        setcur(j, hTs)
                    ms = (lambda nm: print("MS", nm, S.total_ops)) if (MILESTONES and j == 0) else (lambda nm: None)
                    ms("start")
                    lr_and_v(P, l, wq, j)
                    cnt = 0
                    for h in range(4):
                        for dr in range(2):
                            decay_chain(Pv(h, cnt), l, h, dr, False)
                            cnt += 1
                    v_tiles(P, wq, j)
                    u_section(wq, j)
                    def k_part1(h):
                        Pq = Pv(h, 0)
                        wt, Bw = wq.get(("k", j))
                        pf, pb, Bp = mmring.next()
                        fm_matmul(wt, Bw, h * 128, pf[:, :], Bp)
                        if h == 3:
                            wq.done(("k", j))
                        for dr in range(2):
                            kdT, B_kdT = kdTs[h % 2][dr]
                            Ed, B_Ed = Pq["Ed"][dr]
                            S.op("dve", mk("tensor_tensor", out=kdT[:], in0=pf[:, :], in1=Ed[:], op=ALU.mult), r=[Bp, B_Ed], w=[B_kdT])

                    def k_part2(h):
                        tf, tb, Bt = mmring.next()
                        for dr in range(2):
                            kdT, B_kdT = kdTs[h % 2][dr]
                            for t in range(NTL):
                                S.op("pe", mk("transpose", out=tb[:, dr * 512 + t * 128:dr * 512 + (t + 1) * 128],
                                              in_=kdT[:, t * 128:(t + 1) * 128], identity=ident[:]), r=[B_kdT, B_ident], w=[Bt])
                        for dr in range(2):
                            kd, B_kd = kds[h % 2][dr]
                            S.op("act", mk("activation", out=kd[:].rearrange("p t d -> p (t d)"), in_=tb[:, dr * 512:(dr + 1) * 512], func=AF.Identity),
                                 r=[Bt], w=[B_kd])

                    def states(h):
                        Ph = dict(P)
                        Ph["kd"] = kds[h % 2]
                        for dr in range(2):
                            di = dr * 4 + h
                            sw, B_sw = Sw[dr]
                            order = range(NTL) if dr == 0 else range(NTL - 1, -1, -1)
                            first = True
                            for t in order:
                                pk, cof, Bg = kv_mm(Ph, h, dr, t)
                                if first:
                                    S.op("act", mk("activation", out=sw[:, h * 256:(h + 1) * 256], in_=pk[:, cof:cof + 256], func=AF.Identity),
                                         r=[Bg], w=[B_sw])
                                    first = False
                                else:
                                    S.op("dve", mk("scalar_tensor_tensor",
                                        out=sw[:, h * 256:(h + 1) * 256], in0=sw[:, h * 256:(h + 1) * 256], scalar=P["dec"][0][:, di * 4 + t:di * 4 + t + 1],
                                        in1=pk[:, cof:cof + 256], op0=ALU.mult, op1=ALU.add), r=[B_sw, Bg, P["dec"][1]], w=[B_sw])
                            S.op("dve", mk("tensor_reduce", out=LD[:, dr, h, j:j + 1], in_=P["nbl"][0][:, di * 4:di * 4 + 4], axis=AX.X, op=ALU.add),
                                 r=[P["nbl"][1]], w=[B_LD])

                    k_part1(0)
                    k_part1(1)
                    k_part2(0)
                    if j + 1 < NB:
                        setcur(j + 1, hTs)
                        norm_block(l * 40 + 0)
                        setcur(j, hTs)
                    for h in range(4):
                        if h + 2 < 4:
                            k_part1(h + 2)
                        states(h)
                        if h + 1 < 4:
                            k_part2(h + 1)
                    for dr in range(2):
                        S.dma("sp", mk("dma_start", out=SLOC[j, dr], in_=Sw[dr][0][:]), Sw[dr][1], False)
                    if MILESTONES:
                        print("MS endblock", j, S.total_ops)
                S.emit()
                assert not wlive
                WS.slots = list(wsl)

        def phase_C(l, pay, B_pay):
            if True:
                with contextlib.ExitStack() as st:
                    Dall, B_D = sb(st, "Dall", [128, 2, 4, NB], F32)
                    Lt, B_Lt = sb(st, "Lt", [128, 8], F32)
                    stg = [sb(st, f"stg{i}", [128, 1024], F32) for i in range(4)]
                    car = [[sb(st, f"car{d}_{i}", [128, 1024], F32) for i in range(2)] for d in range(2)]
                    G = [sb(st, f"G{i}", [128, PAYW], F32) for i in range(2)]
                    tmpc, B_tmpc = sb(st, "tmpc", [128, 1024], F32)
                    Dm, B_Dm = sb(st, "Dm", [128, 8], F32)
                    uh = [sb(st, f"uh{i}", [128, 512], F32) for i in range(2)]
                    uhb = [sb(st, f"uhb{i}", [128, 512], BF16) for i in range(2)]
                    stgr = Ring(stg)
                    Gr = Ring(G)
                    S.op("act", mk("activation", out=Dall[:].rearrange("p a b c -> p (a b c)"), in_=LD[:].rearrange("p a b c -> p (a b c)"), func=AF.Exp),
                         r=[B_LD], w=[B_D])
                    S.op("dve", mk("tensor_reduce", out=Lt[:, 0:8], in_=LD[:].rearrange("p a b c -> p (a b) c"), axis=AX.X, op=ALU.add), r=[B_LD], w=[B_Lt])
                    S.op("act", mk("activation", out=pay[:, 2048:2056], in_=Lt[:, 0:8], func=AF.Exp), r=[B_Lt], w=[B_pay])
                    for dr in range(2):
                        order = list(range(NB)) if dr == 0 else list(range(NB - 1, -1, -1))
                        for n, j in enumerate(order):
                            if n == 0:
                                S.dma("sp", mk("dma_start", out=pay[:, dr * 1024:(dr + 1) * 1024], in_=SLOC[j, dr]), B_pay, True)
                            else:
                                sg, B_sg = stgr.next()
                                S.dma("sp", mk("dma_start", out=sg[:], in_=SLOC[j, dr]), B_sg, True)
                                for h in range(4):
                                    S.op("dve", mk("scalar_tensor_tensor",
                                        out=pay[:, dr * 1024 + h * 256:dr * 1024 + (h + 1) * 256], in0=pay[:, dr * 1024 + h * 256:dr * 1024 + (h + 1) * 256],
                                        scalar=Dall[:, dr, h, j:j + 1], in1=sg[:, h * 256:(h + 1) * 256], op0=ALU.mult, op1=ALU.add),
                                        r=[B_pay, B_sg, B_D], w=[B_pay])
                    S.dma("sp", mk("dma_start", out=SEGa, in_=pay[:, 0:2048]), B_pay, False)
                    S.dma("sp", mk("dma_start", out=SEGb, in_=pay[:, 2048:PAYW]), B_pay, False)
                    S.emit()
                    cc_count[0] += 1
                    ccv = cc_count[0]
                    B_cc = S.buf("ccdummy")

                    def ccfn(e):
                        if SIM_CC:
                            for rr in range(4):
                                e.dma_start(out=GATHa[rr * 128:(rr + 1) * 128, :], in_=SEGa).then_inc(ccsem, 16)
                                e.dma_start(out=GATHb[rr * 128:(rr + 1) * 128, :], in_=SEGb).then_inc(ccsem, 16)
                            e.wait_ge(ccsem, 128 * ccv)
                            return e.memset(Dm[:, 0:1], 0.0)
                        for (si, go) in ((SEGa, GATHa), (SEGb, GATHb)):
                            i = e.collective_compute("AllGather", ALU.bypass, replica_groups=[[0, 1, 2, 3], [4, 5, 6, 7]], ins=[si], outs=[go])
                            i.then_inc(ccsem, 1)
                        e.wait_ge(ccsem, 2 * ccv)
                        return e.memset(Dm[:, 0:1], 0.0)
                    S.op("pool", ccfn, w=[B_Dm, B_cc])
                    for dr in range(2):
                        c0, B_c0 = car[dr][0]
                        S.op("dve", mk("memset", c0[:], 0.0), w=[B_c0])
                    for dr in range(2):
                        c0, B_c0 = car[dr][0]
                        ranks = range(4) if dr == 0 else range(3, -1, -1)
                        for r in ranks:
                            g, B_g = Gr.next()
                            S.dma("sp", mk("dma_start", out=g[:, 0:2048], in_=GATHa[r * 128:(r + 1) * 128, :]), B_g, True, r=[B_cc])
                            S.dma("sp", mk("dma_start", out=g[:, 2048:PAYW], in_=GATHb[r * 128:(r + 1) * 128, :]), B_g, True, r=[B_cc])
                            mcol = dr * 4 + r
                            S.op("dve", mk("tensor_scalar", out=Dm[:, 0:4], in0=g[:, 2048 + dr * 4:2048 + dr * 4 + 4], scalar1=-1.0, scalar2=None,
                                                                             op0=ALU.add), r=[B_g], w=[B_Dm])
                            S.op("dve", mk("tensor_scalar", out=Dm[:, 0:4], in0=Dm[:, 0:4], scalar1=sel[:, mcol:mcol + 1], scalar2=1.0,
                                                                            op0=ALU.mult, op1=ALU.add), r=[B_Dm, B_sel], w=[B_Dm])
                            S.op("dve", mk("tensor_scalar", out=tmpc[:], in0=g[:, dr * 1024:(dr + 1) * 1024], scalar1=sel[:, mcol:mcol + 1],
                                                                                        scalar2=None, op0=ALU.mult), r=[B_g, B_sel], w=[B_tmpc])
                            for h in range(4):
                                S.op("dve", mk("scalar_tensor_tensor", out=c0[:, h * 256:(h + 1) * 256], in0=c0[:, h * 256:(h + 1) * 256],
                                                                                        scalar=Dm[:, h:h + 1], in1=tmpc[:, h * 256:(h + 1) * 256],
                                                                                        op0=ALU.mult, op1=ALU.add), r=[B_c0, B_Dm, B_tmpc], w=[B_c0])
                            if dr == 0:
                                for k, (colsel, uofs) in enumerate(((8 + r, 2056 + 512), (12 + r, 2056))):
                                    u_, B_u = uh[k]
                                    if r == 0:
                                        S.op("dve", mk("tensor_scalar",
                                            out=u_[:], in0=g[:, uofs:uofs + 512], scalar1=sel[:, colsel:colsel + 1], scalar2=None, op0=ALU.mult),
                                            r=[B_g, B_sel], w=[B_u])
                                    else:
                                        S.op("dve", mk("scalar_tensor_tensor",
                                            out=u_[:], in0=g[:, uofs:uofs + 512], scalar=sel[:, colsel:colsel + 1], in1=u_[:], op0=ALU.mult, op1=ALU.add),
                                            r=[B_g, B_sel, B_u], w=[B_u])
                    for k in range(2):
                        S.op("act", mk("activation", out=uhb[k][0][:], in_=uh[k][0][:], func=AF.Identity), r=[uh[k][1]], w=[uhb[k][1]])
                        S.dma("sp", mk("dma_start", out=UALL[0 if k == 0 else 33], in_=uhb[k][0][:]), uhb[k][1], False)
                    for dr in range(2):
                        order = list(range(NB)) if dr == 0 else list(range(NB - 1, -1, -1))
                        cur = 0
                        for n, j in enumerate(order):
                            cc_, B_cc_ = car[dr][cur]
                            S.dma("sp", mk("dma_start", out=SIN[j, dr], in_=cc_[:]), B_cc_, False)
                            if n == NB - 1:
                                break
                            nx, B_nx = car[dr][1 - cur]
                            sg, B_sg = stgr.next()
                            S.dma("sp", mk("dma_start", out=sg[:], in_=SLOC[j, dr]), B_sg, True)
                            for h in range(4):
                                S.op("dve", mk("scalar_tensor_tensor",
                                    out=nx[:, h * 256:(h + 1) * 256], in0=cc_[:, h * 256:(h + 1) * 256], scalar=Dall[:, dr, h, j:j + 1],
                                    in1=sg[:, h * 256:(h + 1) * 256], op0=ALU.mult, op1=ALU.add), r=[B_cc_, B_sg, B_D], w=[B_nx])
                            cur = 1 - cur
                    S.emit()

        def phase_M(l, xsrc, xdst):
            with contextlib.ExitStack() as st:
                P = {}
                alloc_decay_tiles(st, P, True)
                P["keT"] = [sb(st, f"keT{d}", [128, T], BF16) for d in range(2)]
                qeT = [sb(st, f"qeT{d}", [128, T], BF16) for d in range(2)]
                Sbf = [sb(st, f"Sbf{d}", [128, NTL, 256], BF16) for d in range(2)]
                Swk = [[sb(st, f"Swk{d}_{i}", [128, 256], F32) for i in range(2)] for d in range(2)]
                sinb = [sb(st, f"sin{d}", [128, 1024], F32) for d in range(2)]
                srm, B_srm = sb(st, "srm", [128, NTL * D], BF16)
                sr2 = (srm[:].rearrange("p (t d) -> p t d", t=NTL), B_srm)
                sg_t = [sb(st, f"sgt{i}", [128, 512], F32) for i in range(2)]
                og = xn
                ogT, B_ogT = sb(st, "ogT", [128, 8, T], BF16)
                ut = [sb(st, f"ut{i}", [128, 512], BF16) for i in range(NTL + 2)]
                pooledT, B_pT = sb(st, "pooledT", [128, 4, T], BF16)
                mergedT, B_mT = srm[:].rearrange("p (c t) -> p c t", c=8), B_srm
                mt = [sb(st, f"mt{i}", [128, T], F32) for i in range(4)]
                scm = [sb(st, f"scm{i}", [128, 256], BF16) for i in range(4)]
                ssq, B_ssq = sb(st, "ssq", [128, 16], F32)
                masks, B_masks = sb(st, "masks", [128, 2, 128], BF16)
                bands, B_bands = sb(st, "bands", [128, 7, 4, 128], BF16)
                wpool, B_wpool = sb(st, "wpool", [128, 4, 256], BF16)
                phase_setup(P, l)
                S.dma("pool", mk("dma_start", out=masks[:], in_=masks_d), B_masks, True)
                S.dma("pool", mk("dma_start", out=bands[:], in_=bands_d), B_bands, True)
                S.dma("pool", mk("dma_start", out=wpool[:], in_=w_pg[l].rearrange("g c d -> c g d")), B_wpool, True)
                sgr = Ring(sg_t)
                mtr = Ring(mt)
                scr = Ring(scm)
                v, B_v = P["v"]
                hTs = [hT0, hT0]
                setcur(0, hTs)
                load_x(xsrc, 0)
                norm_block(l * 40 + 0)
                wq = WQ()
                for j in range(NB):
                    wq.add(("lr", j), win_grp(l, C_LR, 32), 8, 32)
                    for half in range(2):
                        wq.add(("v", j, half), win_grp(l, C_V + half * 512))
                    for half in range(2):
                        wq.add(("r", j, half), win_grp(l, C_R + half * 512))
                    wq.add(("k", j), win_grp(l, C_K))
                    wq.add(("q", j), win_grp(l, C_Q))
                    for half in range(2):
                        wq.add(("bg", j, half), w_bg[l].rearrange("(kc p) c -> p kc c", p=128)[:, :, half * 512:(half + 1) * 512])
                        wq.add(("ga", j, half), win_grp(l, C_GA + half * 512))
                        wq.add(("gb", j, half), win_grp(l, C_GB + half * 512))
                    for half in range(2):
                        wq.add(("wo", j, half), w_o[l].rearrange("(kc p) c -> p kc c", p=128)[:, :, half * 512:(half + 1) * 512])
                for j in range(NB):
                    setcur(j, hTs)
                    if j + 1 < NB:
                        setcur(j + 1, hTs)
                        load_x(xsrc, j + 1)
                        setcur(j, hTs)
                    for i in range(NTL + 2):
                        S.dma("sp", mk("dma_start", out=ut[i][0][:], in_=UALL[j * NTL + i]), ut[i][1], True)
                    for dr in range(2):
                        S.dma("sp", mk("dma_start", out=sinb[dr][0][:], in_=SIN[j, dr]), sinb[dr][1], True)
                    lr_and_v(P, l, wq, j)
                    v_tiles(P, wq, j)
                    for half in range(2):
                        wt, Bw = wq.get(("r", j, half))
                        for t in range(NTL):
                            pf, pb, Bp = mmring.next()
                            tm_matmul(t, wt, Bw, pf[:, :], Bp)
                            sgt, B_sgt = sgr.next()
                            S.op("act", mk("activation", out=sgt[:], in_=pf[:, :], func=AF.Tanh, scale=0.5), r=[Bp], w=[B_sgt])
                            S.op("dve", mk("scalar_tensor_tensor",
                                out=sr2[0][:, t, half * 512:(half + 1) * 512], in0=sgt[:], scalar=1.0, in1=pf[:, :], op0=ALU.add, op1=ALU.mult),
                                r=[B_sgt, Bp], w=[sr2[1]])
                        wq.done(("r", j, half))
                    for h in range(4):
                        for dr in range(2):
                            decay_chain(P, l, h, dr, True)
                        k_side(P, wq, j, h, True)
                        wt, Bw = wq.get(("q", j))
                        pf, pb, Bp = mmring.next()
                        fm_matmul(wt, Bw, h * 128, pf[:, :], Bp)
                        if h == 3:
                            wq.done(("q", j))
                        for dr in range(2):
                            S.op("dve", mk("scalar_tensor_tensor", out=qeT[dr][0][:], in0=pf[:, :], scalar=float(128 ** -0.5), in1=P["Ep"][dr][0][:],
                                                                                     op0=ALU.mult, op1=ALU.mult), r=[Bp, P["Ep"][dr][1]], w=[qeT[dr][1]])
                        for dr in range(2):
                            di = dr * 4 + h
                            order = list(range(NTL)) if dr == 0 else list(range(NTL - 1, -1, -1))
                            sbf, B_sbf = Sbf[dr]
                            t0 = order[0]
                            S.op("act", mk("activation", out=sbf[:, t0, :], in_=sinb[dr][0][:, h * 256:(h + 1) * 256], func=AF.Identity),
                                 r=[sinb[dr][1]], w=[B_sbf])
                            prev_ap, B_prev = sinb[dr][0][:, h * 256:(h + 1) * 256], sinb[dr][1]
                            for n in range(NTL - 1):
                                t = order[n]
                                tn = order[n + 1]
                                pk, cof, Bg = kv_mm(P, h, dr, t)
                                sw, B_sw = Swk[dr][n % 2]
                                S.op("dve", mk("scalar_tensor_tensor",
                                    out=sw[:], in0=prev_ap, scalar=P["dec"][0][:, di * 4 + t:di * 4 + t + 1], in1=pk[:, cof:cof + 256],
                                    op0=ALU.mult, op1=ALU.add), r=[B_prev, Bg, P["dec"][1]], w=[B_sw])
                                S.op("act", mk("activation", out=sbf[:, tn, :], in_=sw[:], func=AF.Identity), r=[B_sw], w=[B_sbf])
                                prev_ap, B_prev = sw[:], B_sw
                        scall = []
                        for t in range(NTL):
                            pk, _, Bg = mmring.next()
                            for dr in range(2):
                                keT, B_keT = P["keT"][dr]
                                S.op("pe", mk("matmul", out=pk[:, dr * 128:(dr + 1) * 128], lhsT=keT[:, t * 128:(t + 1) * 128],
                                              rhs=qeT[dr][0][:, t * 128:(t + 1) * 128], start=True, stop=True),
                                     r=[B_keT, qeT[dr][1]], w=[Bg])
                            sc, B_sc = scr.next()
                            S.op("dve", mk("tensor_tensor", out=sc[:], in0=pk[:, 0:256], in1=masks[:].rearrange("p a b -> p (a b)"), op=ALU.mult),
                                 r=[Bg, B_masks], w=[B_sc])
                            scall.append((sc, B_sc))
                        for t in range(NTL):
                            scs = [(scall[t][0][:, 0:128], scall[t][1]), (scall[t][0][:, 128:256], scall[t][1])]
                            po, cofo, Bo = gring.next()
                            vh = v[:, t, h * 256:(h + 1) * 256]
                            S.op("pe", mk("matmul", out=po[:, cofo:cofo + 256], lhsT=scs[0][0], rhs=vh, start=True, stop=False),
                                 r=[scs[0][1], B_v], w=[Bo])
                            S.op("pe", mk("matmul", out=po[:, cofo:cofo + 256], lhsT=qeT[0][0][:, t * 128:(t + 1) * 128], rhs=Sbf[0][0][:, t, :],
                                                                              start=False, stop=False), r=[qeT[0][1], Sbf[0][1]], w=[Bo])
                            S.op("pe", mk("matmul", out=po[:, cofo:cofo + 256], lhsT=scs[1][0], rhs=vh, start=False, stop=False),
                                 r=[scs[1][1], B_v], w=[Bo])
                            S.op("pe", mk("matmul", out=po[:, cofo:cofo + 256], lhsT=qeT[1][0][:, t * 128:(t + 1) * 128], rhs=Sbf[1][0][:, t, :],
                                                                              start=False, stop=True), r=[qeT[1][1], Sbf[1][1]], w=[Bo])
                            col = t * 4 + h
                            S.op("act", mk("activation", out=junk[:, 0:256], in_=po[:, cofo:cofo + 256], func=AF.Square,
                                                                                         accum_out=ssq[:, col:col + 1]), r=[Bo], w=[B_ssq])
                            rsqrt_cols(ssq, B_ssq, col, col + 1, 1.0 / 256)
                            S.op("dve", mk("scalar_tensor_tensor",
                                out=og[t][0][:, h * 256:(h + 1) * 256], in0=po[:, cofo:cofo + 256], scalar=ssq[:, col:col + 1],
                                in1=sr2[0][:, t, h * 256:(h + 1) * 256], op0=ALU.mult, op1=ALU.mult), r=[Bo, B_ssq, sr2[1]], w=[og[t][1]])
                    for t in range(NTL):
                        gt = j * NTL + t
                        sp_ = 3 if gt == 0 else 0
                        sm_ = 4 if gt == 0 else (5 if gt == NB * NTL - 1 else 1)
                        sn_ = 6 if gt == NB * NTL - 1 else 2
                        pf, pb, Bp = mmring.next()
                        for g in range(4):
                            for k, (ui, bs) in enumerate(((t, sp_), (t + 1, sm_), (t + 2, sn_))):
                                S.op("pe", mk("matmul", out=pf[:, g * 128:(g + 1) * 128], lhsT=ut[ui][0][:, g * 128:(g + 1) * 128],
                                                                                             rhs=bands[:, bs, g, :], start=(k == 0), stop=(k == 2)),
                                     r=[ut[ui][1], B_bands], w=[Bp])
                        S.op("act", mk("activation", out=pooledT[:, :, t * 128:(t + 1) * 128], in_=pf[:, :].rearrange("p (g t) -> p g t", g=4),
                                                                      func=AF.Identity), r=[Bp], w=[B_pT])
                    for pc in range(4):
                        pf, pb, Bp = mmring.next()
                        for t in range(NTL):
                            for c2 in range(2):
                                ch = 2 * pc + c2
                                S.op("pe", mk("transpose",
                                    out=pb[:, c2 * 512 + t * 128:c2 * 512 + (t + 1) * 128], in_=og[t][0][:, ch * 128:(ch + 1) * 128], identity=ident[:]),
                                    r=[og[t][1], B_ident], w=[Bp])
                        for c2 in range(2):
                            ch = 2 * pc + c2
                            S.op("act", mk("activation", out=ogT[:, ch, :], in_=pb[:, c2 * 512:(c2 + 1) * 512], func=AF.Identity,
                                                                                  scale=gnh[:, l * 8 + ch:l * 8 + ch + 1]), r=[Bp, B_gnh], w=[B_ogT])
                    for i in range(8):
                        half = i // 4
                        wbg, Bwbg = wq.get(("bg", j, half))
                        wga, Bwga = wq.get(("ga", j, half))
                        wgb, Bwgb = wq.get(("gb", j, half))
                        cofs = (i % 4) * 128
                        pya, _, Bya = fring.next()
                        for kc in range(8):
                            S.op("pe", mk("matmul", out=pya[:, :], lhsT=wbg[:, kc, cofs:cofs + 128], rhs=ogT[:, kc, :],
                                                                                             start=(kc == 0), stop=(kc == 7)), r=[Bwbg, B_ogT], w=[Bya])
                        pyb, _, Byb = fring.next()
                        g = i // 2
                        S.op("pe", mk("matmul", out=pyb[:, :], lhsT=wpool[:, g, (i % 2) * 128:(i % 2 + 1) * 128], rhs=pooledT[:, g, :],
                                                                        start=True, stop=True), r=[B_wpool, B_pT], w=[Byb])
                        pga, _, Bga = fring.next()
                        fm_matmul(wga, Bwga, cofs, pga[:, :], Bga)
                        pgb, _, Bgb = fring.next()
                        fm_matmul(wgb, Bwgb, cofs, pgb[:, :], Bgb)
                        sa, B_sa = mtr.next()
                        sb_, B_sb = mtr.next()
                        S.op("act", mk("activation", out=sa[:], in_=pga[:, :], func=AF.Tanh, scale=0.5), r=[Bga], w=[B_sa])
                        S.op("act", mk("activation", out=sb_[:], in_=pgb[:, :], func=AF.Tanh, scale=0.5), r=[Bgb], w=[B_sb])
                        S.op("dve", mk("scalar_tensor_tensor", out=sa[:], in0=sa[:], scalar=1.0, in1=pya[:, :], op0=ALU.add, op1=ALU.mult),
                             r=[B_sa, Bya], w=[B_sa])
                        S.op("dve", mk("tensor_scalar", out=sb_[:], in0=sb_[:], scalar1=1.0, scalar2=psh[:, l * 8 + i:l * 8 + i + 1], op0=ALU.add, op1=ALU.mult),
                             r=[B_sb, B_psh], w=[B_sb])
                        S.op("dve", mk("tensor_tensor", out=sb_[:], in0=sb_[:], in1=pyb[:, :], op=ALU.mult), r=[B_sb, Byb], w=[B_sb])
                        S.op("dve", mk("scalar_tensor_tensor", out=mergedT[:, i, :], in0=sa[:], scalar=0.5, in1=sb_[:], op0=ALU.mult, op1=ALU.add),
                             r=[B_sa, B_sb], w=[B_mT])
                        if i % 4 == 3:
                            wq.done(("bg", j, half))
                            wq.done(("ga", j, half))
                            wq.done(("gb", j, half))
                    if j + 1 < NB:
                        setcur(j + 1, hTs)
                        norm_block(l * 40 + 0)
                        setcur(j, hTs)
                    for half in range(2):
                        wo_, Bwo = wq.get(("wo", j, half))
                        for t in range(NTL):
                            pf, pb, Bp = fring.next()
                            for kc in range(8):
                                S.op("pe", mk("matmul", out=pf[:, :], lhsT=mergedT[:, kc, t * 128:(t + 1) * 128], rhs=wo_[:, kc, :],
                                                                                        start=(kc == 0), stop=(kc == 7)), r=[B_mT, Bwo], w=[Bp])
                            S.op("dve", mk("tensor_tensor", out=CUR.xt[t][0][:, half * 512:(half + 1) * 512], in0=pf[:, :],
                                                                                        in1=CUR.xt[t][0][:, half * 512:(half + 1) * 512], op=ALU.add),
                                 r=[Bp, CUR.xt[t][1]], w=[CUR.xt[t][1]])
                        wq.done(("wo", j, half))
                    for t in range(NTL):
                        r0 = j * T + t * 128
                        S.dma("sp", mk("dma_start", out=xdst[r0:r0 + 128, :], in_=CUR.xt[t][0][:]), CUR.xt[t][1], False)
                S.emit()

        def phase_F(l, xsrc, xdst, last):
            with contextlib.ExitStack() as st:
                actT, B_aT = sb(st, "actT", [128, NJ, T], BF16)
                ft = [sb(st, f"ft{i}", [128, T], F32) for i in range(4)]
                ftr = Ring(ft)
                if last:
                    gfin, B_gfin = sb(st, "gfin", [128, D], F32)
                    S.dma("sp", mk("dma_start", out=gfin[:], in_=gfin_d), B_gfin, True)
                    fst, B_fst = sb(st, "fst", [128, 8], F32)
                hTs = [hT0, sb(st, "hT1", [128, 8, T], BF16)]
                WS.slots = list(wsl) + [sb(st, f"wx{i}", [128, 8, 512], BF16) for i in range(4)]
                setcur(0, hTs)
                load_x(xsrc, 0)
                norm_block(l * 40 + 8)
                wq = WQ()
                for j in range(NB):
                    for jj in range(NJ // 2):
                        wq.add(("fg", j, jj), w_fi[l].rearrange("(kc p) c -> p kc c", p=128)[:, :, jj * 256:(jj + 1) * 256], 8, 256)
                        wq.add(("fu", j, jj), w_fi[l].rearrange("(kc p) c -> p kc c", p=128)[:, :, DFF + jj * 256:DFF + (jj + 1) * 256], 8, 256)
                    for half in range(2):
                        for jg in range(3):
                            nk = 8 if jg < 2 else NJ - 16
                            wq.add(("fo", j, half, jg), w_fo[l].rearrange("(jc p) c -> p jc c", p=128)[:, jg * 8:jg * 8 + nk, half * 512:(half + 1) * 512], nk, 512)
                for j in range(NB):
                    setcur(j, hTs)
                    if j + 1 < NB:
                        setcur(j + 1, hTs)
                        load_x(xsrc, j + 1)
                        setcur(j, hTs)
                    for jj in range(NJ // 2):
                        if jj == 5 and j + 1 < NB:
                            setcur(j + 1, hTs)
                            norm_block(l * 40 + 8)
                            setcur(j, hTs)
                        wg, Bwg = wq.get(("fg", j, jj))
                        wu, Bwu = wq.get(("fu", j, jj))
                        for c2 in range(2):
                            jc = jj * 2 + c2
                            pg, _, Bpg = fring.next()
                            fm_matmul(wg, Bwg, c2 * 128, pg[:, :], Bpg)
                            pu, _, Bpu = fring.next()
                            fm_matmul(wu, Bwu, c2 * 128, pu[:, :], Bpu)
                            f1, B_f1 = ftr.next()
                            S.op("act", mk("activation", out=f1[:], in_=pg[:, :], func=AF.Tanh, scale=0.5), r=[Bpg], w=[B_f1])
                            S.op("dve", mk("scalar_tensor_tensor", out=f1[:], in0=f1[:], scalar=1.0, in1=pg[:, :], op0=ALU.add, op1=ALU.mult),
                                 r=[B_f1, Bpg], w=[B_f1])
                            S.op("dve", mk("scalar_tensor_tensor", out=actT[:, jc, :], in0=f1[:], scalar=0.5, in1=pu[:, :], op0=ALU.mult, op1=ALU.mult),
                                 r=[B_f1, Bpu], w=[B_aT])
                        wq.done(("fg", j, jj))
                        wq.done(("fu", j, jj))
                    for half in range(2):
                        pts = [fring.next() for _ in range(NTL)]
                        for jg in range(3):
                            nk = 8 if jg < 2 else NJ - 16
                            wf, Bwf = wq.get(("fo", j, half, jg))
                            for t in range(NTL):
                                pf, _, Bp = pts[t]
                                for k in range(nk):
                                    jc = jg * 8 + k
                                    S.op("pe", mk("matmul", out=pf[:, :], lhsT=actT[:, jc, t * 128:(t + 1) * 128], rhs=wf[:, k, :],
                                                                                               start=(jc == 0), stop=(jc == NJ - 1)), r=[B_aT, Bwf], w=[Bp])
                            wq.done(("fo", j, half, jg))
                        for t in range(NTL):
                            pf, _, Bp = pts[t]
                            S.op("dve", mk("tensor_tensor", out=CUR.xt[t][0][:, half * 512:(half + 1) * 512], in0=pf[:, :],
                                                                                        in1=CUR.xt[t][0][:, half * 512:(half + 1) * 512], op=ALU.add),
                                 r=[Bp, CUR.xt[t][1]], w=[CUR.xt[t][1]])
                    if last:
                        for t in range(NTL):
                            S.op("act", mk("activation", out=junk[:], in_=CUR.xt[t][0][:], func=AF.Square, accum_out=fst[:, t:t + 1]), r=[CUR.xt[t][1]], w=[B_fst])
                        rsqrt_cols(fst, B_fst, 0, NTL, 1.0 / D)
                        for t in range(NTL):
                            S.op("dve", mk("scalar_tensor_tensor", out=CUR.xt[t][0][:], in0=CUR.xt[t][0][:], scalar=fst[:, t:t + 1], in1=gfin[:], op0=ALU.mult, op1=ALU.mult),
                                 r=[CUR.xt[t][1], B_fst, B_gfin], w=[CUR.xt[t][1]])
                    for t in range(NTL):
                        r0 = j * T + t * 128
                        S.dma("sp", mk("dma_start", out=xdst[r0:r0 + 128, :], in_=CUR.xt[t][0][:]), CUR.xt[t][1], False)
                S.emit(final=last)
                assert not wlive
                WS.slots = list(wsl)

        xcur = x_in
        for l in range(n_layers if not OP_LIMIT else 0):
            last = (l == n_layers - 1)
            with contextlib.ExitStack() as pst:
                pay, B_pay = sb(pst, "pay", [128, PAYW], F32)
                B_pay.temp = False
                phase_A(l, xcur, pay, B_pay)
                if stop_after == "A":
                    break
                phase_C(l, pay, B_pay)
            if stop_after == "C":
                break
            phase_M(l, xcur, XA)
            if stop_after == "M":
                break
            phase_F(l, XA, out_d if last else XB, last)
            xcur = XB
        if OP_LIMIT:
            try:
                with contextlib.ExitStack() as pst:
                    pay, B_pay = sb(pst, "pay", [128, PAYW], F32)
                    phase_A(0, x_in, pay, B_pay)
                    phase_C(0, pay, B_pay)
                phase_M(0, x_in, XA)
                phase_F(0, XA, out_d, True)
            except StopRecording:
                pass
            S.unlimited = True
            S.op("dve", mk("memset", stat[:], 0.0), w=[B_stat])
            S.emit(final=True)
        elif stop_after is not None:
            S.op("dve", mk("memset", stat[:], 0.0), w=[B_stat])
            S.emit(final=True)
        print("total ops", S.total_ops, "dma sems", len(S.dma_bufs_used))
    return nc


def _band_sets(s):
    L = 4 * TOK
    wins = (2, 4, 8, 16)
    out = np.zeros((7, 4, 128, 128), np.float32)

    def band(tile_start, which, g):
        w = wins[g]
        B = np.zeros((128, 128), np.float32)
        src0 = tile_start + (-128 if which == 0 else (128 if which == 2 else 0))
        for t in range(128):
            P = tile_start + t
            if P < 0 or P >= L:
                continue
            lo = max(P - w // 2, 0)
            hi = min(P + w - w // 2, L)
            cnt = hi - lo
            for sp in range(lo, hi):
                si = sp - src0
                if 0 <= si < 128:
                    B[si, t] += 1.0 / cnt
            si = P - src0
            if 0 <= si < 128:
                B[si, t] -= 1.0
        return B
    gen = 5 * 128 + 4096 * 1
    first = s * TOK
    lastt = s * TOK + TOK - 128
    for g in range(4):
        out[0, g] = band(gen, 0, g)
        out[1, g] = band(gen, 1, g)
        out[2, g] = band(gen, 2, g)
        out[3, g] = band(first, 0, g)
        out[4, g] = band(first, 1, g)
        out[5, g] = band(lastt, 1, g)
        out[6, g] = band(lastt, 2, g)
    return np.ascontiguousarray(out.transpose(2, 0, 1, 3))


def _host_inputs(inputs):
    f = lambda a: np.ascontiguousarray(np.asarray(a, dtype=np.float32))
    x = f(inputs["x"]).reshape(NCORES, TOK, D)
    vecs = np.zeros((128, 80), np.float32)
    for l in range(2):
        b = l * 40
        vecs[:, b + 0:b + 8] = f(inputs["norm_mix"])[l].reshape(8, 128).T
        vecs[:, b + 8:b + 16] = f(inputs["norm_ffn"])[l].reshape(8, 128).T
        vecs[:, b + 16:b + 24] = f(inputs["gla_norm"])[l].reshape(8, 128).T
        vecs[:, b + 24:b + 32] = f(inputs["pool_scale"])[l].reshape(8, 128).T
        vecs[:, b + 32:b + 36] = f(inputs["b_decay_fwd"])[l].reshape(4, 128).T
        vecs[:, b + 36:b + 40] = f(inputs["b_decay_bwd"])[l].reshape(4, 128).T
    gfin = np.ascontiguousarray(np.broadcast_to(f(inputs["norm_final"])[None, :], (128, D)))
    ident = np.eye(128, dtype=np.float32)
    si = np.arange(128)[:, None]
    ci = np.arange(128)[None, :]
    masks = np.stack([(si <= ci).astype(np.float32), (si > ci).astype(np.float32)], axis=1)
    shared = {k: f(inputs[k]) for k in ("w_in", "w_decay_up_fwd", "w_decay_up_bwd", "w_branch_gla", "w_pool_group", "w_out", "w_ffn_in", "w_ffn_out")}
    maps = []
    for c in range(NCORES):
        s = c % 4
        sel = np.zeros((128, 16), np.float32)
        for r in range(4):
            sel[:, r] = 1.0 if r < s else 0.0
            sel[:, 4 + r] = 1.0 if r > s else 0.0
            sel[:, 8 + r] = 1.0 if r == s - 1 else 0.0
            sel[:, 12 + r] = 1.0 if r == s + 1 else 0.0
        m = dict(shared)
        m.update({"x": x[c], "vecs": vecs, "gfin": gfin, "ident": ident, "masks": np.ascontiguousarray(masks),
                  "bands": _band_sets(s), "sel": sel})
        maps.append(m)
    return maps


def kernel(**inputs):
    maps = _host_inputs(inputs)
    nc = build()
    res = run_bass_kernel_spmd(nc, maps, core_ids=list(range(NCORES)))
    out = np.stack([np.asarray(r["out"], dtype=np.float32) for r in res.results], axis=0)
    return out.reshape(2, 4 * TOK, D)
```

```python
import contextlib
import numpy as np
import concourse.bass as bass
import concourse.mybir as mybir
from concourse.bass_utils import run_bass_kernel_spmd

F32 = mybir.dt.float32
BF16 = mybir.dt.bfloat16
AF = mybir.ActivationFunctionType
ALU = mybir.AluOpType
AX = mybir.AxisListType

ENGS = ("pe", "act", "dve", "pool", "sp")
SAME_ENGINE_SYNC = True
SAME_ENGINE_WAR = True

NCORES = 8
TOK = 4096
D = 1024
T = 512
NB = TOK // T
NTL = T // 128
DFF = 2816
NJ = DFF // 128
INW = 5664
C_Q, C_K, C_V, C_R, C_LR, C_U, C_GA, C_GB = 0, 512, 1024, 2048, 3072, 3104, 3616, 4640
PAYW = 2048 + 8 + 1024
EPS = 1e-6
WLOOK = 1
NWSLOT = 4
MMB = 4
NDSEM = 36


OP_LIMIT = 0
MILESTONES = False
DBG_VARIANT = 0
SIM_CC = False


class StopRecording(Exception):
    pass


class Buf:
    __slots__ = ("name", "last_w", "readers", "dsem", "ndma", "psum", "temp", "dkind")

    def __init__(self, name):
        self.name = name
        self.last_w = None
        self.readers = []
        self.dsem = None
        self.ndma = 0
        self.psum = False
        self.temp = False
        self.dkind = None


class Op:
    __slots__ = ("eng", "fn", "idx", "waits", "dwaits", "signal", "sigval", "dma_buf", "dma_val")

    def __init__(self, eng, fn, idx):
        self.eng = eng
        self.fn = fn
        self.idx = idx
        self.waits = []
        self.dwaits = {}
        self.signal = False
        self.sigval = 0
        self.dma_buf = None
        self.dma_val = 0


class Sched:
    def __init__(self, nc, sems, dsem_pool):
        self.nc = nc
        self.sems = sems
        self.dsem_pool = dsem_pool
        self.sigcount = {e: 0 for e in ENGS}
        self.ops = {e: [] for e in ENGS}
        self.seen = {e: {f: -1 for f in ENGS} for e in ENGS}
        self.dseen = {e: {} for e in ENGS}
        self.bufs = []
        self.nops = {e: 0 for e in ENGS}
        self.pending_barrier = None
        self.dma_bufs_used = []
        self.total_ops = 0
        self._dma_base = {}
        self.unlimited = False

    def buf(self, name):
        b = Buf(name)
        self.bufs.append(b)
        return b

    def _add_dep(self, op, p):
        if p is None or p is op:
            return
        if p.dma_buf is not None:
            b = p.dma_buf
            if self.dseen[op.eng].get(b, 0) >= p.dma_val:
                return
            if op.dwaits.get(b, 0) < p.dma_val:
                op.dwaits[b] = p.dma_val
            return
        if p.eng == op.eng:
            if op.eng in ("pe", "sp") or not SAME_ENGINE_SYNC:
                return
        if self.seen[op.eng][p.eng] >= p.idx:
            return
        op.waits.append(p)

    def op(self, eng, fn, r=(), w=()):
        if OP_LIMIT and self.total_ops >= OP_LIMIT and not self.unlimited:
            return Op(eng, fn, -1)
        o = Op(eng, fn, self.nops[eng])
        self.nops[eng] += 1
        self.total_ops += 1
        for b in r:
            self._add_dep(o, b.last_w)
            if b.psum:
                for rd in b.readers:
                    if rd.eng != eng:
                        self._add_dep(o, rd)
        for b in w:
            self._add_dep(o, b.last_w)
            for rd in b.readers:
                if rd.eng == eng and rd.dma_buf is None and (eng in ("pe", "sp") or not SAME_ENGINE_SYNC or not SAME_ENGINE_WAR):
                    continue
                self._add_dep(o, rd)
        best = {}
        for p in o.waits:
            if p.eng not in best or best[p.eng].idx < p.idx:
                best[p.eng] = p
        o.waits = list(best.values())
        for p in o.waits:
            p.signal = True
            self.seen[eng][p.eng] = p.idx
        for b, v in o.dwaits.items():
            self.dseen[eng][b] = v
        for b in r:
            b.readers.append(o)
        for b in w:
            b.last_w = o
            b.readers = []
        self.ops[eng].append(o)
        return o

    def dma(self, eng, fn, buf, load, r=(), w=()):
        if buf.dsem is None:
            buf.dkind = "sw" if eng == "pool" else "hw"
            buf.dsem, buf.ndma = self.dsem_pool[buf.dkind].pop()
            self.dma_bufs_used.append(buf)
            self._dma_base[id(buf)] = 16 * buf.ndma
        assert buf.dkind == ("sw" if eng == "pool" else "hw"), buf.name
        rr = list(r) + ([] if load else [buf])
        ww = list(w) + ([buf] if load else [])
        o = self.op(eng, fn, rr, ww)
        if o.idx < 0:
            return o
        buf.ndma += 1
        o.dma_buf = buf
        o.dma_val = 16 * buf.ndma
        return o

    def _simulate(self):
        ptr = {e: 0 for e in ENGS}
        done = set()
        dmac = {}
        progress = True
        while progress:
            progress = False
            for e in ENGS:
                while ptr[e] < len(self.ops[e]):
                    o = self.ops[e][ptr[e]]
                    ok = all(id(p) in done for p in o.waits) and all(dmac.get(id(b), 0) >= v for b, v in o.dwaits.items())
                    if not ok:
                        break
                    done.add(id(o))
                    if o.dma_buf is not None:
                        dmac[id(o.dma_buf)] = dmac.get(id(o.dma_buf), self._dma_base.get(id(o.dma_buf), 0)) + 16
                    ptr[e] += 1
                    progress = True
        stuck = {e: (ptr[e], len(self.ops[e])) for e in ENGS if ptr[e] < len(self.ops[e])}
        if stuck:
            for e in stuck:
                o = self.ops[e][ptr[e]]
                print("STUCK", e, ptr[e], [(p.eng, p.idx, id(p) in done) for p in o.waits], [(b.name, v, dmac.get(id(b), 0)) for b, v in o.dwaits.items()])
            raise RuntimeError(f"static deadlock: {stuck}")
        for b in self.dma_bufs_used:
            self._dma_base[id(b)] = 16 * b.ndma

    def emit(self, final=False):
        nc = self.nc
        for e in ENGS:
            comp = [o for o in self.ops[e] if o.dma_buf is None]
            if comp:
                comp[-1].signal = True
        for e in ENGS:
            c = self.sigcount[e]
            for o in self.ops[e]:
                if o.signal and o.dma_buf is None:
                    c += 1
                    o.sigval = c
            self.sigcount[e] = c
        self._simulate()
        prev_barrier = self.pending_barrier
        end_vals = {e: self.sigcount[e] for e in ENGS}
        dma_end = [(b.dsem, 16 * b.ndma) for b in self.dma_bufs_used]
        ops = self.ops
        sems = self.sems

        def body(ename):
            def run(eng):
                if prev_barrier is not None:
                    ev, dv = prev_barrier
                    for f in ENGS:
                        if f != ename and ev[f] > 0:
                            eng.wait_ge(sems[f], ev[f])
                    for (ds, v) in dv:
                        if v > 0:
                            eng.wait_ge(ds, v)
                for o in ops[ename]:
                    for p in o.waits:
                        eng.wait_ge(sems[p.eng], p.sigval)
                    for b, v in o.dwaits.items():
                        eng.wait_ge(b.dsem, v)
                    inst = o.fn(eng)
                    if o.dma_buf is not None:
                        inst.then_inc(o.dma_buf.dsem, 16)
                    elif o.signal:
                        inst.then_inc(sems[ename], 1)
                if final:
                    for f in ENGS:
                        if f != ename and end_vals[f] > 0:
                            eng.wait_ge(sems[f], end_vals[f])
                    for (ds, v) in dma_end:
                        if v > 0:
                            eng.wait_ge(ds, v)
            return run

        with nc.Block() as block:
            block.tensor(body("pe"))
            block.scalar(body("act"))
            block.vector(body("dve"))
            block.gpsimd(body("pool"))
            block.sync(body("sp"))
        self.pending_barrier = (end_vals, dma_end)
        keep = []
        for b in self.dma_bufs_used:
            if b.temp:
                self.dsem_pool[b.dkind].append((b.dsem, b.ndma))
                b.dsem = None
            else:
                keep.append(b)
        self.dma_bufs_used = keep
        self.ops = {e: [] for e in ENGS}
        self.seen = {e: {f: -1 for f in ENGS} for e in ENGS}
        self.dseen = {e: {} for e in ENGS}
        self.nops = {e: 0 for e in ENGS}
        for b in self.bufs:
            b.last_w = None
            b.readers = []


def mk(name, *args, **kw):
    def fn(e):
        return getattr(e, name)(*args, **kw)
    return fn


class Ring:
    def __init__(self, items):
        self.items = items
        self.i = 0

    def next(self):
        it = self.items[self.i % len(self.items)]
        self.i += 1
        return it


def build(n_layers=2, dbg=False, stop_after=None):
    nc = bass.Bass("TRN2", target_bir_lowering=False)
    dk = "ExternalOutput" if dbg else "Internal"

    def din(name, shape, dt=F32):
        return nc.dram_tensor(name, shape, dt, kind="ExternalInput").ap()

    x_in = din("x", [TOK, D])
    w_in = din("w_in", [2, D, INW])
    w_upf = din("w_decay_up_fwd", [2, 16, 512])
    w_upb = din("w_decay_up_bwd", [2, 16, 512])
    w_bg = din("w_branch_gla", [2, D, D])
    w_pg = din("w_pool_group", [2, 4, 128, 256])
    w_o = din("w_out", [2, D, D])
    w_fi = din("w_ffn_in", [2, D, 2 * DFF])
    w_fo = din("w_ffn_out", [2, DFF, D])
    vecs_d = din("vecs", [128, 80])
    gfin_d = din("gfin", [128, D])
    ident_d = din("ident", [128, 128])
    masks_d = din("masks", [128, 2, 128])
    bands_d = din("bands", [128, 7, 4, 128])
    sel_d = din("sel", [128, 16])
    out_d = nc.dram_tensor("out", [TOK, D], F32, kind="ExternalOutput").ap()
    XA = nc.dram_tensor("XA", [TOK, D], F32, kind=dk).ap()
    XB = nc.dram_tensor("XB", [TOK, D], F32, kind=dk).ap()
    UALL = nc.dram_tensor("UALL", [34, 128, 512], BF16, kind="Internal").ap()
    SLOC = nc.dram_tensor("SLOC", [NB, 2, 128, 1024], F32, kind="Internal").ap()
    SIN = nc.dram_tensor("SIN", [NB, 2, 128, 1024], F32, kind=dk).ap()
    SEGa = nc.dram_tensor("SEGa", [128, 2048], F32, kind="Internal").ap()
    GATHa = nc.dram_tensor("GATHa", [4 * 128, 2048], F32, kind="Internal").ap()
    SEGb = nc.dram_tensor("SEGb", [128, PAYW - 2048], F32, kind="Internal").ap()
    GATHb = nc.dram_tensor("GATHb", [4 * 128, PAYW - 2048], F32, kind="Internal").ap()

    with contextlib.ExitStack() as top:
        E = top.enter_context
        sems = {e: E(nc.semaphore("s_" + e)) for e in ENGS}
        dpool = {"hw": [(E(nc.semaphore(f"dh{i}")), 0) for i in range(NDSEM)], "sw": [(E(nc.semaphore(f"ds{i}")), 0) for i in range(16)]}
        ccsem = E(nc.semaphore("ccsem"))
        S = Sched(nc, sems, dpool)
        cc_count = [0]

        uid = [0]

        def sb(st, name, shape, dt):
            uid[0] += 1
            t = st.enter_context(nc.sbuf_tensor(f"sb{uid[0]}_{name}", shape, dt))
            b = S.buf(name)
            b.temp = st is not top
            return t, b

        ident, B_ident = sb(top, "ident", [128, 128], BF16)
        vecs, B_vecs = sb(top, "vecs", [128, 80], F32)
        negb, B_negb = sb(top, "negb", [128, 16], F32)
        gnh, B_gnh = sb(top, "gnh", [128, 16], F32)
        psh, B_psh = sb(top, "psh", [128, 16], F32)
        sel, B_sel = sb(top, "sel", [128, 16], F32)
        LD, B_LD = sb(top, "LD", [128, 2, 4, NB], F32)
        wsl = [sb(top, f"wslot{i}", [128, 8, 512], BF16) for i in range(NWSLOT)]
        xt = [sb(top, f"xt{i}", [128, D], F32) for i in range(NTL)]
        xn = [sb(top, f"xn{i}", [128, D], BF16) for i in range(NTL)]
        hT0 = sb(top, "hT", [128, 8, T], BF16)
        xtB = [sb(top, f"xtB{i}", [128, D], F32) for i in range(NTL)]
        xts = [xt, xtB]

        class CUR:
            pass
        CUR.xt = xt
        CUR.hT, CUR.B_hT = hT0

        def setcur(j, hTs):
            CUR.xt = xts[j % 2]
            CUR.hT, CUR.B_hT = hTs[j % 2]
        junk, _ = sb(top, "junk", [128, D], BF16)
        stat, B_stat = sb(top, "stat", [128, 8], F32)
        pbanks = []
        for i in range(8):
            p = E(nc.psum_tensor(f"ps{i}", [128, 512], F32))
            pbanks.append(p)
        bankB = [S.buf(f"psb{i}") for i in range(8)]
        for b_ in bankB:
            b_.psum = True
        mmring = Ring([(pbanks[i], pbanks[i].bitcast(BF16), bankB[i]) for i in range(MMB)])
        gring = Ring([(pbanks[i], 0, bankB[i]) for i in range(MMB, 8)])
        fring = Ring([(pbanks[i], pbanks[i].bitcast(BF16), bankB[i]) for i in range(8)])
        wring = Ring(wsl)

        wlive = {}

        class WS:
            slots = list(wsl)

        class WQ:
            def __init__(self):
                self.plan = []
                self.issued = 0
                self.slots = {}

            def add(self, key, src, nk=8, ncols=512):
                self.plan.append((key, src, nk, ncols))

            def _issue(self, i):
                key, src, nk, ncols = self.plan[i]
                free = [k for k in range(len(WS.slots)) if k not in wlive]
                assert free, ("no free weight slot for", key, dict(wlive))
                k = free[0]
                wlive[k] = key
                t, b = WS.slots[k]
                S.dma("pool", mk("dma_start", out=t[:, 0:nk, 0:ncols], in_=src), b, True)
                self.slots[key] = (t, b, k)

            def get(self, key):
                idx = [i for i, p in enumerate(self.plan) if p[0] == key][0]
                while self.issued <= idx:
                    self._issue(self.issued)
                    self.issued += 1
                self.prefetch()
                t, b, k = self.slots[key]
                assert wlive.get(k) == key, (key, wlive)
                return t, b

            def prefetch(self):
                while self.issued < len(self.plan) and len(wlive) < len(WS.slots):
                    self._issue(self.issued)
                    self.issued += 1

            def done(self, key):
                t, b, k = self.slots[key]
                assert wlive.get(k) == key
                del wlive[k]
                self.prefetch()

        def win_grp(l, c0, n=512):
            return w_in[l].rearrange("(kc p) c -> p kc c", p=128)[:, :, c0:c0 + n]

        def load_consts():
            S.dma("pool", mk("dma_start", out=ident[:], in_=ident_d), B_ident, True)
            S.dma("sp", mk("dma_start", out=vecs[:], in_=vecs_d), B_vecs, True)
            S.dma("sp", mk("dma_start", out=sel[:], in_=sel_d), B_sel, True)
            S.op("pool", mk("memset", mhw[:], -0.5), w=[B_mhw])
            for l in range(2):
                S.op("dve", mk("tensor_scalar", out=negb[:, l * 8:(l + 1) * 8], in0=vecs[:, l * 40 + 32:l * 40 + 40],
                                                           scalar1=-1.0, scalar2=None, op0=ALU.mult), r=[B_vecs], w=[B_negb])
                S.op("dve", mk("tensor_scalar", out=gnh[:, l * 8:(l + 1) * 8], in0=vecs[:, l * 40 + 16:l * 40 + 24],
                                                           scalar1=0.5, scalar2=None, op0=ALU.mult), r=[B_vecs], w=[B_gnh])
                S.op("dve", mk("tensor_scalar", out=psh[:, l * 8:(l + 1) * 8], in0=vecs[:, l * 40 + 24:l * 40 + 32],
                                                           scalar1=0.5, scalar2=None, op0=ALU.mult), r=[B_vecs], w=[B_psh])

        def load_x(src, j):
            for t in range(NTL):
                r0 = j * T + t * 128
                S.dma("sp", mk("dma_start", out=CUR.xt[t][0][:], in_=src[r0:r0 + 128, :]), CUR.xt[t][1], True)

        mhw, B_mhw = sb(top, "mhw", [128, 16], F32)

        def rsqrt_cols(tile, B, c0, c1, scale):
            S.op("dve", mk("tensor_scalar", out=tile[:, c0:c1], in0=tile[:, c0:c1], scalar1=scale, scalar2=EPS,
                                                  op0=ALU.mult, op1=ALU.add), r=[B], w=[B])
            S.op("pool", mk("tensor_tensor", out=tile[:, c0:c1], in0=tile[:, c0:c1], in1=mhw[:, 0:c1 - c0], op=ALU.pow),
                 r=[B, B_mhw], w=[B])

        def norm_block(gcol):
            for t in range(NTL):
                S.op("act", mk("activation", out=junk[:], in_=CUR.xt[t][0][:], func=AF.Square, accum_out=stat[:, t:t + 1]),
                     r=[CUR.xt[t][1]], w=[B_stat])
            rsqrt_cols(stat, B_stat, 0, NTL, 1.0 / D)
            for t in range(NTL):
                S.op("dve", mk("tensor_scalar", out=xn[t][0][:], in0=CUR.xt[t][0][:], scalar1=stat[:, t:t + 1], scalar2=None,
                                                           op0=ALU.mult), r=[CUR.xt[t][1], B_stat], w=[xn[t][1]])
            for pc in range(4):
                pf, pb, Bp = mmring.next()
                for t in range(NTL):
                    for c2 in range(2):
                        ch = 2 * pc + c2
                        S.op("pe", mk("transpose",
                            out=pb[:, c2 * 512 + t * 128:c2 * 512 + (t + 1) * 128], in_=xn[t][0][:, ch * 128:(ch + 1) * 128], identity=ident[:]),
                            r=[xn[t][1], B_ident], w=[Bp])
                for c2 in range(2):
                    ch = 2 * pc + c2
                    S.op("act", mk("activation", out=CUR.hT[:, ch, :], in_=pb[:, c2 * 512:(c2 + 1) * 512], func=AF.Identity,
                                                                          scale=vecs[:, gcol + ch:gcol + ch + 1]), r=[Bp, B_vecs], w=[CUR.B_hT])

        def fm_matmul(wt, Bw, cofs, dst_ap, Bd):
            for kc in range(8):
                S.op("pe", mk("matmul", out=dst_ap, lhsT=wt[:, kc, cofs:cofs + 128], rhs=CUR.hT[:, kc, :], start=(kc == 0), stop=(kc == 7)),
                     r=[Bw, CUR.B_hT], w=[Bd])

        def tm_matmul(t, wt, Bw, dst_ap, Bd, ncols=512):
            for kc in range(8):
                S.op("pe", mk("matmul", out=dst_ap, lhsT=CUR.hT[:, kc, t * 128:(t + 1) * 128], rhs=wt[:, kc, 0:ncols], start=(kc == 0), stop=(kc == 7)),
                     r=[Bw, CUR.B_hT], w=[Bd])

        def decay_chain(P, l, h, dr, full):
            sp, B_sp = P["sp"]
            bneg, B_bn = P["bneg"]
            tmp, B_tmp = P["tmp"]
            Ed, B_Ed = P["Ed"][dr]
            nbl, B_nbl = P["nbl"]
            dec, B_dec = P["dec"]
            lrT, B_lrT = P["lrT"]
            wup, B_wup = P["wup"]
            di = dr * 4 + h
            pf, pb, Bp = mmring.next()
            S.op("pe", mk("matmul", out=pf[:, :], lhsT=wup[0:32, dr, h * 128:(h + 1) * 128], rhs=lrT[0:32, :], start=True, stop=True),
                 r=[B_wup, B_lrT], w=[Bp])
            S.op("act", mk("activation", out=sp[:], in_=pf[:, :], func=AF.Exp, bias=negb[:, l * 8 + di:l * 8 + di + 1], scale=-1.0),
                 r=[Bp, B_negb], w=[B_sp])
            S.op("act", mk("activation", out=sp[:], in_=sp[:], func=AF.Ln, bias=1.0, scale=1.0), r=[B_sp], w=[B_sp])
            S.op("dve", mk("tensor_tensor_scan", out=bneg[:], data0=P["msk"][0][:], data1=sp[:], initial=0.0, op0=ALU.mult, op1=ALU.add),
                 r=[B_sp, P["msk"][1]], w=[B_bn])
            S.op("dve", mk("tensor_scalar", out=nbl[:, di * 4:di * 4 + 4], in0=bneg[:].rearrange("p (c t) -> p c t", t=128)[:, :, 127],
                                                  scalar1=-1.0 / 16, scalar2=None, op0=ALU.mult), r=[B_bn], w=[B_nbl])
            S.op("act", mk("activation", out=dec[:, di * 4:di * 4 + 4], in_=nbl[:, di * 4:di * 4 + 4], func=AF.Exp), r=[B_nbl], w=[B_dec])
            if dr == 0:
                for c in range(4):
                    S.op("act", mk("activation", out=Ed[:, c * 128:(c + 1) * 128], in_=bneg[:, c * 128:(c + 1) * 128], func=AF.Exp,
                                                            bias=nbl[:, di * 4 + c:di * 4 + c + 1], scale=1.0 / 16), r=[B_bn, B_nbl], w=[B_Ed])
                if full:
                    Ep, B_Ep = P["Ep"][dr]
                    Em, B_Em = P["Em"][dr]
                    S.op("act", mk("activation", out=Ep[:], in_=bneg[:], func=AF.Exp, scale=-1.0 / 16), r=[B_bn], w=[B_Ep])
                    S.op("act", mk("activation", out=Em[:], in_=bneg[:], func=AF.Exp, scale=1.0 / 16), r=[B_bn], w=[B_Em])
            else:
                S.op("dve", mk("tensor_tensor", out=tmp[:], in0=sp[:], in1=bneg[:], op=ALU.subtract), r=[B_sp, B_bn], w=[B_tmp])
                S.op("act", mk("activation", out=Ed[:], in_=tmp[:], func=AF.Exp, scale=1.0 / 16), r=[B_tmp], w=[B_Ed])
                if full:
                    Ep, B_Ep = P["Ep"][dr]
                    Em, B_Em = P["Em"][dr]
                    nnbl, B_nn = P["nnbl"]
                    S.op("dve", mk("tensor_scalar", out=nnbl[:, 0:4], in0=nbl[:, di * 4:di * 4 + 4], scalar1=-1.0, scalar2=None, op0=ALU.mult),
                         r=[B_nbl], w=[B_nn])
                    for c in range(4):
                        S.op("act", mk("activation", out=Ep[:, c * 128:(c + 1) * 128], in_=tmp[:, c * 128:(c + 1) * 128], func=AF.Exp,
                                                                bias=nbl[:, di * 4 + c:di * 4 + c + 1], scale=-1.0 / 16), r=[B_tmp, B_nbl], w=[B_Ep])
                        S.op("act", mk("activation", out=Em[:, c * 128:(c + 1) * 128], in_=tmp[:, c * 128:(c + 1) * 128], func=AF.Exp,
                                                                bias=nnbl[:, c:c + 1], scale=1.0 / 16), r=[B_tmp, B_nn], w=[B_Em])

        def alloc_decay_tiles(st, P, full):
            P["sp"] = sb(st, "sp", [128, T], F32)
            P["bneg"] = sb(st, "bneg", [128, T], F32)
            P["tmp"] = sb(st, "tmpd", [128, T], F32)
            P["Ed"] = [sb(st, f"Ed{d}", [128, T], F32) for d in range(2)]
            P["nbl"] = sb(st, "nbl", [128, 32], F32)
            P["nnbl"] = sb(st, "nnbl", [128, 4], F32)
            P["dec"] = sb(st, "dec", [128, 32], F32)
            P["lrT"] = sb(st, "lrT", [128, T], BF16)
            P["wup"] = sb(st, "wup", [32, 2, 512], BF16)
            P["msk"] = sb(st, "msk", [128, T], F32)
            P["kdT"] = [sb(st, f"kdT{d}", [128, T], BF16) for d in range(2)]
            P["kd"] = [sb(st, f"kd{d}", [128, NTL, 128], BF16) for d in range(2)]
            P["v"] = sb(st, "v", [128, NTL, D], BF16)
            if full:
                P["Ep"] = [sb(st, f"Ep{d}", [128, T], F32) for d in range(2)]
                P["Em"] = [sb(st, f"Em{d}", [128, T], F32) for d in range(2)]

        def phase_setup(P, l):
            msk, B_msk = P["msk"]
            wup, B_wup = P["wup"]
            S.op("dve", mk("memset", msk[:], 1.0), w=[B_msk])
            S.op("dve", mk("memset", msk[:].rearrange("p (c t) -> p c t", t=128)[:, :, 0:1], 0.0), w=[B_msk])
            S.op("dve", mk("memset", wup[:], 0.0), w=[B_wup])
            S.dma("pool", mk("dma_start", out=wup[0:16, 0, :], in_=w_upf[l]), B_wup, True)
            S.dma("pool", mk("dma_start", out=wup[16:32, 1, :], in_=w_upb[l]), B_wup, True)

        def lr_and_v(P, l, wq, j):
            lrT, B_lrT = P["lrT"]
            v, B_v = P["v"]
            wt, Bw = wq.get(("lr", j))
            pf, pb, Bp = mmring.next()
            for kc in range(8):
                S.op("pe", mk("matmul", out=pf[0:32, :], lhsT=wt[:, kc, 0:32], rhs=CUR.hT[:, kc, :], start=(kc == 0), stop=(kc == 7)),
                     r=[Bw, CUR.B_hT], w=[Bp])
            S.op("act", mk("activation", out=lrT[0:32, :], in_=pf[0:32, :], func=AF.Identity), r=[Bp], w=[B_lrT])
            wq.done(("lr", j))

        def v_tiles(P, wq, j):
            v, B_v = P["v"]
            for half in range(2):
                wt, Bw = wq.get(("v", j, half))
                for t in range(NTL):
                    pf, pb, Bp = mmring.next()
                    tm_matmul(t, wt, Bw, pf[:, :], Bp)
                    S.op("act", mk("activation", out=v[:, t, half * 512:(half + 1) * 512], in_=pf[:, :], func=AF.Identity),
                         r=[Bp], w=[B_v])
                wq.done(("v", j, half))

        def k_side(P, wq, j, h, full):
            wt, Bw = wq.get(("k", j))
            pf, pb, Bp = mmring.next()
            fm_matmul(wt, Bw, h * 128, pf[:, :], Bp)
            if h == 3:
                wq.done(("k", j))
            for dr in range(2):
                kdT, B_kdT = P["kdT"][dr]
                Ed, B_Ed = P["Ed"][dr]
                S.op("dve", mk("tensor_tensor", out=kdT[:], in0=pf[:, :], in1=Ed[:], op=ALU.mult), r=[Bp, B_Ed], w=[B_kdT])
                if full:
                    keT, B_keT = P["keT"][dr]
                    Em, B_Em = P["Em"][dr]
                    S.op("dve", mk("tensor_tensor", out=keT[:], in0=pf[:, :], in1=Em[:], op=ALU.mult), r=[Bp, B_Em], w=[B_keT])
            tf, tb, Bt = mmring.next()
            for dr in range(2):
                kdT, B_kdT = P["kdT"][dr]
                for t in range(NTL):
                    S.op("pe", mk("transpose", out=tb[:, dr * 512 + t * 128:dr * 512 + (t + 1) * 128],
                                                                          in_=kdT[:, t * 128:(t + 1) * 128], identity=ident[:]),
                         r=[B_kdT, B_ident], w=[Bt])
            for dr in range(2):
                kd, B_kd = P["kd"][dr]
                S.op("act", mk("activation", out=kd[:].rearrange("p t d -> p (t d)"), in_=tb[:, dr * 512:(dr + 1) * 512], func=AF.Identity),
                     r=[Bt], w=[B_kd])

        def kv_mm(P, h, dr, t):
            kd, B_kd = P["kd"][dr]
            v, B_v = P["v"]
            pk, cof, Bg = gring.next()
            tt = 0 if DBG_VARIANT == 1 else t
            if DBG_VARIANT == 2:
                cof = 0
            S.op("pe", mk("matmul", out=pk[:, cof:cof + 256], lhsT=kd[:, tt, :], rhs=v[:, tt, h * 256:(h + 1) * 256], start=True, stop=True),
                 r=[B_kd, B_v], w=[Bg])
            return pk, cof, Bg

        def phase_A(l, xsrc, pay, B_pay):
            with contextlib.ExitStack() as st:
                P = {}
                alloc_decay_tiles(st, P, False)
                Sw = [sb(st, f"Sw{i}", [128, 1024], F32) for i in range(2)]
                ubf = [sb(st, f"ubf{i}", [128, 512], BF16) for i in range(2)]
                if l == 0:
                    load_consts()
                phase_setup(P, l)
                ubr = Ring(ubf)
                hTs = [hT0, sb(st, "hT1", [128, 8, T], BF16)]
                WS.slots = list(wsl) + [sb(st, f"wx{i}", [128, 8, 512], BF16) for i in range(2)]
                EdA = [[sb(st, f"EdA{h}{d}", [128, T], F32) for d in range(2)] for h in range(4)]
                kdTs = [P["kdT"], [sb(st, f"kdTb{d}", [128, T], BF16) for d in range(2)]]
                kds = [P["kd"], [sb(st, f"kdb{d}", [128, NTL, 128], BF16) for d in range(2)]]
                scrA = [(P["sp"], P["bneg"], P["tmp"]), (sb(st, "sp2", [128, T], F32), sb(st, "bneg2", [128, T], F32), sb(st, "tmp2", [128, T], F32))]

                def Pv(h, k):
                    d = dict(P)
                    d["Ed"] = EdA[h]
                    d["sp"], d["bneg"], d["tmp"] = scrA[k % 2]
                    return d

                def u_section(wq, j):
                        wt, Bw = wq.get(("u", j))
                        for t in range(NTL):
                            pf, pb, Bp = mmring.next()
                            tm_matmul(t, wt, Bw, pf[:, :], Bp)
                            ub, B_ub = ubr.next()
                            gt = j * NTL + t
                            if gt == 0 or gt == NB * NTL - 1:
                                pc0 = 2056 if gt == 0 else 2056 + 512
                                S.op("act", mk("activation", out=pay[:, pc0:pc0 + 512], in_=pf[:, :], func=AF.Identity), r=[Bp], w=[B_pay])
                                S.op("dve", mk("tensor_copy", out=ub[:], in_=pay[:, pc0:pc0 + 512]), r=[B_pay], w=[B_ub])
                            else:
                                S.op("act", mk("activation", out=ub[:], in_=pf[:, :], func=AF.Identity), r=[Bp], w=[B_ub])
                            S.dma("sp", mk("dma_start", out=UALL[1 + gt], in_=ub[:]), B_ub, False)
                        wq.done(("u", j))

                setcur(0, hTs)
                load_x(xsrc, 0)
                norm_block(l * 40 + 0)
                wq = WQ()
                for j in range(NB):
                    wq.add(("lr", j), win_grp(l, C_LR, 32), 8, 32)
                    for half in range(2):
                        wq.add(("v", j, half), win_grp(l, C_V + half * 512))
                    wq.add(("u", j), win_grp(l, C_U))
                    wq.add(("k", j), win_grp(l, C_K))
                for j in range(NB):
                    setcur(j, hTs)
                    if j + 1 < NB:
                        setcur(j + 1, hTs)
                        load_x(xsrc, j + 1)
                        setcur(j, hTs)
                    ms = (lambda nm: print("MS", nm, S.total_ops)) if (MILESTONES and j == 0) else (lambda nm: None)
                    ms("start")
                    lr_and_v(P, l, wq, j)
                    cnt = 0
                    for h in range(4):
                        for dr in range(2):
                            decay_chain(Pv(h, cnt), l, h, dr, False)
                            cnt += 1
                    v_tiles(P, wq, j)
                    u_section(wq, j)
                    def k_part1(h):
                        Pq = Pv(h, 0)
                        wt, Bw = wq.get(("k", j))
                        pf, pb, Bp = mmring.next()
                        fm_matmul(wt, Bw, h * 128, pf[:, :], Bp)
                        if h == 3:
                            wq.done(("k", j))
                        for dr in range(2):
                            kdT, B_kdT = kdTs[h % 2][dr]
                            Ed, B_Ed = Pq["Ed"][dr]
                            S.op("dve", mk("tensor_tensor", out=kdT[:], in0=pf[:, :], in1=Ed[:], op=ALU.mult), r=[Bp, B_Ed], w=[B_kdT])

                    def k_part2(h):
                        tf, tb, Bt = mmring.next()
                        for dr in range(2):
                            kdT, B_kdT = kdTs[h % 2][dr]
                            for t in range(NTL):
                                S.op("pe", mk("transpose", out=tb[:, dr * 512 + t * 128:dr * 512 + (t + 1) * 128],
                                              in_=kdT[:, t * 128:(t + 1) * 128], identity=ident[:]), r=[B_kdT, B_ident], w=[Bt])
                        for dr in range(2):
                            kd, B_kd = kds[h % 2][dr]
                            S.op("act", mk("activation", out=kd[:].rearrange("p t d -> p (t d)"), in_=tb[:, dr * 512:(dr + 1) * 512], func=AF.Identity),
                                 r=[Bt], w=[B_kd])

                    def states(h):
                        Ph = dict(P)
                        Ph["kd"] = kds[h % 2]
                        for dr in range(2):
                            di = dr * 4 + h
                            sw, B_sw = Sw[dr]
                            order = range(NTL) if dr == 0 else range(NTL - 1, -1, -1)
                            first = True
                            for t in order:
                                pk, cof, Bg = kv_mm(Ph, h, dr, t)
                                if first:
                                    S.op("act", mk("activation", out=sw[:, h * 256:(h + 1) * 256], in_=pk[:, cof:cof + 256], func=AF.Identity),
                                         r=[Bg], w=[B_sw])
                                    first = False
                                else:
                                    S.op("dve", mk("scalar_tensor_tensor",
                                        out=sw[:, h * 256:(h + 1) * 256], in0=sw[:, h * 256:(h + 1) * 256], scalar=P["dec"][0][:, di * 4 + t:di * 4 + t + 1],
                                        in1=pk[:, cof:cof + 256], op0=ALU.mult, op1=ALU.add), r=[B_sw, Bg, P["dec"][1]], w=[B_sw])
                            S.op("dve", mk("tensor_reduce", out=LD[:, dr, h, j:j + 1], in_=P["nbl"][0][:, di * 4:di * 4 + 4], axis=AX.X, op=ALU.add),
                                 r=[P["nbl"][1]], w=[B_LD])

                    k_part1(0)
                    k_part1(1)
                    k_part2(0)
                    if j + 1 < NB:
                        setcur(j + 1, hTs)
                        norm_block(l * 40 + 0)
                        setcur(j, hTs)
                    for h in range(4):
                        if h + 2 < 4:
                            k_part1(h + 2)
                        states(h)
                        if h + 1 < 4:
                            k_part2(h + 1)
                    for dr in range(2):
                        S.dma("sp", mk("dma_start", out=SLOC[j, dr], in_=Sw[dr][0][:]), Sw[dr][1], False)
                    if MILESTONES:
                        print("MS endblock", j, S.total_ops)
                S.emit()
                assert not wlive
                WS.slots = list(wsl)

        def phase_C(l, pay, B_pay):
            if True:
                with contextlib.ExitStack() as st:
                    Dall, B_D = sb(st, "Dall", [128, 2, 4, NB], F32)
                    Lt, B_Lt = sb(st, "Lt", [128, 8], F32)
                    stg = [sb(st, f"stg{i}", [128, 1024], F32) for i in range(4)]
                    car = [[sb(st, f"car{d}_{i}", [128, 1024], F32) for i in range(2)] for d in range(2)]
                    G = [sb(st, f"G{i}", [128, PAYW], F32) for i in range(2)]
                    tmpc, B_tmpc = sb(st, "tmpc", [128, 1024], F32)
                    Dm, B_Dm = sb(st, "Dm", [128, 8], F32)
                    uh = [sb(st, f"uh{i}", [128, 512], F32) for i in range(2)]
                    uhb = [sb(st, f"uhb{i}", [128, 512], BF16) for i in range(2)]
                    stgr = Ring(stg)
                    Gr = Ring(G)
                    S.op("act", mk("activation", out=Dall[:].rearrange("p a b c -> p (a b c)"), in_=LD[:].rearrange("p a b c -> p (a b c)"), func=AF.Exp),
                         r=[B_LD], w=[B_D])
                    S.op("dve", mk("tensor_reduce", out=Lt[:, 0:8], in_=LD[:].rearrange("p a b c -> p (a b) c"), axis=AX.X, op=ALU.add), r=[B_LD], w=[B_Lt])
                    S.op("act", mk("activation", out=pay[:, 2048:2056], in_=Lt[:, 0:8], func=AF.Exp), r=[B_Lt], w=[B_pay])
                    for dr in range(2):
                        order = list(range(NB)) if dr == 0 else list(range(NB - 1, -1, -1))
                        for n, j in enumerate(order):
                            if n == 0:
                                S.dma("sp", mk("dma_start", out=pay[:, dr * 1024:(dr + 1) * 1024], in_=SLOC[j, dr]), B_pay, True)
                            else:
                                sg, B_sg = stgr.next()
                                S.dma("sp", mk("dma_start", out=sg[:], in_=SLOC[j, dr]), B_sg, True)
                                for h in range(4):
                                    S.op("dve", mk("scalar_tensor_tensor",
                                        out=pay[:, dr * 1024 + h * 256:dr * 1024 + (h + 1) * 256], in0=pay[:, dr * 1024 + h * 256:dr * 1024 + (h + 1) * 256],
                                        scalar=Dall[:, dr, h, j:j + 1], in1=sg[:, h * 256:(h + 1) * 256], op0=ALU.mult, op1=ALU.add),
                                        r=[B_pay, B_sg, B_D], w=[B_pay])
                    S.dma("sp", mk("dma_start", out=SEGa, in_=pay[:, 0:2048]), B_pay, False)
                    S.dma("sp", mk("dma_start", out=SEGb, in_=pay[:, 2048:PAYW]), B_pay, False)
                    S.emit()
                    cc_count[0] += 1
                    ccv = cc_count[0]
                    B_cc = S.buf("ccdummy")

                    def ccfn(e):
                        if SIM_CC:
                            for rr in range(4):
                                e.dma_start(out=GATHa[rr * 128:(rr + 1) * 128, :], in_=SEGa).then_inc(ccsem, 16)
                                e.dma_start(out=GATHb[rr * 128:(rr + 1) * 128, :], in_=SEGb).then_inc(ccsem, 16)
                            e.wait_ge(ccsem, 128 * ccv)
                            return e.memset(Dm[:, 0:1], 0.0)
                        for (si, go) in ((SEGa, GATHa), (SEGb, GATHb)):
                            i = e.collective_compute("AllGather", ALU.bypass, replica_groups=[[0, 1, 2, 3], [4, 5, 6, 7]], ins=[si], outs=[go])
                            i.then_inc(ccsem, 1)
                        e.wait_ge(ccsem, 2 * ccv)
                        return e.memset(Dm[:, 0:1], 0.0)
                    S.op("pool", ccfn, w=[B_Dm, B_cc])
                    for dr in range(2):
                        c0, B_c0 = car[dr][0]
                        S.op("dve", mk("memset", c0[:], 0.0), w=[B_c0])
                    for dr in range(2):
                        c0, B_c0 = car[dr][0]
                        ranks = range(4) if dr == 0 else range(3, -1, -1)
                        for r in ranks:
                            g, B_g = Gr.next()
                            S.dma("sp", mk("dma_start", out=g[:, dr * 1024:(dr + 1) * 1024], in_=GATHa[r * 128:(r + 1) * 128, dr * 1024:(dr + 1) * 1024]),
                                  B_g, True, r=[B_cc])
                            if dr == 0:
                                S.dma("sp", mk("dma_start", out=g[:, 2048:PAYW], in_=GATHb[r * 128:(r + 1) * 128, :]), B_g, True, r=[B_cc])
                            else:
                                S.dma("sp", mk("dma_start", out=g[:, 2048:2056], in_=GATHb[r * 128:(r + 1) * 128, 0:8]), B_g, True, r=[B_cc])
                            mcol = dr * 4 + r
                            S.op("dve", mk("tensor_scalar", out=Dm[:, 0:4], in0=g[:, 2048 + dr * 4:2048 + dr * 4 + 4], scalar1=-1.0, scalar2=None,
                                                                             op0=ALU.add), r=[B_g], w=[B_Dm])
                            S.op("dve", mk("tensor_scalar", out=Dm[:, 0:4], in0=Dm[:, 0:4], scalar1=sel[:, mcol:mcol + 1], scalar2=1.0,
                                                                            op0=ALU.mult, op1=ALU.add), r=[B_Dm, B_sel], w=[B_Dm])
                            S.op("dve", mk("tensor_scalar", out=tmpc[:], in0=g[:, dr * 1024:(dr + 1) * 1024], scalar1=sel[:, mcol:mcol + 1],
                                                                                        scalar2=None, op0=ALU.mult), r=[B_g, B_sel], w=[B_tmpc])
                            for h in range(4):
                                S.op("dve", mk("scalar_tensor_tensor", out=c0[:, h * 256:(h + 1) * 256], in0=c0[:, h * 256:(h + 1) * 256],
                                                                                        scalar=Dm[:, h:h + 1], in1=tmpc[:, h * 256:(h + 1) * 256],
                                                                                        op0=ALU.mult, op1=ALU.add), r=[B_c0, B_Dm, B_tmpc], w=[B_c0])
                            if dr == 0:
                                for k, (colsel, uofs) in enumerate(((8 + r, 2056 + 512), (12 + r, 2056))):
                                    u_, B_u = uh[k]
                                    if r == 0:
                                        S.op("dve", mk("tensor_scalar",
                                            out=u_[:], in0=g[:, uofs:uofs + 512], scalar1=sel[:, colsel:colsel + 1], scalar2=None, op0=ALU.mult),
                                            r=[B_g, B_sel], w=[B_u])
                                    else:
                                        S.op("dve", mk("scalar_tensor_tensor",
                                            out=u_[:], in0=g[:, uofs:uofs + 512], scalar=sel[:, colsel:colsel + 1], in1=u_[:], op0=ALU.mult, op1=ALU.add),
                                            r=[B_g, B_sel, B_u], w=[B_u])
                    for k in range(2):
                        S.op("act", mk("activation", out=uhb[k][0][:], in_=uh[k][0][:], func=AF.Identity), r=[uh[k][1]], w=[uhb[k][1]])
                        S.dma("sp", mk("dma_start", out=UALL[0 if k == 0 else 33], in_=uhb[k][0][:]), uhb[k][1], False)
                    for dr in range(2):
                        order = list(range(NB)) if dr == 0 else list(range(NB - 1, -1, -1))
                        cur = 0
                        for n, j in enumerate(order):
                            cc_, B_cc_ = car[dr][cur]
                            S.dma("sp", mk("dma_start", out=SIN[j, dr], in_=cc_[:]), B_cc_, False)
                            if n == NB - 1:
                                break
                            nx, B_nx = car[dr][1 - cur]
                            sg, B_sg = stgr.next()
                            S.dma("sp", mk("dma_start", out=sg[:], in_=SLOC[j, dr]), B_sg, True)
                            for h in range(4):
                                S.op("dve", mk("scalar_tensor_tensor",
                                    out=nx[:, h * 256:(h + 1) * 256], in0=cc_[:, h * 256:(h + 1) * 256], scalar=Dall[:, dr, h, j:j + 1],
                                    in1=sg[:, h * 256:(h + 1) * 256], op0=ALU.mult, op1=ALU.add), r=[B_cc_, B_sg, B_D], w=[B_nx])
                            cur = 1 - cur
                    S.emit()

        def phase_M(l, xsrc, xdst):
            with contextlib.ExitStack() as st:
                P = {}
                alloc_decay_tiles(st, P, True)
                P["keT"] = [sb(st, f"keT{d}", [128, T], BF16) for d in range(2)]
                qeT = [sb(st, f"qeT{d}", [128, T], BF16) for d in range(2)]
                Sbf = [sb(st, f"Sbf{d}", [128, NTL, 256], BF16) for d in range(2)]
                Swk = [[sb(st, f"Swk{d}_{i}", [128, 256], F32) for i in range(2)] for d in range(2)]
                sinb = [sb(st, f"sin{d}", [128, 1024], F32) for d in range(2)]
                srm, B_srm = sb(st, "srm", [128, NTL * D], BF16)
                sr2 = (srm[:].rearrange("p (t d) -> p t d", t=NTL), B_srm)
                sg_t = [sb(st, f"sgt{i}", [128, 512], F32) for i in range(2)]
                og = xn
                ogT, B_ogT = sb(st, "ogT", [128, 8, T], BF16)
                ut = [sb(st, f"ut{i}", [128, 512], BF16) for i in range(NTL + 2)]
                pooledT, B_pT = sb(st, "pooledT", [128, 4, T], BF16)
                mergedT, B_mT = srm[:].rearrange("p (c t) -> p c t", c=8), B_srm
                mt = [sb(st, f"mt{i}", [128, T], F32) for i in range(4)]
                scm = [sb(st, f"scm{i}", [128, 256], BF16) for i in range(4)]
                ssq, B_ssq = sb(st, "ssq", [128, 16], F32)
                masks, B_masks = sb(st, "masks", [128, 2, 128], BF16)
                bands, B_bands = sb(st, "bands", [128, 7, 4, 128], BF16)
                wpool, B_wpool = sb(st, "wpool", [128, 4, 256], BF16)
                phase_setup(P, l)
                S.dma("pool", mk("dma_start", out=masks[:], in_=masks_d), B_masks, True)
                S.dma("pool", mk("dma_start", out=bands[:], in_=bands_d), B_bands, True)
                S.dma("pool", mk("dma_start", out=wpool[:], in_=w_pg[l].rearrange("g c d -> c g d")), B_wpool, True)
                sgr = Ring(sg_t)
                mtr = Ring(mt)
                scr = Ring(scm)
                v, B_v = P["v"]
                hTs = [hT0, hT0]
                setcur(0, hTs)
                load_x(xsrc, 0)
                norm_block(l * 40 + 0)
                wq = WQ()
                for j in range(NB):
                    wq.add(("lr", j), win_grp(l, C_LR, 32), 8, 32)
                    for half in range(2):
                        wq.add(("v", j, half), win_grp(l, C_V + half * 512))
                    for half in range(2):
                        wq.add(("r", j, half), win_grp(l, C_R + half * 512))
                    wq.add(("k", j), win_grp(l, C_K))
                    wq.add(("q", j), win_grp(l, C_Q))
                    for half in range(2):
                        wq.add(("bg", j, half), w_bg[l].rearrange("(kc p) c -> p kc c", p=128)[:, :, half * 512:(half + 1) * 512])
                        wq.add(("ga", j, half), win_grp(l, C_GA + half * 512))
                        wq.add(("gb", j, half), win_grp(l, C_GB + half * 512))
                    for half in range(2):
                        wq.add(("wo", j, half), w_o[l].rearrange("(kc p) c -> p kc c", p=128)[:, :, half * 512:(half + 1) * 512])
                for j in range(NB):
                    setcur(j, hTs)
                    if j + 1 < NB:
                        setcur(j + 1, hTs)
                        load_x(xsrc, j + 1)
                        setcur(j, hTs)
                    for i in range(NTL + 2):
                        S.dma("sp", mk("dma_start", out=ut[i][0][:], in_=UALL[j * NTL + i]), ut[i][1], True)
                    for dr in range(2):
                        S.dma("sp", mk("dma_start", out=sinb[dr][0][:], in_=SIN[j, dr]), sinb[dr][1], True)
                    lr_and_v(P, l, wq, j)
                    v_tiles(P, wq, j)
                    for half in range(2):
                        wt, Bw = wq.get(("r", j, half))
                        for t in range(NTL):
                            pf, pb, Bp = mmring.next()
                            tm_matmul(t, wt, Bw, pf[:, :], Bp)
                            sgt, B_sgt = sgr.next()
                            S.op("act", mk("activation", out=sgt[:], in_=pf[:, :], func=AF.Tanh, scale=0.5), r=[Bp], w=[B_sgt])
                            S.op("dve", mk("scalar_tensor_tensor",
                                out=sr2[0][:, t, half * 512:(half + 1) * 512], in0=sgt[:], scalar=1.0, in1=pf[:, :], op0=ALU.add, op1=ALU.mult),
                                r=[B_sgt, Bp], w=[sr2[1]])
                        wq.done(("r", j, half))
                    for h in range(4):
                        for dr in range(2):
                            decay_chain(P, l, h, dr, True)
                        k_side(P, wq, j, h, True)
                        wt, Bw = wq.get(("q", j))
                        pf, pb, Bp = mmring.next()
                        fm_matmul(wt, Bw, h * 128, pf[:, :], Bp)
                        if h == 3:
                            wq.done(("q", j))
                        for dr in range(2):
                            S.op("dve", mk("scalar_tensor_tensor", out=qeT[dr][0][:], in0=pf[:, :], scalar=float(128 ** -0.5), in1=P["Ep"][dr][0][:],
                                                                                     op0=ALU.mult, op1=ALU.mult), r=[Bp, P["Ep"][dr][1]], w=[qeT[dr][1]])
                        for dr in range(2):
                            di = dr * 4 + h
                            order = list(range(NTL)) if dr == 0 else list(range(NTL - 1, -1, -1))
                            sbf, B_sbf = Sbf[dr]
                            t0 = order[0]
                            S.op("act", mk("activation", out=sbf[:, t0, :], in_=sinb[dr][0][:, h * 256:(h + 1) * 256], func=AF.Identity),
                                 r=[sinb[dr][1]], w=[B_sbf])
                            prev_ap, B_prev = sinb[dr][0][:, h * 256:(h + 1) * 256], sinb[dr][1]
                            for n in range(NTL - 1):
                                t = order[n]
                                tn = order[n + 1]
                                pk, cof, Bg = kv_mm(P, h, dr, t)
                                sw, B_sw = Swk[dr][n % 2]
                                S.op("dve", mk("scalar_tensor_tensor",
                                    out=sw[:], in0=prev_ap, scalar=P["dec"][0][:, di * 4 + t:di * 4 + t + 1], in1=pk[:, cof:cof + 256],
                                    op0=ALU.mult, op1=ALU.add), r=[B_prev, Bg, P["dec"][1]], w=[B_sw])
                                S.op("act", mk("activation", out=sbf[:, tn, :], in_=sw[:], func=AF.Identity), r=[B_sw], w=[B_sbf])
                                prev_ap, B_prev = sw[:], B_sw
                        scall = []
                        for t in range(NTL):
                            pk, _, Bg = mmring.next()
                            for dr in range(2):
                                keT, B_keT = P["keT"][dr]
                                S.op("pe", mk("matmul", out=pk[:, dr * 128:(dr + 1) * 128], lhsT=keT[:, t * 128:(t + 1) * 128],
                                              rhs=qeT[dr][0][:, t * 128:(t + 1) * 128], start=True, stop=True),
                                     r=[B_keT, qeT[dr][1]], w=[Bg])
                            sc, B_sc = scr.next()
                            S.op("dve", mk("tensor_tensor", out=sc[:], in0=pk[:, 0:256], in1=masks[:].rearrange("p a b -> p (a b)"), op=ALU.mult),
                                 r=[Bg, B_masks], w=[B_sc])
                            scall.append((sc, B_sc))
                        for t in range(NTL):
                            scs = [(scall[t][0][:, 0:128], scall[t][1]), (scall[t][0][:, 128:256], scall[t][1])]
                            po, cofo, Bo = gring.next()
                            vh = v[:, t, h * 256:(h + 1) * 256]
                            S.op("pe", mk("matmul", out=po[:, cofo:cofo + 256], lhsT=scs[0][0], rhs=vh, start=True, stop=False),
                                 r=[scs[0][1], B_v], w=[Bo])
                            S.op("pe", mk("matmul", out=po[:, cofo:cofo + 256], lhsT=qeT[0][0][:, t * 128:(t + 1) * 128], rhs=Sbf[0][0][:, t, :],
                                                                              start=False, stop=False), r=[qeT[0][1], Sbf[0][1]], w=[Bo])
                            S.op("pe", mk("matmul", out=po[:, cofo:cofo + 256], lhsT=scs[1][0], rhs=vh, start=False, stop=False),
                                 r=[scs[1][1], B_v], w=[Bo])
                            S.op("pe", mk("matmul", out=po[:, cofo:cofo + 256], lhsT=qeT[1][0][:, t * 128:(t + 1) * 128], rhs=Sbf[1][0][:, t, :],
                                                                              start=False, stop=True), r=[qeT[1][1], Sbf[1][1]], w=[Bo])
                            col = t * 4 + h
                            S.op("act", mk("activation", out=junk[:, 0:256], in_=po[:, cofo:cofo + 256], func=AF.Square,
                                                                                         accum_out=ssq[:, col:col + 1]), r=[Bo], w=[B_ssq])
                            rsqrt_cols(ssq, B_ssq, col, col + 1, 1.0 / 256)
                            S.op("dve", mk("scalar_tensor_tensor",
                                out=og[t][0][:, h * 256:(h + 1) * 256], in0=po[:, cofo:cofo + 256], scalar=ssq[:, col:col + 1],
                                in1=sr2[0][:, t, h * 256:(h + 1) * 256], op0=ALU.mult, op1=ALU.mult), r=[Bo, B_ssq, sr2[1]], w=[og[t][1]])
                    for t in range(NTL):
                        gt = j * NTL + t
                        sp_ = 3 if gt == 0 else 0
                        sm_ = 4 if gt == 0 else (5 if gt == NB * NTL - 1 else 1)
                        sn_ = 6 if gt == NB * NTL - 1 else 2
                        pf, pb, Bp = mmring.next()
                        for g in range(4):
                            for k, (ui, bs) in enumerate(((t, sp_), (t + 1, sm_), (t + 2, sn_))):
                                S.op("pe", mk("matmul", out=pf[:, g * 128:(g + 1) * 128], lhsT=ut[ui][0][:, g * 128:(g + 1) * 128],
                                                                                             rhs=bands[:, bs, g, :], start=(k == 0), stop=(k == 2)),
                                     r=[ut[ui][1], B_bands], w=[Bp])
                        S.op("act", mk("activation", out=pooledT[:, :, t * 128:(t + 1) * 128], in_=pf[:, :].rearrange("p (g t) -> p g t", g=4),
                                                                      func=AF.Identity), r=[Bp], w=[B_pT])
                    for pc in range(4):
                        pf, pb, Bp = mmring.next()
                        for t in range(NTL):
                            for c2 in range(2):
                                ch = 2 * pc + c2
                                S.op("pe", mk("transpose",
                                    out=pb[:, c2 * 512 + t * 128:c2 * 512 + (t + 1) * 128], in_=og[t][0][:, ch * 128:(ch + 1) * 128], identity=ident[:]),
                                    r=[og[t][1], B_ident], w=[Bp])
                        for c2 in range(2):
                            ch = 2 * pc + c2
                            S.op("act", mk("activation", out=ogT[:, ch, :], in_=pb[:, c2 * 512:(c2 + 1) * 512], func=AF.Identity,
                                                                                  scale=gnh[:, l * 8 + ch:l * 8 + ch + 1]), r=[Bp, B_gnh], w=[B_ogT])
                    for i in range(8):
                        half = i // 4
                        wbg, Bwbg = wq.get(("bg", j, half))
                        wga, Bwga = wq.get(("ga", j, half))
                        wgb, Bwgb = wq.get(("gb", j, half))
                        cofs = (i % 4) * 128
                        pya, _, Bya = fring.next()
                        for kc in range(8):
                            S.op("pe", mk("matmul", out=pya[:, :], lhsT=wbg[:, kc, cofs:cofs + 128], rhs=ogT[:, kc, :],
                                                                                             start=(kc == 0), stop=(kc == 7)), r=[Bwbg, B_ogT], w=[Bya])
                        pyb, _, Byb = fring.next()
                        g = i // 2
                        S.op("pe", mk("matmul", out=pyb[:, :], lhsT=wpool[:, g, (i % 2) * 128:(i % 2 + 1) * 128], rhs=pooledT[:, g, :],
                                                                        start=True, stop=True), r=[B_wpool, B_pT], w=[Byb])
                        pga, _, Bga = fring.next()
                        fm_matmul(wga, Bwga, cofs, pga[:, :], Bga)
                        pgb, _, Bgb = fring.next()
                        fm_matmul(wgb, Bwgb, cofs, pgb[:, :], Bgb)
                        sa, B_sa = mtr.next()
                        sb_, B_sb = mtr.next()
                        S.op("act", mk("activation", out=sa[:], in_=pga[:, :], func=AF.Tanh, scale=0.5), r=[Bga], w=[B_sa])
                        S.op("act", mk("activation", out=sb_[:], in_=pgb[:, :], func=AF.Tanh, scale=0.5), r=[Bgb], w=[B_sb])
                        S.op("dve", mk("scalar_tensor_tensor", out=sa[:], in0=sa[:], scalar=1.0, in1=pya[:, :], op0=ALU.add, op1=ALU.mult),
                             r=[B_sa, Bya], w=[B_sa])
                        S.op("dve", mk("tensor_scalar", out=sb_[:], in0=sb_[:], scalar1=1.0, scalar2=psh[:, l * 8 + i:l * 8 + i + 1], op0=ALU.add, op1=ALU.mult),
                             r=[B_sb, B_psh], w=[B_sb])
                        S.op("dve", mk("tensor_tensor", out=sb_[:], in0=sb_[:], in1=pyb[:, :], op=ALU.mult), r=[B_sb, Byb], w=[B_sb])
                        S.op("dve", mk("scalar_tensor_tensor", out=mergedT[:, i, :], in0=sa[:], scalar=0.5, in1=sb_[:], op0=ALU.mult, op1=ALU.add),
                             r=[B_sa, B_sb], w=[B_mT])
                        if i % 4 == 3:
                            wq.done(("bg", j, half))
                            wq.done(("ga", j, half))
                            wq.done(("gb", j, half))
                    if j + 1 < NB:
                        setcur(j + 1, hTs)
                        norm_block(l * 40 + 0)
                        setcur(j, hTs)
                    for half in range(2):
                        wo_, Bwo = wq.get(("wo", j, half))
                        for t in range(NTL):
                            pf, pb, Bp = fring.next()
                            for kc in range(8):
                                S.op("pe", mk("matmul", out=pf[:, :], lhsT=mergedT[:, kc, t * 128:(t + 1) * 128], rhs=wo_[:, kc, :],
                                                                                        start=(kc == 0), stop=(kc == 7)), r=[B_mT, Bwo], w=[Bp])
                            S.op("dve", mk("tensor_tensor", out=CUR.xt[t][0][:, half * 512:(half + 1) * 512], in0=pf[:, :],
                                                                                        in1=CUR.xt[t][0][:, half * 512:(half + 1) * 512], op=ALU.add),
                                 r=[Bp, CUR.xt[t][1]], w=[CUR.xt[t][1]])
                        wq.done(("wo", j, half))
                    for t in range(NTL):
                        r0 = j * T + t * 128
                        S.dma("sp", mk("dma_start", out=xdst[r0:r0 + 128, :], in_=CUR.xt[t][0][:]), CUR.xt[t][1], False)
                S.emit()

        def phase_F(l, xsrc, xdst, last):
            with contextlib.ExitStack() as st:
                actT, B_aT = sb(st, "actT", [128, NJ, T], BF16)
                ft = [sb(st, f"ft{i}", [128, T], F32) for i in range(4)]
                ftr = Ring(ft)
                if last:
                    gfin, B_gfin = sb(st, "gfin", [128, D], F32)
                    S.dma("sp", mk("dma_start", out=gfin[:], in_=gfin_d), B_gfin, True)
                    fst, B_fst = sb(st, "fst", [128, 8], F32)
                hTs = [hT0, sb(st, "hT1", [128, 8, T], BF16)]
                WS.slots = list(wsl) + [sb(st, f"wx{i}", [128, 8, 512], BF16) for i in range(4)]
                setcur(0, hTs)
                load_x(xsrc, 0)
                norm_block(l * 40 + 8)
                wq = WQ()
                for j in range(NB):
                    for jj in range(NJ // 2):
                        wq.add(("fg", j, jj), w_fi[l].rearrange("(kc p) c -> p kc c", p=128)[:, :, jj * 256:(jj + 1) * 256], 8, 256)
                        wq.add(("fu", j, jj), w_fi[l].rearrange("(kc p) c -> p kc c", p=128)[:, :, DFF + jj * 256:DFF + (jj + 1) * 256], 8, 256)
                    for half in range(2):
                        for jg in range(3):
                            nk = 8 if jg < 2 else NJ - 16
                            wq.add(("fo", j, half, jg), w_fo[l].rearrange("(jc p) c -> p jc c", p=128)[:, jg * 8:jg * 8 + nk, half * 512:(half + 1) * 512], nk, 512)
                for j in range(NB):
                    setcur(j, hTs)
                    if j + 1 < NB:
                        setcur(j + 1, hTs)
                        load_x(xsrc, j + 1)
                        setcur(j, hTs)
                    for jj in range(NJ // 2):
                        if jj == 5 and j + 1 < NB:
                            setcur(j + 1, hTs)
                            norm_block(l * 40 + 8)
                            setcur(j, hTs)
                        wg, Bwg = wq.get(("fg", j, jj))
                        wu, Bwu = wq.get(("fu", j, jj))
                        for c2 in range(2):
                            jc = jj * 2 + c2
                            pg, _, Bpg = fring.next()
                            fm_matmul(wg, Bwg, c2 * 128, pg[:, :], Bpg)
                            pu, _, Bpu = fring.next()
                            fm_matmul(wu, Bwu, c2 * 128, pu[:, :], Bpu)
                            f1, B_f1 = ftr.next()
                            S.op("act", mk("activation", out=f1[:], in_=pg[:, :], func=AF.Tanh, scale=0.5), r=[Bpg], w=[B_f1])
                            S.op("dve", mk("scalar_tensor_tensor", out=f1[:], in0=f1[:], scalar=1.0, in1=pg[:, :], op0=ALU.add, op1=ALU.mult),
                                 r=[B_f1, Bpg], w=[B_f1])
                            S.op("dve", mk("scalar_tensor_tensor", out=actT[:, jc, :], in0=f1[:], scalar=0.5, in1=pu[:, :], op0=ALU.mult, op1=ALU.mult),
                                 r=[B_f1, Bpu], w=[B_aT])
                        wq.done(("fg", j, jj))
                        wq.done(("fu", j, jj))
                    for half in range(2):
                        pts = [fring.next() for _ in range(NTL)]
                        for jg in range(3):
                            nk = 8 if jg < 2 else NJ - 16
                            wf, Bwf = wq.get(("fo", j, half, jg))
                            for t in range(NTL):
                                pf, _, Bp = pts[t]
                                for k in range(nk):
                                    jc = jg * 8 + k
                                    S.op("pe", mk("matmul", out=pf[:, :], lhsT=actT[:, jc, t * 128:(t + 1) * 128], rhs=wf[:, k, :],
                                                                                               start=(jc == 0), stop=(jc == NJ - 1)), r=[B_aT, Bwf], w=[Bp])
                            wq.done(("fo", j, half, jg))
                        for t in range(NTL):
                            pf, _, Bp = pts[t]
                            S.op("dve", mk("tensor_tensor", out=CUR.xt[t][0][:, half * 512:(half + 1) * 512], in0=pf[:, :],
                                                                                        in1=CUR.xt[t][0][:, half * 512:(half + 1) * 512], op=ALU.add),
                                 r=[Bp, CUR.xt[t][1]], w=[CUR.xt[t][1]])
                    if last:
                        for t in range(NTL):
                            S.op("act", mk("activation", out=junk[:], in_=CUR.xt[t][0][:], func=AF.Square, accum_out=fst[:, t:t + 1]), r=[CUR.xt[t][1]], w=[B_fst])
                        rsqrt_cols(fst, B_fst, 0, NTL, 1.0 / D)
                        for t in range(NTL):
                            S.op("dve", mk("scalar_tensor_tensor", out=CUR.xt[t][0][:], in0=CUR.xt[t][0][:], scalar=fst[:, t:t + 1], in1=gfin[:], op0=ALU.mult, op1=ALU.mult),
                                 r=[CUR.xt[t][1], B_fst, B_gfin], w=[CUR.xt[t][1]])
                    for t in range(NTL):
                        r0 = j * T + t * 128
                        S.dma("sp", mk("dma_start", out=xdst[r0:r0 + 128, :], in_=CUR.xt[t][0][:]), CUR.xt[t][1], False)
                S.emit(final=last)
                assert not wlive
                WS.slots = list(wsl)

        xcur = x_in
        for l in range(n_layers if not OP_LIMIT else 0):
            last = (l == n_layers - 1)
            with contextlib.ExitStack() as pst:
                pay, B_pay = sb(pst, "pay", [128, PAYW], F32)
                B_pay.temp = False
                phase_A(l, xcur, pay, B_pay)
                if stop_after == "A":
                    break
                phase_C(l, pay, B_pay)
            if stop_after == "C":
                break
            phase_M(l, xcur, XA)
            if stop_after == "M":
                break
            phase_F(l, XA, out_d if last else XB, last)
            xcur = XB
        if OP_LIMIT:
            try:
                with contextlib.ExitStack() as pst:
                    pay, B_pay = sb(pst, "pay", [128, PAYW], F32)
                    phase_A(0, x_in, pay, B_pay)
                    phase_C(0, pay, B_pay)
                phase_M(0, x_in, XA)
                phase_F(0, XA, out_d, True)
            except StopRecording:
                pass
            S.unlimited = True
            S.op("dve", mk("memset", stat[:], 0.0), w=[B_stat])
            S.emit(final=True)
        elif stop_after is not None:
            S.op("dve", mk("memset", stat[:], 0.0), w=[B_stat])
            S.emit(final=True)
        print("total ops", S.total_ops, "dma sems", len(S.dma_bufs_used))
    return nc


def _band_sets(s):
    L = 4 * TOK
    wins = (2, 4, 8, 16)
    out = np.zeros((7, 4, 128, 128), np.float32)

    def band(tile_start, which, g):
        w = wins[g]
        B = np.zeros((128, 128), np.float32)
        src0 = tile_start + (-128 if which == 0 else (128 if which == 2 else 0))
        for t in range(128):
            P = tile_start + t
            if P < 0 or P >= L:
                continue
            lo = max(P - w // 2, 0)
            hi = min(P + w - w // 2, L)
            cnt = hi - lo
            for sp in range(lo, hi):
                si = sp - src0
                if 0 <= si < 128:
                    B[si, t] += 1.0 / cnt
            si = P - src0
            if 0 <= si < 128:
                B[si, t] -= 1.0
        return B
    gen = 5 * 128 + 4096 * 1
    first = s * TOK
    lastt = s * TOK + TOK - 128
    for g in range(4):
        out[0, g] = band(gen, 0, g)
        out[1, g] = band(gen, 1, g)
        out[2, g] = band(gen, 2, g)
        out[3, g] = band(first, 0, g)
        out[4, g] = band(first, 1, g)
        out[5, g] = band(lastt, 1, g)
        out[6, g] = band(lastt, 2, g)
    return np.ascontiguousarray(out.transpose(2, 0, 1, 3))


def _host_inputs(inputs):
    f = lambda a: np.ascontiguousarray(np.asarray(a, dtype=np.float32))
    x = f(inputs["x"]).reshape(NCORES, TOK, D)
    vecs = np.zeros((128, 80), np.float32)
    for l in range(2):
        b = l * 40
        vecs[:, b + 0:b + 8] = f(inputs["norm_mix"])[l].reshape(8, 128).T
        vecs[:, b + 8:b + 16] = f(inputs["norm_ffn"])[l].reshape(8, 128).T
        vecs[:, b + 16:b + 24] = f(inputs["gla_norm"])[l].reshape(8, 128).T
        vecs[:, b + 24:b + 32] = f(inputs["pool_scale"])[l].reshape(8, 128).T
        vecs[:, b + 32:b + 36] = f(inputs["b_decay_fwd"])[l].reshape(4, 128).T
        vecs[:, b + 36:b + 40] = f(inputs["b_decay_bwd"])[l].reshape(4, 128).T
    gfin = np.ascontiguousarray(np.broadcast_to(f(inputs["norm_final"])[None, :], (128, D)))
    ident = np.eye(128, dtype=np.float32)
    si = np.arange(128)[:, None]
    ci = np.arange(128)[None, :]
    masks = np.stack([(si <= ci).astype(np.float32), (si > ci).astype(np.float32)], axis=1)
    shared = {k: f(inputs[k]) for k in ("w_in", "w_decay_up_fwd", "w_decay_up_bwd", "w_branch_gla", "w_pool_group", "w_out", "w_ffn_in", "w_ffn_out")}
    maps = []
    for c in range(NCORES):
        s = c % 4
        sel = np.zeros((128, 16), np.float32)
        for r in range(4):
            sel[:, r] = 1.0 if r < s else 0.0
            sel[:, 4 + r] = 1.0 if r > s else 0.0
            sel[:, 8 + r] = 1.0 if r == s - 1 else 0.0
            sel[:, 12 + r] = 1.0 if r == s + 1 else 0.0
        m = dict(shared)
        m.update({"x": x[c], "vecs": vecs, "gfin": gfin, "ident": ident, "masks": np.ascontiguousarray(masks),
                  "bands": _band_sets(s), "sel": sel})
        maps.append(m)
    return maps


def kernel(**inputs):
    maps = _host_inputs(inputs)
    nc = build()
    res = run_bass_kernel_spmd(nc, maps, core_ids=list(range(NCORES)))
    out = np.stack([np.asarray(r["out"], dtype=np.float32) for r in res.results], axis=0)
    return out.reshape(2, 4 * TOK, D)
```

```python
import contextlib
import numpy as np
import concourse.bass as bass
import concourse.mybir as mybir
from concourse.bass_utils import run_bass_kernel_spmd

F32 = mybir.dt.float32
BF16 = mybir.dt.bfloat16
AF = mybir.ActivationFunctionType
ALU = mybir.AluOpType
AX = mybir.AxisListType

ENGS = ("pe", "act", "dve", "pool", "sp")
SAME_ENGINE_SYNC = True
SAME_ENGINE_WAR = True

NCORES = 8
TOK = 4096
D = 1024
T = 512
NB = TOK // T
NTL = T // 128
DFF = 2816
NJ = DFF // 128
INW = 5664
C_Q, C_K, C_V, C_R, C_LR, C_U, C_GA, C_GB = 0, 512, 1024, 2048, 3072, 3104, 3616, 4640
PAYW = 2048 + 8 + 1024
EPS = 1e-6
WLOOK = 1
NWSLOT = 4
MMB = 4
NDSEM = 36


OP_LIMIT = 0
MILESTONES = False
DBG_VARIANT = 0
SIM_CC = False


class StopRecording(Exception):
    pass


class Buf:
    __slots__ = ("name", "last_w", "readers", "dsem", "ndma", "psum", "temp", "dkind")

    def __init__(self, name):
        self.name = name
        self.last_w = None
        self.readers = []
        self.dsem = None
        self.ndma = 0
        self.psum = False
        self.temp = False
        self.dkind = None


class Op:
    __slots__ = ("eng", "fn", "idx", "waits", "dwaits", "signal", "sigval", "dma_buf", "dma_val")

    def __init__(self, eng, fn, idx):
        self.eng = eng
        self.fn = fn
        self.idx = idx
        self.waits = []
        self.dwaits = {}
        self.signal = False
        self.sigval = 0
        self.dma_buf = None
        self.dma_val = 0


class Sched:
    def __init__(self, nc, sems, dsem_pool):
        self.nc = nc
        self.sems = sems
        self.dsem_pool = dsem_pool
        self.sigcount = {e: 0 for e in ENGS}
        self.ops = {e: [] for e in ENGS}
        self.seen = {e: {f: -1 for f in ENGS} for e in ENGS}
        self.dseen = {e: {} for e in ENGS}
        self.bufs = []
        self.nops = {e: 0 for e in ENGS}
        self.pending_barrier = None
        self.dma_bufs_used = []
        self.total_ops = 0
        self._dma_base = {}
        self.unlimited = False

    def buf(self, name):
        b = Buf(name)
        self.bufs.append(b)
        return b

    def _add_dep(self, op, p):
        if p is None or p is op:
            return
        if p.dma_buf is not None:
            b = p.dma_buf
            if self.dseen[op.eng].get(b, 0) >= p.dma_val:
                return
            if op.dwaits.get(b, 0) < p.dma_val:
                op.dwaits[b] = p.dma_val
            return
        if p.eng == op.eng:
            if op.eng in ("pe", "sp") or not SAME_ENGINE_SYNC:
                return
        if self.seen[op.eng][p.eng] >= p.idx:
            return
        op.waits.append(p)

    def op(self, eng, fn, r=(), w=()):
        if OP_LIMIT and self.total_ops >= OP_LIMIT and not self.unlimited:
            return Op(eng, fn, -1)
        o = Op(eng, fn, self.nops[eng])
        self.nops[eng] += 1
        self.total_ops += 1
        for b in r:
            self._add_dep(o, b.last_w)
            if b.psum:
                for rd in b.readers:
                    if rd.eng != eng:
                        self._add_dep(o, rd)
        for b in w:
            self._add_dep(o, b.last_w)
            for rd in b.readers:
                if rd.eng == eng and rd.dma_buf is None and (eng in ("pe", "sp") or not SAME_ENGINE_SYNC or not SAME_ENGINE_WAR):
                    continue
                self._add_dep(o, rd)
        best = {}
        for p in o.waits:
            if p.eng not in best or best[p.eng].idx < p.idx:
                best[p.eng] = p
        o.waits = list(best.values())
        for p in o.waits:
            p.signal = True
            self.seen[eng][p.eng] = p.idx
        for b, v in o.dwaits.items():
            self.dseen[eng][b] = v
        for b in r:
            b.readers.append(o)
        for b in w:
            b.last_w = o
            b.readers = []
        self.ops[eng].append(o)
        return o

    def dma(self, eng, fn, buf, load, r=(), w=()):
        if buf.dsem is None:
            buf.dkind = "sw" if eng == "pool" else "hw"
            buf.dsem, buf.ndma = self.dsem_pool[buf.dkind].pop()
            self.dma_bufs_used.append(buf)
            self._dma_base[id(buf)] = 16 * buf.ndma
        assert buf.dkind == ("sw" if eng == "pool" else "hw"), buf.name
        rr = list(r) + ([] if load else [buf])
        ww = list(w) + ([buf] if load else [])
        o = self.op(eng, fn, rr, ww)
        if o.idx < 0:
            return o
        buf.ndma += 1
        o.dma_buf = buf
        o.dma_val = 16 * buf.ndma
        return o

    def _simulate(self):
        ptr = {e: 0 for e in ENGS}
        done = set()
        dmac = {}
        progress = True
        while progress:
            progress = False
            for e in ENGS:
                while ptr[e] < len(self.ops[e]):
                    o = self.ops[e][ptr[e]]
                    ok = all(id(p) in done for p in o.waits) and all(dmac.get(id(b), 0) >= v for b, v in o.dwaits.items())
                    if not ok:
                        break
                    done.add(id(o))
                    if o.dma_buf is not None:
                        dmac[id(o.dma_buf)] = dmac.get(id(o.dma_buf), self._dma_base.get(id(o.dma_buf), 0)) + 16
                    ptr[e] += 1
                    progress = True
        stuck = {e: (ptr[e], len(self.ops[e])) for e in ENGS if ptr[e] < len(self.ops[e])}
        if stuck:
            for e in stuck:
                o = self.ops[e][ptr[e]]
                print("STUCK", e, ptr[e], [(p.eng, p.idx, id(p) in done) for p in o.waits], [(b.name, v, dmac.get(id(b), 0)) for b, v in o.dwaits.items()])
            raise RuntimeError(f"static deadlock: {stuck}")
        for b in self.dma_bufs_used:
            self._dma_base[id(b)] = 16 * b.ndma

    def emit(self, final=False):
        nc = self.nc
        for e in ENGS:
            comp = [o for o in self.ops[e] if o.dma_buf is None]
            if comp:
                comp[-1].signal = True
        for e in ENGS:
            c = self.sigcount[e]
            for o in self.ops[e]:
                if o.signal and o.dma_buf is None:
                    c += 1
                    o.sigval = c
            self.sigcount[e] = c
        self._simulate()
        prev_barrier = self.pending_barrier
        end_vals = {e: self.sigcount[e] for e in ENGS}
        dma_end = [(b.dsem, 16 * b.ndma) for b in self.dma_bufs_used]
        ops = self.ops
        sems = self.sems

        def body(ename):
            def run(eng):
                if prev_barrier is not None:
                    ev, dv = prev_barrier
                    for f in ENGS:
                        if f != ename and ev[f] > 0:
                            eng.wait_ge(sems[f], ev[f])
                    for (ds, v) in dv:
                        if v > 0:
                            eng.wait_ge(ds, v)
                for o in ops[ename]:
                    for p in o.waits:
                        eng.wait_ge(sems[p.eng], p.sigval)
                    for b, v in o.dwaits.items():
                        eng.wait_ge(b.dsem, v)
                    inst = o.fn(eng)
                    if o.dma_buf is not None:
                        inst.then_inc(o.dma_buf.dsem, 16)
                    elif o.signal:
                        inst.then_inc(sems[ename], 1)
                if final:
                    for f in ENGS:
                        if f != ename and end_vals[f] > 0:
                            eng.wait_ge(sems[f], end_vals[f])
                    for (ds, v) in dma_end:
                        if v > 0:
                            eng.wait_ge(ds, v)
            return run

        with nc.Block() as block:
            block.tensor(body("pe"))
            block.scalar(body("act"))
            block.vector(body("dve"))
            block.gpsimd(body("pool"))
            block.sync(body("sp"))
        self.pending_barrier = (end_vals, dma_end)
        keep = []
        for b in self.dma_bufs_used:
            if b.temp:
                self.dsem_pool[b.dkind].append((b.dsem, b.ndma))
                b.dsem = None
            else:
                keep.append(b)
        self.dma_bufs_used = keep
        self.ops = {e: [] for e in ENGS}
        self.seen = {e: {f: -1 for f in ENGS} for e in ENGS}
        self.dseen = {e: {} for e in ENGS}
        self.nops = {e: 0 for e in ENGS}
        for b in self.bufs:
            b.last_w = None
            b.readers = []


def mk(name, *args, **kw):
    def fn(e):
        return getattr(e, name)(*args, **kw)
    return fn


class Ring:
    def __init__(self, items):
        self.items = items
        self.i = 0

    def next(self):
        it = self.items[self.i % len(self.items)]
        self.i += 1
        return it


def build(n_layers=2, dbg=False, stop_after=None):
    nc = bass.Bass("TRN2", target_bir_lowering=False)
    dk = "ExternalOutput" if dbg else "Internal"

    def din(name, shape, dt=F32):
        return nc.dram_tensor(name, shape, dt, kind="ExternalInput").ap()

    x_in = din("x", [TOK, D])
    w_in = din("w_in", [2, D, INW])
    w_upf = din("w_decay_up_fwd", [2, 16, 512])
    w_upb = din("w_decay_up_bwd", [2, 16, 512])
    w_bg = din("w_branch_gla", [2, D, D])
    w_pg = din("w_pool_group", [2, 4, 128, 256])
    w_o = din("w_out", [2, D, D])
    w_fi = din("w_ffn_in", [2, D, 2 * DFF])
    w_fo = din("w_ffn_out", [2, DFF, D])
    vecs_d = din("vecs", [128, 80])
    gfin_d = din("gfin", [128, D])
    ident_d = din("ident", [128, 128])
    masks_d = din("masks", [128, 2, 128])
    bands_d = din("bands", [128, 7, 4, 128])
    sel_d = din("sel", [128, 16])
    out_d = nc.dram_tensor("out", [TOK, D], F32, kind="ExternalOutput").ap()
    XA = nc.dram_tensor("XA", [TOK, D], F32, kind=dk).ap()
    XB = nc.dram_tensor("XB", [TOK, D], F32, kind=dk).ap()
    UALL = nc.dram_tensor("UALL", [34, 128, 512], BF16, kind="Internal").ap()
    SLOC = nc.dram_tensor("SLOC", [NB, 2, 128, 1024], F32, kind="Internal").ap()
    SIN = nc.dram_tensor("SIN", [NB, 2, 128, 1024], F32, kind=dk).ap()
    SEGa = nc.dram_tensor("SEGa", [128, 2048], F32, kind="Internal").ap()
    GATHa = nc.dram_tensor("GATHa", [4 * 128, 2048], F32, kind="Internal").ap()
    SEGb = nc.dram_tensor("SEGb", [128, PAYW - 2048], F32, kind="Internal").ap()
    GATHb = nc.dram_tensor("GATHb", [4 * 128, PAYW - 2048], F32, kind="Internal").ap()

    with contextlib.ExitStack() as top:
        E = top.enter_context
        sems = {e: E(nc.semaphore("s_" + e)) for e in ENGS}
        dpool = {"hw": [(E(nc.semaphore(f"dh{i}")), 0) for i in range(NDSEM)], "sw": [(E(nc.semaphore(f"ds{i}")), 0) for i in range(16)]}
        ccsem = E(nc.semaphore("ccsem"))
        S = Sched(nc, sems, dpool)
        cc_count = [0]

        uid = [0]

        def sb(st, name, shape, dt):
            uid[0] += 1
            t = st.enter_context(nc.sbuf_tensor(f"sb{uid[0]}_{name}", shape, dt))
            b = S.buf(name)
            b.temp = st is not top
            return t, b

        ident, B_ident = sb(top, "ident", [128, 128], BF16)
        vecs, B_vecs = sb(top, "vecs", [128, 80], F32)
        negb, B_negb = sb(top, "negb", [128, 16], F32)
        gnh, B_gnh = sb(top, "gnh", [128, 16], F32)
        psh, B_psh = sb(top, "psh", [128, 16], F32)
        sel, B_sel = sb(top, "sel", [128, 16], F32)
        LD, B_LD = sb(top, "LD", [128, 2, 4, NB], F32)
        wsl = [sb(top, f"wslot{i}", [128, 8, 512], BF16) for i in range(NWSLOT)]
        xt = [sb(top, f"xt{i}", [128, D], F32) for i in range(NTL)]
        xn = [sb(top, f"xn{i}", [128, D], BF16) for i in range(NTL)]
        hT0 = sb(top, "hT", [128, 8, T], BF16)
        xtB = [sb(top, f"xtB{i}", [128, D], F32) for i in range(NTL)]
        xts = [xt, xtB]

        class CUR:
            pass
        CUR.xt = xt
        CUR.hT, CUR.B_hT = hT0

        def setcur(j, hTs):
            CUR.xt = xts[j % 2]
            CUR.hT, CUR.B_hT = hTs[j % 2]
        junk, _ = sb(top, "junk", [128, D], BF16)
        stat, B_stat = sb(top, "stat", [128, 8], F32)
        pbanks = []
        for i in range(8):
            p = E(nc.psum_tensor(f"ps{i}", [128, 512], F32))
            pbanks.append(p)
        bankB = [S.buf(f"psb{i}") for i in range(8)]
        for b_ in bankB:
            b_.psum = True
        mmring = Ring([(pbanks[i], pbanks[i].bitcast(BF16), bankB[i]) for i in range(MMB)])
        gring = Ring([(pbanks[i], 0, bankB[i]) for i in range(MMB, 8)])
        fring = Ring([(pbanks[i], pbanks[i].bitcast(BF16), bankB[i]) for i in range(8)])
        wring = Ring(wsl)

        wlive = {}

        class WS:
            slots = list(wsl)

        class WQ:
            def __init__(self):
                self.plan = []
                self.issued = 0
                self.slots = {}

            def add(self, key, src, nk=8, ncols=512):
                self.plan.append((key, src, nk, ncols))

            def _issue(self, i):
                key, src, nk, ncols = self.plan[i]
                free = [k for k in range(len(WS.slots)) if k not in wlive]
                assert free, ("no free weight slot for", key, dict(wlive))
                k = free[0]
                wlive[k] = key
                t, b = WS.slots[k]
                S.dma("pool", mk("dma_start", out=t[:, 0:nk, 0:ncols], in_=src), b, True)
                self.slots[key] = (t, b, k)

            def get(self, key):
                idx = [i for i, p in enumerate(self.plan) if p[0] == key][0]
                while self.issued <= idx:
                    self._issue(self.issued)
                    self.issued += 1
                self.prefetch()
                t, b, k = self.slots[key]
                assert wlive.get(k) == key, (key, wlive)
                return t, b

            def prefetch(self):
                while self.issued < len(self.plan) and len(wlive) < len(WS.slots):
                    self._issue(self.issued)
                    self.issued += 1

            def done(self, key):
                t, b, k = self.slots[key]
                assert wlive.get(k) == key
                del wlive[k]
                self.prefetch()

        def win_grp(l, c0, n=512):
            return w_in[l].rearrange("(kc p) c -> p kc c", p=128)[:, :, c0:c0 + n]

        def load_consts():
            S.dma("pool", mk("dma_start", out=ident[:], in_=ident_d), B_ident, True)
            S.dma("sp", mk("dma_start", out=vecs[:], in_=vecs_d), B_vecs, True)
            S.dma("sp", mk("dma_start", out=sel[:], in_=sel_d), B_sel, True)
            S.op("pool", mk("memset", mhw[:], -0.5), w=[B_mhw])
            for l in range(2):
                S.op("dve", mk("tensor_scalar", out=negb[:, l * 8:(l + 1) * 8], in0=vecs[:, l * 40 + 32:l * 40 + 40],
                                                           scalar1=-1.0, scalar2=None, op0=ALU.mult), r=[B_vecs], w=[B_negb])
                S.op("dve", mk("tensor_scalar", out=gnh[:, l * 8:(l + 1) * 8], in0=vecs[:, l * 40 + 16:l * 40 + 24],
                                                           scalar1=0.5, scalar2=None, op0=ALU.mult), r=[B_vecs], w=[B_gnh])
                S.op("dve", mk("tensor_scalar", out=psh[:, l * 8:(l + 1) * 8], in0=vecs[:, l * 40 + 24:l * 40 + 32],
                                                           scalar1=0.5, scalar2=None, op0=ALU.mult), r=[B_vecs], w=[B_psh])

        def load_x(src, j):
            for t in range(NTL):
                r0 = j * T + t * 128
                S.dma("sp", mk("dma_start", out=CUR.xt[t][0][:], in_=src[r0:r0 + 128, :]), CUR.xt[t][1], True)

        mhw, B_mhw = sb(top, "mhw", [128, 16], F32)

        def rsqrt_cols(tile, B, c0, c1, scale):
            S.op("dve", mk("tensor_scalar", out=tile[:, c0:c1], in0=tile[:, c0:c1], scalar1=scale, scalar2=EPS,
                                                  op0=ALU.mult, op1=ALU.add), r=[B], w=[B])
            S.op("pool", mk("tensor_tensor", out=tile[:, c0:c1], in0=tile[:, c0:c1], in1=mhw[:, 0:c1 - c0], op=ALU.pow),
                 r=[B, B_mhw], w=[B])

        def norm_block(gcol):
            for t in range(NTL):
                S.op("act", mk("activation", out=junk[:], in_=CUR.xt[t][0][:], func=AF.Square, accum_out=stat[:, t:t + 1]),
                     r=[CUR.xt[t][1]], w=[B_stat])
            rsqrt_cols(stat, B_stat, 0, NTL, 1.0 / D)
            for t in range(NTL):
                S.op("dve", mk("tensor_scalar", out=xn[t][0][:], in0=CUR.xt[t][0][:], scalar1=stat[:, t:t + 1], scalar2=None,
                                                           op0=ALU.mult), r=[CUR.xt[t][1], B_stat], w=[xn[t][1]])
            for pc in range(4):
                pf, pb, Bp = mmring.next()
                for t in range(NTL):
                    for c2 in range(2):
                        ch = 2 * pc + c2
                        S.op("pe", mk("transpose",
                            out=pb[:, c2 * 512 + t * 128:c2 * 512 + (t + 1) * 128], in_=xn[t][0][:, ch * 128:(ch + 1) * 128], identity=ident[:]),
                            r=[xn[t][1], B_ident], w=[Bp])
                for c2 in range(2):
                    ch = 2 * pc + c2
                    S.op("act", mk("activation", out=CUR.hT[:, ch, :], in_=pb[:, c2 * 512:(c2 + 1) * 512], func=AF.Identity,
                                                                          scale=vecs[:, gcol + ch:gcol + ch + 1]), r=[Bp, B_vecs], w=[CUR.B_hT])

        def fm_matmul(wt, Bw, cofs, dst_ap, Bd):
            for kc in range(8):
                S.op("pe", mk("matmul", out=dst_ap, lhsT=wt[:, kc, cofs:cofs + 128], rhs=CUR.hT[:, kc, :], start=(kc == 0), stop=(kc == 7)),
                     r=[Bw, CUR.B_hT], w=[Bd])

        def tm_matmul(t, wt, Bw, dst_ap, Bd, ncols=512):
            for kc in range(8):
                S.op("pe", mk("matmul", out=dst_ap, lhsT=CUR.hT[:, kc, t * 128:(t + 1) * 128], rhs=wt[:, kc, 0:ncols], start=(kc == 0), stop=(kc == 7)),
                     r=[Bw, CUR.B_hT], w=[Bd])

        def decay_chain(P, l, h, dr, full):
            sp, B_sp = P["sp"]
            bneg, B_bn = P["bneg"]
            tmp, B_tmp = P["tmp"]
            Ed, B_Ed = P["Ed"][dr]
            nbl, B_nbl = P["nbl"]
            dec, B_dec = P["dec"]
            lrT, B_lrT = P["lrT"]
            wup, B_wup = P["wup"]
            di = dr * 4 + h
            pf, pb, Bp = mmring.next()
            S.op("pe", mk("matmul", out=pf[:, :], lhsT=wup[0:32, dr, h * 128:(h + 1) * 128], rhs=lrT[0:32, :], start=True, stop=True),
                 r=[B_wup, B_lrT], w=[Bp])
            S.op("act", mk("activation", out=sp[:], in_=pf[:, :], func=AF.Exp, bias=negb[:, l * 8 + di:l * 8 + di + 1], scale=-1.0),
                 r=[Bp, B_negb], w=[B_sp])
            S.op("act", mk("activation", out=sp[:], in_=sp[:], func=AF.Ln, bias=1.0, scale=1.0), r=[B_sp], w=[B_sp])
            S.op("dve", mk("tensor_tensor_scan", out=bneg[:], data0=P["msk"][0][:], data1=sp[:], initial=0.0, op0=ALU.mult, op1=ALU.add),
                 r=[B_sp, P["msk"][1]], w=[B_bn])
            S.op("dve", mk("tensor_scalar", out=nbl[:, di * 4:di * 4 + 4], in0=bneg[:].rearrange("p (c t) -> p c t", t=128)[:, :, 127],
                                                  scalar1=-1.0 / 16, scalar2=None, op0=ALU.mult), r=[B_bn], w=[B_nbl])
            S.op("act", mk("activation", out=dec[:, di * 4:di * 4 + 4], in_=nbl[:, di * 4:di * 4 + 4], func=AF.Exp), r=[B_nbl], w=[B_dec])
            if dr == 0:
                for c in range(4):
                    S.op("act", mk("activation", out=Ed[:, c * 128:(c + 1) * 128], in_=bneg[:, c * 128:(c + 1) * 128], func=AF.Exp,
                                                            bias=nbl[:, di * 4 + c:di * 4 + c + 1], scale=1.0 / 16), r=[B_bn, B_nbl], w=[B_Ed])
                if full:
                    Ep, B_Ep = P["Ep"][dr]
                    Em, B_Em = P["Em"][dr]
                    S.op("act", mk("activation", out=Ep[:], in_=bneg[:], func=AF.Exp, scale=-1.0 / 16), r=[B_bn], w=[B_Ep])
                    S.op("act", mk("activation", out=Em[:], in_=bneg[:], func=AF.Exp, scale=1.0 / 16), r=[B_bn], w=[B_Em])
            else:
                S.op("dve", mk("tensor_tensor", out=tmp[:], in0=sp[:], in1=bneg[:], op=ALU.subtract), r=[B_sp, B_bn], w=[B_tmp])
                S.op("act", mk("activation", out=Ed[:], in_=tmp[:], func=AF.Exp, scale=1.0 / 16), r=[B_tmp], w=[B_Ed])
                if full:
                    Ep, B_Ep = P["Ep"][dr]
                    Em, B_Em = P["Em"][dr]
                    nnbl, B_nn = P["nnbl"]
                    S.op("dve", mk("tensor_scalar", out=nnbl[:, 0:4], in0=nbl[:, di * 4:di * 4 + 4], scalar1=-1.0, scalar2=None, op0=ALU.mult),
                         r=[B_nbl], w=[B_nn])
                    for c in range(4):
                        S.op("act", mk("activation", out=Ep[:, c * 128:(c + 1) * 128], in_=tmp[:, c * 128:(c + 1) * 128], func=AF.Exp,
                                                                bias=nbl[:, di * 4 + c:di * 4 + c + 1], scale=-1.0 / 16), r=[B_tmp, B_nbl], w=[B_Ep])
                        S.op("act", mk("activation", out=Em[:, c * 128:(c + 1) * 128], in_=tmp[:, c * 128:(c + 1) * 128], func=AF.Exp,
                                                                bias=nnbl[:, c:c + 1], scale=1.0 / 16), r=[B_tmp, B_nn], w=[B_Em])

        def alloc_decay_tiles(st, P, full):
            P["sp"] = sb(st, "sp", [128, T], F32)
            P["bneg"] = sb(st, "bneg", [128, T], F32)
            P["tmp"] = sb(st, "tmpd", [128, T], F32)
            P["Ed"] = [sb(st, f"Ed{d}", [128, T], F32) for d in range(2)]
            P["nbl"] = sb(st, "nbl", [128, 32], F32)
            P["nnbl"] = sb(st, "nnbl", [128, 4], F32)
            P["dec"] = sb(st, "dec", [128, 32], F32)
            P["lrT"] = sb(st, "lrT", [128, T], BF16)
            P["wup"] = sb(st, "wup", [32, 2, 512], BF16)
            P["msk"] = sb(st, "msk", [128, T], F32)
            P["kdT"] = [sb(st, f"kdT{d}", [128, T], BF16) for d in range(2)]
            P["kd"] = [sb(st, f"kd{d}", [128, NTL, 128], BF16) for d in range(2)]
            P["v"] = sb(st, "v", [128, NTL, D], BF16)
            if full:
                P["Ep"] = [sb(st, f"Ep{d}", [128, T], F32) for d in range(2)]
                P["Em"] = [sb(st, f"Em{d}", [128, T], F32) for d in range(2)]

        def phase_setup(P, l):
            msk, B_msk = P["msk"]
            wup, B_wup = P["wup"]
            S.op("dve", mk("memset", msk[:], 1.0), w=[B_msk])
            S.op("dve", mk("memset", msk[:].rearrange("p (c t) -> p c t", t=128)[:, :, 0:1], 0.0), w=[B_msk])
            S.op("dve", mk("memset", wup[:], 0.0), w=[B_wup])
            S.dma("pool", mk("dma_start", out=wup[0:16, 0, :], in_=w_upf[l]), B_wup, True)
            S.dma("pool", mk("dma_start", out=wup[16:32, 1, :], in_=w_upb[l]), B_wup, True)

        def lr_and_v(P, l, wq, j):
            lrT, B_lrT = P["lrT"]
            v, B_v = P["v"]
            wt, Bw = wq.get(("lr", j))
            pf, pb, Bp = mmring.next()
            for kc in range(8):
                S.op("pe", mk("matmul", out=pf[0:32, :], lhsT=wt[:, kc, 0:32], rhs=CUR.hT[:, kc, :], start=(kc == 0), stop=(kc == 7)),
                     r=[Bw, CUR.B_hT], w=[Bp])
            S.op("act", mk("activation", out=lrT[0:32, :], in_=pf[0:32, :], func=AF.Identity), r=[Bp], w=[B_lrT])
            wq.done(("lr", j))

        def v_tiles(P, wq, j):
            v, B_v = P["v"]
            for half in range(2):
                wt, Bw = wq.get(("v", j, half))
                for t in range(NTL):
                    pf, pb, Bp = mmring.next()
                    tm_matmul(t, wt, Bw, pf[:, :], Bp)
                    S.op("act", mk("activation", out=v[:, t, half * 512:(half + 1) * 512], in_=pf[:, :], func=AF.Identity),
                         r=[Bp], w=[B_v])
                wq.done(("v", j, half))

        def k_side(P, wq, j, h, full):
            wt, Bw = wq.get(("k", j))
            pf, pb, Bp = mmring.next()
            fm_matmul(wt, Bw, h * 128, pf[:, :], Bp)
            if h == 3:
                wq.done(("k", j))
            for dr in range(2):
                kdT, B_kdT = P["kdT"][dr]
                Ed, B_Ed = P["Ed"][dr]
                S.op("dve", mk("tensor_tensor", out=kdT[:], in0=pf[:, :], in1=Ed[:], op=ALU.mult), r=[Bp, B_Ed], w=[B_kdT])
                if full:
                    keT, B_keT = P["keT"][dr]
                    Em, B_Em = P["Em"][dr]
                    S.op("dve", mk("tensor_tensor", out=keT[:], in0=pf[:, :], in1=Em[:], op=ALU.mult), r=[Bp, B_Em], w=[B_keT])
            tf, tb, Bt = mmring.next()
            for dr in range(2):
                kdT, B_kdT = P["kdT"][dr]
                for t in range(NTL):
                    S.op("pe", mk("transpose", out=tb[:, dr * 512 + t * 128:dr * 512 + (t + 1) * 128],
                                                                          in_=kdT[:, t * 128:(t + 1) * 128], identity=ident[:]),
                         r=[B_kdT, B_ident], w=[Bt])
            for dr in range(2):
                kd, B_kd = P["kd"][dr]
                S.op("act", mk("activation", out=kd[:].rearrange("p t d -> p (t d)"), in_=tb[:, dr * 512:(dr + 1) * 512], func=AF.Identity),
                     r=[Bt], w=[B_kd])

        def kv_mm(P, h, dr, t):
            kd, B_kd = P["kd"][dr]
            v, B_v = P["v"]
            pk, cof, Bg = gring.next()
            tt = 0 if DBG_VARIANT == 1 else t
            if DBG_VARIANT == 2:
                cof = 0
            S.op("pe", mk("matmul", out=pk[:, cof:cof + 256], lhsT=kd[:, tt, :], rhs=v[:, tt, h * 256:(h + 1) * 256], start=True, stop=True),
                 r=[B_kd, B_v], w=[Bg])
            return pk, cof, Bg

        def phase_A(l, xsrc, pay, B_pay):
            with contextlib.ExitStack() as st:
                P = {}
                alloc_decay_tiles(st, P, False)
                Sw = [sb(st, f"Sw{i}", [128, 1024], F32) for i in range(2)]
                ubf = [sb(st, f"ubf{i}", [128, 512], BF16) for i in range(2)]
                if l == 0:
                    load_consts()
                phase_setup(P, l)
                ubr = Ring(ubf)
                hTs = [hT0, sb(st, "hT1", [128, 8, T], BF16)]
                WS.slots = list(wsl) + [sb(st, f"wx{i}", [128, 8, 512], BF16) for i in range(2)]
                EdA = [[sb(st, f"EdA{h}{d}", [128, T], F32) for d in range(2)] for h in range(4)]
                kdTs = [P["kdT"], [sb(st, f"kdTb{d}", [128, T], BF16) for d in range(2)]]
                kds = [P["kd"], [sb(st, f"kdb{d}", [128, NTL, 128], BF16) for d in range(2)]]
                scrA = [(P["sp"], P["bneg"], P["tmp"]), (sb(st, "sp2", [128, T], F32), sb(st, "bneg2", [128, T], F32), sb(st, "tmp2", [128, T], F32))]

                def Pv(h, k):
                    d = dict(P)
                    d["Ed"] = EdA[h]
                    d["sp"], d["bneg"], d["tmp"] = scrA[k % 2]
                    return d

                def u_section(wq, j):
                        wt, Bw = wq.get(("u", j))
                        for t in range(NTL):
                            pf, pb, Bp = mmring.next()
                            tm_matmul(t, wt, Bw, pf[:, :], Bp)
                            ub, B_ub = ubr.next()
                            gt = j * NTL + t
                            if gt == 0 or gt == NB * NTL - 1:
                                pc0 = 2056 if gt == 0 else 2056 + 512
                                S.op("act", mk("activation", out=pay[:, pc0:pc0 + 512], in_=pf[:, :], func=AF.Identity), r=[Bp], w=[B_pay])
                                S.op("dve", mk("tensor_copy", out=ub[:], in_=pay[:, pc0:pc0 + 512]), r=[B_pay], w=[B_ub])
                            else:
                                S.op("act", mk("activation", out=ub[:], in_=pf[:, :], func=AF.Identity), r=[Bp], w=[B_ub])
                            S.dma("sp", mk("dma_start", out=UALL[1 + gt], in_=ub[:]), B_ub, False)
                        wq.done(("u", j))

                setcur(0, hTs)
                load_x(xsrc, 0)
                norm_block(l * 40 + 0)
                wq = WQ()
                for j in range(NB):
                    wq.add(("lr", j), win_grp(l, C_LR, 32), 8, 32)
                    for half in range(2):
                        wq.add(("v", j, half), win_grp(l, C_V + half * 512))
                    wq.add(("u", j), win_grp(l, C_U))
                    wq.add(("k", j), win_grp(l, C_K))
                for j in range(NB):
                    setcur(j, hTs)
                    if j + 1 < NB:
                        setcur(j + 1, hTs)
                        load_x(xsrc, j + 1)
                        setcur(j, hTs)
                    ms = (lambda nm: print("MS", nm, S.total_ops)) if (MILESTONES and j == 0) else (lambda nm: None)
                    ms("start")
                    lr_and_v(P, l, wq, j)
                    cnt = 0
                    for h in range(4):
                        for dr in range(2):
                            decay_chain(Pv(h, cnt), l, h, dr, False)
                            cnt += 1
                    v_tiles(P, wq, j)
                    u_section(wq, j)
                    def k_part1(h):
                        Pq = Pv(h, 0)
                        wt, Bw = wq.get(("k", j))
                        pf, pb, Bp = mmring.next()
                        fm_matmul(wt, Bw, h * 128, pf[:, :], Bp)
                        if h == 3:
                            wq.done(("k", j))
                        for dr in range(2):
                            kdT, B_kdT = kdTs[h % 2][dr]
                            Ed, B_Ed = Pq["Ed"][dr]
                            S.op("dve", mk("tensor_tensor", out=kdT[:], in0=pf[:, :], in1=Ed[:], op=ALU.mult), r=[Bp, B_Ed], w=[B_kdT])

                    def k_part2(h):
                        tf, tb, Bt = mmring.next()
                        for dr in range(2):
                            kdT, B_kdT = kdTs[h % 2][dr]
                            for t in range(NTL):
                                S.op("pe", mk("transpose", out=tb[:, dr * 512 + t * 128:dr * 512 + (t + 1) * 128],
                                              in_=kdT[:, t * 128:(t + 1) * 128], identity=ident[:]), r=[B_kdT, B_ident], w=[Bt])
                        for dr in range(2):
                            kd, B_kd = kds[h % 2][dr]
                            S.op("act", mk("activation", out=kd[:].rearrange("p t d -> p (t d)"), in_=tb[:, dr * 512:(dr + 1) * 512], func=AF.Identity),
                                 r=[Bt], w=[B_kd])

                    def states(h):
                        Ph = dict(P)
                        Ph["kd"] = kds[h % 2]
                        for dr in range(2):
                            di = dr * 4 + h
                            sw, B_sw = Sw[dr]
                            order = range(NTL) if dr == 0 else range(NTL - 1, -1, -1)
                            first = True
                            for t in order:
                                pk, cof, Bg = kv_mm(Ph, h, dr, t)
                                if first:
                                    S.op("act", mk("activation", out=sw[:, h * 256:(h + 1) * 256], in_=pk[:, cof:cof + 256], func=AF.Identity),
                                         r=[Bg], w=[B_sw])
                                    first = False
                                else:
                                    S.op("dve", mk("scalar_tensor_tensor",
                                        out=sw[:, h * 256:(h + 1) * 256], in0=sw[:, h * 256:(h + 1) * 256], scalar=P["dec"][0][:, di * 4 + t:di * 4 + t + 1],
                                        in1=pk[:, cof:cof + 256], op0=ALU.mult, op1=ALU.add), r=[B_sw, Bg, P["dec"][1]], w=[B_sw])
                            S.op("dve", mk("tensor_reduce", out=LD[:, dr, h, j:j + 1], in_=P["nbl"][0][:, di * 4:di * 4 + 4], axis=AX.X, op=ALU.add),
                                 r=[P["nbl"][1]], w=[B_LD])

                    k_part1(0)
                    k_part1(1)
                    k_part2(0)
                    if j + 1 < NB:
                        setcur(j + 1, hTs)
                        norm_block(l * 40 + 0)
                        setcur(j, hTs)
                    for h in range(4):
                        if h + 2 < 4:
                            k_part1(h + 2)
                        states(h)
                        if h + 1 < 4:
                            k_part2(h + 1)
                    for dr in range(2):
                        S.dma("sp", mk("dma_start", out=SLOC[j, dr], in_=Sw[dr][0][:]), Sw[dr][1], False)
                    if MILESTONES:
                        print("MS endblock", j, S.total_ops)
                S.emit()
                assert not wlive
                WS.slots = list(wsl)

        def phase_C(l, pay, B_pay):
            if True:
                with contextlib.ExitStack() as st:
                    Dall, B_D = sb(st, "Dall", [128, 2, 4, NB], F32)
                    Lt, B_Lt = sb(st, "Lt", [128, 8], F32)
                    stg = [sb(st, f"stg{i}", [128, 1024], F32) for i in range(8)]
                    car = [[sb(st, f"car{d}_{i}", [128, 1024], F32) for i in range(2)] for d in range(2)]
                    G = [sb(st, f"G{i}", [128, PAYW], F32) for i in range(2)]
                    tmpc, B_tmpc = sb(st, "tmpc", [128, 1024], F32)
                    Dm, B_Dm = sb(st, "Dm", [128, 8], F32)
                    uh = [sb(st, f"uh{i}", [128, 512], F32) for i in range(2)]
                    uhb = [sb(st, f"uhb{i}", [128, 512], BF16) for i in range(2)]
                    stgr = Ring(stg)
                    Gr = Ring(G)
                    S.op("act", mk("activation", out=Dall[:].rearrange("p a b c -> p (a b c)"), in_=LD[:].rearrange("p a b c -> p (a b c)"), func=AF.Exp),
                         r=[B_LD], w=[B_D])
                    S.op("dve", mk("tensor_reduce", out=Lt[:, 0:8], in_=LD[:].rearrange("p a b c -> p (a b) c"), axis=AX.X, op=ALU.add), r=[B_LD], w=[B_Lt])
                    S.op("act", mk("activation", out=pay[:, 2048:2056], in_=Lt[:, 0:8], func=AF.Exp), r=[B_Lt], w=[B_pay])
                    for dr in range(2):
                        order = list(range(NB)) if dr == 0 else list(range(NB - 1, -1, -1))
                        for n, j in enumerate(order):
                            if n == 0:
                                S.dma("sp", mk("dma_start", out=pay[:, dr * 1024:(dr + 1) * 1024], in_=SLOC[j, dr]), B_pay, True)
                            else:
                                sg, B_sg = stgr.next()
                                S.dma("sp", mk("dma_start", out=sg[:], in_=SLOC[j, dr]), B_sg, True)
                                for h in range(4):
                                    S.op("dve", mk("scalar_tensor_tensor",
                                        out=pay[:, dr * 1024 + h * 256:dr * 1024 + (h + 1) * 256], in0=pay[:, dr * 1024 + h * 256:dr * 1024 + (h + 1) * 256],
                                        scalar=Dall[:, dr, h, j:j + 1], in1=sg[:, h * 256:(h + 1) * 256], op0=ALU.mult, op1=ALU.add),
                                        r=[B_pay, B_sg, B_D], w=[B_pay])
                    S.dma("sp", mk("dma_start", out=SEGa, in_=pay[:, 0:2048]), B_pay, False)
                    S.dma("sp", mk("dma_start", out=SEGb, in_=pay[:, 2048:PAYW]), B_pay, False)
                    S.emit()
                    cc_count[0] += 1
                    ccv = cc_count[0]
                    B_cc = S.buf("ccdummy")

                    def ccfn(e):
                        if SIM_CC:
                            for rr in range(4):
                                e.dma_start(out=GATHa[rr * 128:(rr + 1) * 128, :], in_=SEGa).then_inc(ccsem, 16)
                                e.dma_start(out=GATHb[rr * 128:(rr + 1) * 128, :], in_=SEGb).then_inc(ccsem, 16)
                            e.wait_ge(ccsem, 128 * ccv)
                            return e.memset(Dm[:, 0:1], 0.0)
                        for (si, go) in ((SEGa, GATHa), (SEGb, GATHb)):
                            i = e.collective_compute("AllGather", ALU.bypass, replica_groups=[[0, 1, 2, 3], [4, 5, 6, 7]], ins=[si], outs=[go])
                            i.then_inc(ccsem, 1)
                        e.wait_ge(ccsem, 2 * ccv)
                        return e.memset(Dm[:, 0:1], 0.0)
                    S.op("pool", ccfn, w=[B_Dm, B_cc])
                    for dr in range(2):
                        c0, B_c0 = car[dr][0]
                        S.op("dve", mk("memset", c0[:], 0.0), w=[B_c0])
                    for dr in range(2):
                        c0, B_c0 = car[dr][0]
                        ranks = range(4) if dr == 0 else range(3, -1, -1)
                        for r in ranks:
                            g, B_g = Gr.next()
                            S.dma("sp", mk("dma_start", out=g[:, dr * 1024:(dr + 1) * 1024], in_=GATHa[r * 128:(r + 1) * 128, dr * 1024:(dr + 1) * 1024]),
                                  B_g, True, r=[B_cc])
                            if dr == 0:
                                S.dma("sp", mk("dma_start", out=g[:, 2048:PAYW], in_=GATHb[r * 128:(r + 1) * 128, :]), B_g, True, r=[B_cc])
                            else:
                                S.dma("sp", mk("dma_start", out=g[:, 2048:2056], in_=GATHb[r * 128:(r + 1) * 128, 0:8]), B_g, True, r=[B_cc])
                            mcol = dr * 4 + r
                            S.op("dve", mk("tensor_scalar", out=Dm[:, 0:4], in0=g[:, 2048 + dr * 4:2048 + dr * 4 + 4], scalar1=-1.0, scalar2=None,
                                                                             op0=ALU.add), r=[B_g], w=[B_Dm])
                            S.op("dve", mk("tensor_scalar", out=Dm[:, 0:4], in0=Dm[:, 0:4], scalar1=sel[:, mcol:mcol + 1], scalar2=1.0,
                                                                            op0=ALU.mult, op1=ALU.add), r=[B_Dm, B_sel], w=[B_Dm])
                            S.op("dve", mk("tensor_scalar", out=tmpc[:], in0=g[:, dr * 1024:(dr + 1) * 1024], scalar1=sel[:, mcol:mcol + 1],
                                                                                        scalar2=None, op0=ALU.mult), r=[B_g, B_sel], w=[B_tmpc])
                            for h in range(4):
                                S.op("dve", mk("scalar_tensor_tensor", out=c0[:, h * 256:(h + 1) * 256], in0=c0[:, h * 256:(h + 1) * 256],
                                                                                        scalar=Dm[:, h:h + 1], in1=tmpc[:, h * 256:(h + 1) * 256],
                                                                                        op0=ALU.mult, op1=ALU.add), r=[B_c0, B_Dm, B_tmpc], w=[B_c0])
                            if dr == 0:
                                for k, (colsel, uofs) in enumerate(((8 + r, 2056 + 512), (12 + r, 2056))):
                                    u_, B_u = uh[k]
                                    if r == 0:
                                        S.op("dve", mk("tensor_scalar",
                                            out=u_[:], in0=g[:, uofs:uofs + 512], scalar1=sel[:, colsel:colsel + 1], scalar2=None, op0=ALU.mult),
                                            r=[B_g, B_sel], w=[B_u])
                                    else:
                                        S.op("dve", mk("scalar_tensor_tensor",
                                            out=u_[:], in0=g[:, uofs:uofs + 512], scalar=sel[:, colsel:colsel + 1], in1=u_[:], op0=ALU.mult, op1=ALU.add),
                                            r=[B_g, B_sel, B_u], w=[B_u])
                    for k in range(2):
                        S.op("act", mk("activation", out=uhb[k][0][:], in_=uh[k][0][:], func=AF.Identity), r=[uh[k][1]], w=[uhb[k][1]])
                        S.dma("sp", mk("dma_start", out=UALL[0 if k == 0 else 33], in_=uhb[k][0][:]), uhb[k][1], False)
                    for dr in range(2):
                        order = list(range(NB)) if dr == 0 else list(range(NB - 1, -1, -1))
                        cur = 0
                        for n, j in enumerate(order):
                            cc_, B_cc_ = car[dr][cur]
                            S.dma("sp", mk("dma_start", out=SIN[j, dr], in_=cc_[:]), B_cc_, False)
                            if n == NB - 1:
                                break
                            nx, B_nx = car[dr][1 - cur]
                            sg, B_sg = stgr.next()
                            S.dma("sp", mk("dma_start", out=sg[:], in_=SLOC[j, dr]), B_sg, True)
                            for h in range(4):
                                S.op("dve", mk("scalar_tensor_tensor",
                                    out=nx[:, h * 256:(h + 1) * 256], in0=cc_[:, h * 256:(h + 1) * 256], scalar=Dall[:, dr, h, j:j + 1],
                                    in1=sg[:, h * 256:(h + 1) * 256], op0=ALU.mult, op1=ALU.add), r=[B_cc_, B_sg, B_D], w=[B_nx])
                            cur = 1 - cur
                    S.emit()

        def phase_M(l, xsrc, xdst):
            with contextlib.ExitStack() as st:
                P = {}
                alloc_decay_tiles(st, P, True)
                P["keT"] = [sb(st, f"keT{d}", [128, T], BF16) for d in range(2)]
                qeT = [sb(st, f"qeT{d}", [128, T], BF16) for d in range(2)]
                Sbf = [sb(st, f"Sbf{d}", [128, NTL, 256], BF16) for d in range(2)]
                Swk = [[sb(st, f"Swk{d}_{i}", [128, 256], F32) for i in range(2)] for d in range(2)]
                sinb = [sb(st, f"sin{d}", [128, 1024], F32) for d in range(2)]
                srm, B_srm = sb(st, "srm", [128, NTL * D], BF16)
                sr2 = (srm[:].rearrange("p (t d) -> p t d", t=NTL), B_srm)
                sg_t = [sb(st, f"sgt{i}", [128, 512], F32) for i in range(2)]
                og = xn
                ogT, B_ogT = sb(st, "ogT", [128, 8, T], BF16)
                ut = [sb(st, f"ut{i}", [128, 512], BF16) for i in range(NTL + 2)]
                pooledT, B_pT = sb(st, "pooledT", [128, 4, T], BF16)
                mergedT, B_mT = srm[:].rearrange("p (c t) -> p c t", c=8), B_srm
                mt = [sb(st, f"mt{i}", [128, T], F32) for i in range(4)]
                scm = [sb(st, f"scm{i}", [128, 256], BF16) for i in range(4)]
                ssq, B_ssq = sb(st, "ssq", [128, 16], F32)
                masks, B_masks = sb(st, "masks", [128, 2, 128], BF16)
                bands, B_bands = sb(st, "bands", [128, 7, 4, 128], BF16)
                wpool, B_wpool = sb(st, "wpool", [128, 4, 256], BF16)
                phase_setup(P, l)
                S.dma("pool", mk("dma_start", out=masks[:], in_=masks_d), B_masks, True)
                S.dma("pool", mk("dma_start", out=bands[:], in_=bands_d), B_bands, True)
                S.dma("pool", mk("dma_start", out=wpool[:], in_=w_pg[l].rearrange("g c d -> c g d")), B_wpool, True)
                sgr = Ring(sg_t)
                mtr = Ring(mt)
                scr = Ring(scm)
                v, B_v = P["v"]
                hTs = [hT0, hT0]
                setcur(0, hTs)
                load_x(xsrc, 0)
                norm_block(l * 40 + 0)
                wq = WQ()
                for j in range(NB):
                    wq.add(("lr", j), win_grp(l, C_LR, 32), 8, 32)
                    for half in range(2):
                        wq.add(("v", j, half), win_grp(l, C_V + half * 512))
                    for half in range(2):
                        wq.add(("r", j, half), win_grp(l, C_R + half * 512))
                    wq.add(("k", j), win_grp(l, C_K))
                    wq.add(("q", j), win_grp(l, C_Q))
                    for half in range(2):
                        wq.add(("bg", j, half), w_bg[l].rearrange("(kc p) c -> p kc c", p=128)[:, :, half * 512:(half + 1) * 512])
                        wq.add(("ga", j, half), win_grp(l, C_GA + half * 512))
                        wq.add(("gb", j, half), win_grp(l, C_GB + half * 512))
                    for half in range(2):
                        wq.add(("wo", j, half), w_o[l].rearrange("(kc p) c -> p kc c", p=128)[:, :, half * 512:(half + 1) * 512])
                for j in range(NB):
                    setcur(j, hTs)
                    if j + 1 < NB:
                        setcur(j + 1, hTs)
                        load_x(xsrc, j + 1)
                        setcur(j, hTs)
                    for i in range(NTL + 2):
                        S.dma("sp", mk("dma_start", out=ut[i][0][:], in_=UALL[j * NTL + i]), ut[i][1], True)
                    for dr in range(2):
                        S.dma("sp", mk("dma_start", out=sinb[dr][0][:], in_=SIN[j, dr]), sinb[dr][1], True)
                    lr_and_v(P, l, wq, j)
                    v_tiles(P, wq, j)
                    for half in range(2):
                        wt, Bw = wq.get(("r", j, half))
                        for t in range(NTL):
                            pf, pb, Bp = mmring.next()
                            tm_matmul(t, wt, Bw, pf[:, :], Bp)
                            sgt, B_sgt = sgr.next()
                            S.op("act", mk("activation", out=sgt[:], in_=pf[:, :], func=AF.Tanh, scale=0.5), r=[Bp], w=[B_sgt])
                            S.op("dve", mk("scalar_tensor_tensor",
                                out=sr2[0][:, t, half * 512:(half + 1) * 512], in0=sgt[:], scalar=1.0, in1=pf[:, :], op0=ALU.add, op1=ALU.mult),
                                r=[B_sgt, Bp], w=[sr2[1]])
                        wq.done(("r", j, half))
                    for h in range(4):
                        for dr in range(2):
                            decay_chain(P, l, h, dr, True)
                        k_side(P, wq, j, h, True)
                        wt, Bw = wq.get(("q", j))
                        pf, pb, Bp = mmring.next()
                        fm_matmul(wt, Bw, h * 128, pf[:, :], Bp)
                        if h == 3:
                            wq.done(("q", j))
                        for dr in range(2):
                            S.op("dve", mk("scalar_tensor_tensor", out=qeT[dr][0][:], in0=pf[:, :], scalar=float(128 ** -0.5), in1=P["Ep"][dr][0][:],
                                                                                     op0=ALU.mult, op1=ALU.mult), r=[Bp, P["Ep"][dr][1]], w=[qeT[dr][1]])
                        for dr in range(2):
                            di = dr * 4 + h
                            order = list(range(NTL)) if dr == 0 else list(range(NTL - 1, -1, -1))
                            sbf, B_sbf = Sbf[dr]
                            t0 = order[0]
                            S.op("act", mk("activation", out=sbf[:, t0, :], in_=sinb[dr][0][:, h * 256:(h + 1) * 256], func=AF.Identity),
                                 r=[sinb[dr][1]], w=[B_sbf])
                            prev_ap, B_prev = sinb[dr][0][:, h * 256:(h + 1) * 256], sinb[dr][1]
                            for n in range(NTL - 1):
                                t = order[n]
                                tn = order[n + 1]
                                pk, cof, Bg = kv_mm(P, h, dr, t)
                                sw, B_sw = Swk[dr][n % 2]
                                S.op("dve", mk("scalar_tensor_tensor",
                                    out=sw[:], in0=prev_ap, scalar=P["dec"][0][:, di * 4 + t:di * 4 + t + 1], in1=pk[:, cof:cof + 256],
                                    op0=ALU.mult, op1=ALU.add), r=[B_prev, Bg, P["dec"][1]], w=[B_sw])
                                S.op("act", mk("activation", out=sbf[:, tn, :], in_=sw[:], func=AF.Identity), r=[B_sw], w=[B_sbf])
                                prev_ap, B_prev = sw[:], B_sw
                        scall = []
                        for t in range(NTL):
                            pk, _, Bg = mmring.next()
                            for dr in range(2):
                                keT, B_keT = P["keT"][dr]
                                S.op("pe", mk("matmul", out=pk[:, dr * 128:(dr + 1) * 128], lhsT=keT[:, t * 128:(t + 1) * 128],
                                              rhs=qeT[dr][0][:, t * 128:(t + 1) * 128], start=True, stop=True),
                                     r=[B_keT, qeT[dr][1]], w=[Bg])
                            sc, B_sc = scr.next()
                            S.op("dve", mk("tensor_tensor", out=sc[:], in0=pk[:, 0:256], in1=masks[:].rearrange("p a b -> p (a b)"), op=ALU.mult),
                                 r=[Bg, B_masks], w=[B_sc])
                            scall.append((sc, B_sc))
                        for t in range(NTL):
                            scs = [(scall[t][0][:, 0:128], scall[t][1]), (scall[t][0][:, 128:256], scall[t][1])]
                            po, cofo, Bo = gring.next()
                            vh = v[:, t, h * 256:(h + 1) * 256]
                            S.op("pe", mk("matmul", out=po[:, cofo:cofo + 256], lhsT=scs[0][0], rhs=vh, start=True, stop=False),
                                 r=[scs[0][1], B_v], w=[Bo])
                            S.op("pe", mk("matmul", out=po[:, cofo:cofo + 256], lhsT=qeT[0][0][:, t * 128:(t + 1) * 128], rhs=Sbf[0][0][:, t, :],
                                                                              start=False, stop=False), r=[qeT[0][1], Sbf[0][1]], w=[Bo])
                            S.op("pe", mk("matmul", out=po[:, cofo:cofo + 256], lhsT=scs[1][0], rhs=vh, start=False, stop=False),
                                 r=[scs[1][1], B_v], w=[Bo])
                            S.op("pe", mk("matmul", out=po[:, cofo:cofo + 256], lhsT=qeT[1][0][:, t * 128:(t + 1) * 128], rhs=Sbf[1][0][:, t, :],
                                                                              start=False, stop=True), r=[qeT[1][1], Sbf[1][1]], w=[Bo])
                            col = t * 4 + h
                            S.op("act", mk("activation", out=junk[:, 0:256], in_=po[:, cofo:cofo + 256], func=AF.Square,
                                                                                         accum_out=ssq[:, col:col + 1]), r=[Bo], w=[B_ssq])
                            rsqrt_cols(ssq, B_ssq, col, col + 1, 1.0 / 256)
                            S.op("dve", mk("scalar_tensor_tensor",
                                out=og[t][0][:, h * 256:(h + 1) * 256], in0=po[:, cofo:cofo + 256], scalar=ssq[:, col:col + 1],
                                in1=sr2[0][:, t, h * 256:(h + 1) * 256], op0=ALU.mult, op1=ALU.mult), r=[Bo, B_ssq, sr2[1]], w=[og[t][1]])
                    for t in range(NTL):
                        gt = j * NTL + t
                        sp_ = 3 if gt == 0 else 0
                        sm_ = 4 if gt == 0 else (5 if gt == NB * NTL - 1 else 1)
                        sn_ = 6 if gt == NB * NTL - 1 else 2
                        pf, pb, Bp = mmring.next()
                        for g in range(4):
                            for k, (ui, bs) in enumerate(((t, sp_), (t + 1, sm_), (t + 2, sn_))):
                                S.op("pe", mk("matmul", out=pf[:, g * 128:(g + 1) * 128], lhsT=ut[ui][0][:, g * 128:(g + 1) * 128],
                                                                                             rhs=bands[:, bs, g, :], start=(k == 0), stop=(k == 2)),
                                     r=[ut[ui][1], B_bands], w=[Bp])
                        S.op("act", mk("activation", out=pooledT[:, :, t * 128:(t + 1) * 128], in_=pf[:, :].rearrange("p (g t) -> p g t", g=4),
                                                                      func=AF.Identity), r=[Bp], w=[B_pT])
                    for pc in range(4):
                        pf, pb, Bp = mmring.next()
                        for t in range(NTL):
                            for c2 in range(2):
                                ch = 2 * pc + c2
                                S.op("pe", mk("transpose",
                                    out=pb[:, c2 * 512 + t * 128:c2 * 512 + (t + 1) * 128], in_=og[t][0][:, ch * 128:(ch + 1) * 128], identity=ident[:]),
                                    r=[og[t][1], B_ident], w=[Bp])
                        for c2 in range(2):
                            ch = 2 * pc + c2
                            S.op("act", mk("activation", out=ogT[:, ch, :], in_=pb[:, c2 * 512:(c2 + 1) * 512], func=AF.Identity,
                                                                                  scale=gnh[:, l * 8 + ch:l * 8 + ch + 1]), r=[Bp, B_gnh], w=[B_ogT])
                    for i in range(8):
                        half = i // 4
                        wbg, Bwbg = wq.get(("bg", j, half))
                        wga, Bwga = wq.get(("ga", j, half))
                        wgb, Bwgb = wq.get(("gb", j, half))
                        cofs = (i % 4) * 128
                        pya, _, Bya = fring.next()
                        for kc in range(8):
                            S.op("pe", mk("matmul", out=pya[:, :], lhsT=wbg[:, kc, cofs:cofs + 128], rhs=ogT[:, kc, :],
                                                                                             start=(kc == 0), stop=(kc == 7)), r=[Bwbg, B_ogT], w=[Bya])
                        pyb, _, Byb = fring.next()
                        g = i // 2
                        S.op("pe", mk("matmul", out=pyb[:, :], lhsT=wpool[:, g, (i % 2) * 128:(i % 2 + 1) * 128], rhs=pooledT[:, g, :],
                                                                        start=True, stop=True), r=[B_wpool, B_pT], w=[Byb])
                        pga, _, Bga = fring.next()
                        fm_matmul(wga, Bwga, cofs, pga[:, :], Bga)
                        pgb, _, Bgb = fring.next()
                        fm_matmul(wgb, Bwgb, cofs, pgb[:, :], Bgb)
                        sa, B_sa = mtr.next()
                        sb_, B_sb = mtr.next()
                        S.op("act", mk("activation", out=sa[:], in_=pga[:, :], func=AF.Tanh, scale=0.5), r=[Bga], w=[B_sa])
                        S.op("act", mk("activation", out=sb_[:], in_=pgb[:, :], func=AF.Tanh, scale=0.5), r=[Bgb], w=[B_sb])
                        S.op("dve", mk("scalar_tensor_tensor", out=sa[:], in0=sa[:], scalar=1.0, in1=pya[:, :], op0=ALU.add, op1=ALU.mult),
                             r=[B_sa, Bya], w=[B_sa])
                        S.op("dve", mk("tensor_scalar", out=sb_[:], in0=sb_[:], scalar1=1.0, scalar2=psh[:, l * 8 + i:l * 8 + i + 1], op0=ALU.add, op1=ALU.mult),
                             r=[B_sb, B_psh], w=[B_sb])
                        S.op("dve", mk("tensor_tensor", out=sb_[:], in0=sb_[:], in1=pyb[:, :], op=ALU.mult), r=[B_sb, Byb], w=[B_sb])
                        S.op("dve", mk("scalar_tensor_tensor", out=mergedT[:, i, :], in0=sa[:], scalar=0.5, in1=sb_[:], op0=ALU.mult, op1=ALU.add),
                             r=[B_sa, B_sb], w=[B_mT])
                        if i % 4 == 3:
                            wq.done(("bg", j, half))
                            wq.done(("ga", j, half))
                            wq.done(("gb", j, half))
                    if j + 1 < NB:
                        setcur(j + 1, hTs)
                        norm_block(l * 40 + 0)
                        setcur(j, hTs)
                    for half in range(2):
                        wo_, Bwo = wq.get(("wo", j, half))
                        for t in range(NTL):
                            pf, pb, Bp = fring.next()
                            for kc in range(8):
                                S.op("pe", mk("matmul", out=pf[:, :], lhsT=mergedT[:, kc, t * 128:(t + 1) * 128], rhs=wo_[:, kc, :],
                                                                                        start=(kc == 0), stop=(kc == 7)), r=[B_mT, Bwo], w=[Bp])
                            S.op("dve", mk("tensor_tensor", out=CUR.xt[t][0][:, half * 512:(half + 1) * 512], in0=pf[:, :],
                                                                                        in1=CUR.xt[t][0][:, half * 512:(half + 1) * 512], op=ALU.add),
                                 r=[Bp, CUR.xt[t][1]], w=[CUR.xt[t][1]])
                        wq.done(("wo", j, half))
                    for t in range(NTL):
                        r0 = j * T + t * 128
                        S.dma("sp", mk("dma_start", out=xdst[r0:r0 + 128, :], in_=CUR.xt[t][0][:]), CUR.xt[t][1], False)
                S.emit()

        def phase_F(l, xsrc, xdst, last):
            with contextlib.ExitStack() as st:
                actT, B_aT = sb(st, "actT", [128, NJ, T], BF16)
                ft = [sb(st, f"ft{i}", [128, T], F32) for i in range(4)]
                ftr = Ring(ft)
                if last:
                    gfin, B_gfin = sb(st, "gfin", [128, D], F32)
                    S.dma("sp", mk("dma_start", out=gfin[:], in_=gfin_d), B_gfin, True)
                    fst, B_fst = sb(st, "fst", [128, 8], F32)
                hTs = [hT0, sb(st, "hT1", [128, 8, T], BF16)]
                WS.slots = list(wsl) + [sb(st, f"wx{i}", [128, 8, 512], BF16) for i in range(4)]
                setcur(0, hTs)
                load_x(xsrc, 0)
                norm_block(l * 40 + 8)
                wq = WQ()
                for j in range(NB):
                    for jj in range(NJ // 2):
                        wq.add(("fg", j, jj), w_fi[l].rearrange("(kc p) c -> p kc c", p=128)[:, :, jj * 256:(jj + 1) * 256], 8, 256)
                        wq.add(("fu", j, jj), w_fi[l].rearrange("(kc p) c -> p kc c", p=128)[:, :, DFF + jj * 256:DFF + (jj + 1) * 256], 8, 256)
                    for half in range(2):
                        for jg in range(3):
                            nk = 8 if jg < 2 else NJ - 16
                            wq.add(("fo", j, half, jg), w_fo[l].rearrange("(jc p) c -> p jc c", p=128)[:, jg * 8:jg * 8 + nk, half * 512:(half + 1) * 512], nk, 512)
                for j in range(NB):
                    setcur(j, hTs)
                    if j + 1 < NB:
                        setcur(j + 1, hTs)
                        load_x(xsrc, j + 1)
                        setcur(j, hTs)
                    for jj in range(NJ // 2):
                        if jj == 5 and j + 1 < NB:
                            setcur(j + 1, hTs)
                            norm_block(l * 40 + 8)
                            setcur(j, hTs)
                        wg, Bwg = wq.get(("fg", j, jj))
                        wu, Bwu = wq.get(("fu", j, jj))
                        for c2 in range(2):
                            jc = jj * 2 + c2
                            pg, _, Bpg = fring.next()
                            fm_matmul(wg, Bwg, c2 * 128, pg[:, :], Bpg)
                            pu, _, Bpu = fring.next()
                            fm_matmul(wu, Bwu, c2 * 128, pu[:, :], Bpu)
                            f1, B_f1 = ftr.next()
                            S.op("act", mk("activation", out=f1[:], in_=pg[:, :], func=AF.Tanh, scale=0.5), r=[Bpg], w=[B_f1])
                            S.op("dve", mk("scalar_tensor_tensor", out=f1[:], in0=f1[:], scalar=1.0, in1=pg[:, :], op0=ALU.add, op1=ALU.mult),
                                 r=[B_f1, Bpg], w=[B_f1])
                            S.op("dve", mk("scalar_tensor_tensor", out=actT[:, jc, :], in0=f1[:], scalar=0.5, in1=pu[:, :], op0=ALU.mult, op1=ALU.mult),
                                 r=[B_f1, Bpu], w=[B_aT])
                        wq.done(("fg", j, jj))
                        wq.done(("fu", j, jj))
                    for half in range(2):
                        pts = [fring.next() for _ in range(NTL)]
                        for jg in range(3):
                            nk = 8 if jg < 2 else NJ - 16
                            wf, Bwf = wq.get(("fo", j, half, jg))
                            for t in range(NTL):
                                pf, _, Bp = pts[t]
                                for k in range(nk):
                                    jc = jg * 8 + k
                                    S.op("pe", mk("matmul", out=pf[:, :], lhsT=actT[:, jc, t * 128:(t + 1) * 128], rhs=wf[:, k, :],
                                                                                               start=(jc == 0), stop=(jc == NJ - 1)), r=[B_aT, Bwf], w=[Bp])
                            wq.done(("fo", j, half, jg))
                        for t in range(NTL):
                            pf, _, Bp = pts[t]
                            S.op("dve", mk("tensor_tensor", out=CUR.xt[t][0][:, half * 512:(half + 1) * 512], in0=pf[:, :],
                                                                                        in1=CUR.xt[t][0][:, half * 512:(half + 1) * 512], op=ALU.add),
                                 r=[Bp, CUR.xt[t][1]], w=[CUR.xt[t][1]])
                    if last:
                        for t in range(NTL):
                            S.op("act", mk("activation", out=junk[:], in_=CUR.xt[t][0][:], func=AF.Square, accum_out=fst[:, t:t + 1]), r=[CUR.xt[t][1]], w=[B_fst])
                        rsqrt_cols(fst, B_fst, 0, NTL, 1.0 / D)
                        for t in range(NTL):
                            S.op("dve", mk("scalar_tensor_tensor", out=CUR.xt[t][0][:], in0=CUR.xt[t][0][:], scalar=fst[:, t:t + 1], in1=gfin[:], op0=ALU.mult, op1=ALU.mult),
                                 r=[CUR.xt[t][1], B_fst, B_gfin], w=[CUR.xt[t][1]])
                    for t in range(NTL):
                        r0 = j * T + t * 128
                        S.dma("sp", mk("dma_start", out=xdst[r0:r0 + 128, :], in_=CUR.xt[t][0][:]), CUR.xt[t][1], False)
                S.emit(final=last)
                assert not wlive
                WS.slots = list(wsl)

        xcur = x_in
        for l in range(n_layers if not OP_LIMIT else 0):
            last = (l == n_layers - 1)
            with contextlib.ExitStack() as pst:
                pay, B_pay = sb(pst, "pay", [128, PAYW], F32)
                B_pay.temp = False
                phase_A(l, xcur, pay, B_pay)
                if stop_after == "A":
                    break
                phase_C(l, pay, B_pay)
            if stop_after == "C":
                break
            phase_M(l, xcur, XA)
            if stop_after == "M":
                break
            phase_F(l, XA, out_d if last else XB, last)
            xcur = XB
        if OP_LIMIT:
            try:
                with contextlib.ExitStack() as pst:
                    pay, B_pay = sb(pst, "pay", [128, PAYW], F32)
                    phase_A(0, x_in, pay, B_pay)
                    phase_C(0, pay, B_pay)
                phase_M(0, x_in, XA)
                phase_F(0, XA, out_d, True)
            except StopRecording:
                pass
            S.unlimited = True
            S.op("dve", mk("memset", stat[:], 0.0), w=[B_stat])
            S.emit(final=True)
        elif stop_after is not None:
            S.op("dve", mk("memset", stat[:], 0.0), w=[B_stat])
            S.emit(final=True)
        print("total ops", S.total_ops, "dma sems", len(S.dma_bufs_used))
    return nc


def _band_sets(s):
    L = 4 * TOK
    wins = (2, 4, 8, 16)
    out = np.zeros((7, 4, 128, 128), np.float32)

    def band(tile_start, which, g):
        w = wins[g]
        B = np.zeros((128, 128), np.float32)
        src0 = tile_start + (-128 if which == 0 else (128 if which == 2 else 0))
        for t in range(128):
            P = tile_start + t
            if P < 0 or P >= L:
                continue
            lo = max(P - w // 2, 0)
            hi = min(P + w - w // 2, L)
            cnt = hi - lo
            for sp in range(lo, hi):
                si = sp - src0
                if 0 <= si < 128:
                    B[si, t] += 1.0 / cnt
            si = P - src0
            if 0 <= si < 128:
                B[si, t] -= 1.0
        return B
    gen = 5 * 128 + 4096 * 1
    first = s * TOK
    lastt = s * TOK + TOK - 128
    for g in range(4):
        out[0, g] = band(gen, 0, g)
        out[1, g] = band(gen, 1, g)
        out[2, g] = band(gen, 2, g)
        out[3, g] = band(first, 0, g)
        out[4, g] = band(first, 1, g)
        out[5, g] = band(lastt, 1, g)
        out[6, g] = band(lastt, 2, g)
    return np.ascontiguousarray(out.transpose(2, 0, 1, 3))


def _host_inputs(inputs):
    f = lambda a: np.ascontiguousarray(np.asarray(a, dtype=np.float32))
    x = f(inputs["x"]).reshape(NCORES, TOK, D)
    vecs = np.zeros((128, 80), np.float32)
    for l in range(2):
        b = l * 40
        vecs[:, b + 0:b + 8] = f(inputs["norm_mix"])[l].reshape(8, 128).T
        vecs[:, b + 8:b + 16] = f(inputs["norm_ffn"])[l].reshape(8, 128).T
        vecs[:, b + 16:b + 24] = f(inputs["gla_norm"])[l].reshape(8, 128).T
        vecs[:, b + 24:b + 32] = f(inputs["pool_scale"])[l].reshape(8, 128).T
        vecs[:, b + 32:b + 36] = f(inputs["b_decay_fwd"])[l].reshape(4, 128).T
        vecs[:, b + 36:b + 40] = f(inputs["b_decay_bwd"])[l].reshape(4, 128).T
    gfin = np.ascontiguousarray(np.broadcast_to(f(inputs["norm_final"])[None, :], (128, D)))
    ident = np.eye(128, dtype=np.float32)
    si = np.arange(128)[:, None]
    ci = np.arange(128)[None, :]
    masks = np.stack([(si <= ci).astype(np.float32), (si > ci).astype(np.float32)], axis=1)
    shared = {k: f(inputs[k]) for k in ("w_in", "w_decay_up_fwd", "w_decay_up_bwd", "w_branch_gla", "w_pool_group", "w_out", "w_ffn_in", "w_ffn_out")}
    maps = []
    for c in range(NCORES):
        s = c % 4
        sel = np.zeros((128, 16), np.float32)
        for r in range(4):
            sel[:, r] = 1.0 if r < s else 0.0
            sel[:, 4 + r] = 1.0 if r > s else 0.0
            sel[:, 8 + r] = 1.0 if r == s - 1 else 0.0
            sel[:, 12 + r] = 1.0 if r == s + 1 else 0.0
        m = dict(shared)
        m.update({"x": x[c], "vecs": vecs, "gfin": gfin, "ident": ident, "masks": np.ascontiguousarray(masks),
                  "bands": _band_sets(s), "sel": sel})
        maps.append(m)
    return maps


def kernel(**inputs):
    maps = _host_inputs(inputs)
    nc = build()
    res = run_bass_kernel_spmd(nc, maps, core_ids=list(range(NCORES)))
    out = np.stack([np.asarray(r["out"], dtype=np.float32) for r in res.results], axis=0)
    return out.reshape(2, 4 * TOK, D)
```
